# Optimizing a Trainium2 kernel written in Bass

```python
import math
import jax
import jax.numpy as jnp
from jax import lax
import numpy as np

D_MODEL = 4096
BATCH = 8
SEQ = 2048
DEPTH = 2

N_MIX_GROUPS = 4
GROUP_W = D_MODEL // N_MIX_GROUPS
MIX_W = N_MIX_GROUPS * GROUP_W

RWKV_HEAD = 64
RWKV_HEADS = GROUP_W // RWKV_HEAD
RWKV_DECAY_RANK = 64
RWKV_A_RANK = 64
RWKV_GATE_RANK = 160
RWKV_GN_EPS = 64e-5

GDN_HEAD = 128
GDN_V_HEADS = GROUP_W // GDN_HEAD
GDN_QK_HEADS = GDN_V_HEADS // 2
GDN_QK_W = GDN_QK_HEADS * GDN_HEAD
GDN_CHUNK = 64

RET_HEADS = 8
RET_V_HEAD = GROUP_W // RET_HEADS
RET_QK_HEAD = RET_V_HEAD // 2
RET_QK_W = RET_HEADS * RET_QK_HEAD
RET_CHUNK = 128
ROPE_BASE = 10000.0

SSD_HEAD = 64
SSD_HEADS = GROUP_W // SSD_HEAD
SSD_GROUPS = 4
SSD_STATE = 128
SSD_BC_W = SSD_GROUPS * SSD_STATE
SSD_CHUNK = 128

CONV_W = 5
D_FF = ((8 * D_MODEL // 3 + 255) // 256) * 256
NORM_EPS = 1e-6
F32 = jnp.float32

RWKV_COLS = (GROUP_W, GROUP_W, GROUP_W, RWKV_DECAY_RANK, RWKV_DECAY_RANK, RWKV_A_RANK, RWKV_GATE_RANK)
GDN_COLS = (GDN_QK_W, GDN_QK_W, GROUP_W, GROUP_W, GDN_V_HEADS, GDN_V_HEADS, GDN_V_HEADS)
RET_COLS = (RET_QK_W, RET_QK_W, GROUP_W, GROUP_W)
SSD_COLS = (GROUP_W, GROUP_W, SSD_BC_W, SSD_BC_W, SSD_HEADS, SSD_HEADS)
N_RWKV = sum(RWKV_COLS)
N_GDN = sum(GDN_COLS)
N_RET = sum(RET_COLS)
N_SSD = sum(SSD_COLS)
N_IN = N_RWKV + N_GDN + N_RET + N_SSD
GDN_CONV_CH = 2 * GDN_QK_W + GROUP_W
SSD_CONV_CH = GROUP_W + 2 * SSD_BC_W

kernel_name = 'hybrid_parallel_head_groups_encoder'


def _split(p, widths):
    return jnp.split(p, np.cumsum(widths)[:-1].tolist(), axis=-1)


def _rms_scale(x, eps=NORM_EPS):
    xf = x.astype(F32)
    return xf * lax.rsqrt(jnp.mean(xf * xf, axis=-1, keepdims=True) + eps)


def _rmsnorm(x, g):
    return (_rms_scale(x) * g.astype(F32)).astype(x.dtype)


def _l2norm(x, eps=1e-6):
    xf = x.astype(F32)
    return xf * lax.rsqrt(jnp.sum(xf * xf, axis=-1, keepdims=True) + eps)


def _centred_conv(x, w):
    half = CONV_W // 2
    return lax.conv_general_dilated(
        x, w[:, None, :].astype(x.dtype), window_strides=(1,), padding=[(half, half)],
        dimension_numbers=('NWC', 'WIO', 'NWC'), feature_group_count=x.shape[-1])


def _centred_shift(p):
    pad = jnp.pad(p, ((0, 0), (1, 1), (0, 0)))
    return 0.5 * (pad[:, :-2] + pad[:, 2:])


def _heads_major(t):
    return jnp.swapaxes(t, 1, 2)


def _flip_t(t):
    return jnp.flip(t, axis=2)


def _chunk_decay_attn(q, k, v, log_a, chunk, strict):
    N, H, S, dk = q.shape
    dv = v.shape[-1]
    nc = S // chunk
    q, k, v = [t.reshape(N, H, nc, chunk, -1) for t in (q, k, v)]
    g = jnp.cumsum(log_a.reshape(N, H, nc, chunk), axis=-1)
    idx = jnp.arange(chunk)
    mask = (idx[:, None] > idx[None, :]) if strict else (idx[:, None] >= idx[None, :])
    decay = jnp.exp(jnp.where(mask, g[..., :, None] - g[..., None, :], -jnp.inf))
    scores = jnp.einsum('nhctd,nhcsd->nhcts', q, k) * decay
    y = jnp.einsum('nhcts,nhcsv->nhctv', scores, v)
    chunk_states = jnp.einsum('nhcsd,nhcsv->nhcdv', k * jnp.exp(g[..., -1:] - g)[..., None], v)
    chunk_decay = jnp.exp(g[..., -1])

    def step(state, inp):
        st, cd = inp
        return state * cd[..., None, None] + st, state

    _, prev = lax.scan(step, jnp.zeros((N, H, dk, dv), q.dtype),
                       (jnp.moveaxis(chunk_states, 2, 0), jnp.moveaxis(chunk_decay, 2, 0)))
    y = y + jnp.einsum('nhctd,nhcdv->nhctv', q * jnp.exp(g)[..., None], jnp.moveaxis(prev, 0, 2))
    return y.reshape(N, H, S, dv)


def _chunk_gated_delta(q, k, v, beta, g_log, chunk):
    N, H, S, dk = q.shape
    dv = v.shape[-1]
    nc = S // chunk
    q, k, v = [t.reshape(N, H, nc, chunk, -1) for t in (q, k, v)]
    beta = beta.reshape(N, H, nc, chunk)
    g = jnp.cumsum(g_log.reshape(N, H, nc, chunk), axis=-1)
    idx = jnp.arange(chunk)
    incl = idx[:, None] >= idx[None, :]
    strict = idx[:, None] > idx[None, :]
    decay = jnp.exp(jnp.where(incl, g[..., :, None] - g[..., None, :], -jnp.inf))
    kb = k * beta[..., None]
    lower = jnp.where(strict, jnp.einsum('nhctd,nhcsd->nhcts', kb, k) * decay, 0.0)
    eye = jnp.eye(chunk, dtype=q.dtype)
    rhs = jnp.concatenate([v * beta[..., None], kb * jnp.exp(g)[..., None]], axis=-1)
    sol = lax.linalg.triangular_solve(eye + lower, rhs, left_side=True, lower=True, unit_diagonal=True)
    u, w = sol[..., :dv], sol[..., dv:]
    attn = jnp.einsum('nhctd,nhcsd->nhcts', q, k) * decay
    q_dec = q * jnp.exp(g)[..., None]
    k_dec = k * jnp.exp(g[..., -1:] - g)[..., None]
    g_last = jnp.exp(g[..., -1])

    def step(state, inp):
        u_c, w_c, qd_c, a_c, kd_c, gl_c = inp
        v_new = u_c - jnp.einsum('nhtk,nhkv->nhtv', w_c, state)
        o = jnp.einsum('nhtk,nhkv->nhtv', qd_c, state) + jnp.einsum('nhts,nhsv->nhtv', a_c, v_new)
        state = state * gl_c[..., None, None] + jnp.einsum('nhsk,nhsv->nhkv', kd_c, v_new)
        return state, o

    xs = tuple(jnp.moveaxis(t, 2, 0) for t in (u, w, q_dec, attn, k_dec, g_last))
    _, o = lax.scan(step, jnp.zeros((N, H, dk, dv), q.dtype), xs)
    return jnp.moveaxis(o, 0, 2).reshape(N, H, S, dv)


def _rwkv7(p, mu, w0, w2, a0, a2, g2, k_k, k_a, r_k, ln_w, ln_b):
    B_, S, _ = p.shape
    H, N = RWKV_HEADS, RWKV_HEAD
    p = p + (_centred_shift(p) - p) * mu
    r, k, v, pw_f, pw_b, pa, pg = _split(p, RWKV_COLS)
    pw = jnp.stack([pw_f, pw_b], axis=0)
    w = (w0[:, None, None, :] + jnp.einsum('dbsr,drc->dbsc', jnp.tanh(pw), w2)).astype(F32)
    w = -jax.nn.softplus(-w) - 0.5
    decay = jnp.exp(-jnp.exp(w)).reshape(2, B_, S, H, N)
    a = jax.nn.sigmoid(a0 + pa @ a2)
    g = jax.nn.sigmoid(pg) @ g2
    kk = _l2norm((k * k_k).reshape(B_, S, H, N))
    k = k * (1.0 + (a - 1.0) * k_a)
    r_h, k_h, v_h, a_h = [t.reshape(B_, S, H, N).astype(F32) for t in (r, k, v, a)]

    def both(t_f, t_b):
        return jnp.stack([t_f, jnp.flip(t_b, axis=1)], axis=0).transpose(2, 0, 1, 3, 4)

    xs = (both(r_h, r_h), both(decay[0], decay[1]), both(k_h, k_h), both(v_h, v_h),
          both(kk, kk), both(kk * a_h, kk * a_h))

    def step(state, inp):
        r_t, w_t, k_t, v_t, kk_t, b_t = inp
        s_kk = jnp.einsum('dbhvk,dbhk->dbhv', state, kk_t)
        state = (state * w_t[..., None, :] - s_kk[..., :, None] * b_t[..., None, :]
                 + v_t[..., :, None] * k_t[..., None, :])
        return state, jnp.einsum('dbhvk,dbhk->dbhv', state, r_t)

    _, ys = lax.scan(step, jnp.zeros((2, B_, H, N, N), F32), xs)
    y = jnp.swapaxes(ys[:, 0] + jnp.flip(ys[:, 1], axis=0), 0, 1)
    mean = jnp.mean(y, axis=-1, keepdims=True)
    var = jnp.mean(jnp.square(y - mean), axis=-1, keepdims=True)
    y = (y - mean) * lax.rsqrt(var + RWKV_GN_EPS)
    y = y * ln_w.reshape(H, N).astype(F32) + ln_b.reshape(H, N).astype(F32)
    bonus = jnp.sum(r_h * k_h * r_k.astype(F32), axis=-1, keepdims=True) * v_h
    return ((y + bonus).reshape(B_, S, GROUP_W) * g.astype(F32)).astype(p.dtype)


def _gated_deltanet(p, conv_w, A_log, dt_bias, norm_w):
    B_, S, _ = p.shape
    q, k, v, z, pb, pa_f, pa_b = _split(p, GDN_COLS)
    qkv = jax.nn.silu(_centred_conv(jnp.concatenate([q, k, v], axis=-1), conv_w))
    q, k, v = _split(qkv, (GDN_QK_W, GDN_QK_W, GROUP_W))
    rep = GDN_V_HEADS // GDN_QK_HEADS
    q = jnp.repeat(_l2norm(q.reshape(B_, S, GDN_QK_HEADS, GDN_HEAD)), rep, axis=2) * GDN_HEAD ** -0.5
    k = jnp.repeat(_l2norm(k.reshape(B_, S, GDN_QK_HEADS, GDN_HEAD)), rep, axis=2)
    v = v.reshape(B_, S, GDN_V_HEADS, GDN_HEAD).astype(F32)
    beta = jax.nn.sigmoid(pb.astype(F32))
    pa = jnp.stack([pa_f, pa_b], axis=0).astype(F32)
    g = -jnp.exp(A_log.astype(F32))[:, None, None, :] * jax.nn.softplus(pa + dt_bias.astype(F32)[:, None, None, :])

    def two(t):
        t = _heads_major(t)
        return jnp.concatenate([t, _flip_t(t)], axis=0)

    g_dirs = jnp.concatenate([_heads_major(g[0]), _flip_t(_heads_major(g[1]))], axis=0)
    o = _chunk_gated_delta(two(q), two(k), two(v), two(beta), g_dirs, GDN_CHUNK)
    o = _heads_major(o[:B_] + _flip_t(o[B_:]))
    o = _rms_scale(o) * norm_w.astype(F32)
    return (o.reshape(B_, S, GROUP_W) * jax.nn.silu(z.astype(F32))).astype(p.dtype)


def _retention(p):
    B_, S, _ = p.shape
    q, k, v, gate = _split(p, RET_COLS)
    pos = jnp.arange(S, dtype=F32)
    theta = 1.0 / (ROPE_BASE ** jnp.linspace(0.0, 1.0, RET_QK_HEAD // 2))
    ang = pos[:, None] * theta[None, :]
    cos, sin = jnp.cos(ang)[:, None, :], jnp.sin(ang)[:, None, :]

    def rot(t):
        t = t.reshape(B_, S, RET_HEADS, RET_QK_HEAD // 2, 2).astype(F32)
        t1, t2 = t[..., 0], t[..., 1]
        return jnp.stack([t1 * cos - t2 * sin, t1 * sin + t2 * cos], axis=-1).reshape(B_, S, RET_HEADS, RET_QK_HEAD)

    qh = _heads_major(rot(q))
    kh = _heads_major(rot(k) * RET_QK_HEAD ** -0.5)
    vh = _heads_major(v.reshape(B_, S, RET_HEADS, RET_V_HEAD).astype(F32))
    log_gamma = jnp.log(1.0 - 2.0 ** (-5.0 - jnp.arange(RET_HEADS, dtype=F32)))
    log_a = jnp.broadcast_to(log_gamma[None, :, None], (B_, RET_HEADS, S))
    y_f = _chunk_decay_attn(qh, kh, vh, log_a, RET_CHUNK, False)
    y_b = _flip_t(_chunk_decay_attn(_flip_t(qh), _flip_t(kh), _flip_t(vh), log_a, RET_CHUNK, True))
    y = _rms_scale(_heads_major(y_f + y_b))
    return (jax.nn.silu(gate.astype(F32)) * y.reshape(B_, S, GROUP_W)).astype(p.dtype)


def _ssd(p, conv_w, conv_b, dt_bias, A_log, D, norm_w):
    B_, S, _ = p.shape
    z, xr, Bm, Cm, dt_f, dt_b = _split(p, SSD_COLS)
    xbc = jax.nn.silu(_centred_conv(jnp.concatenate([xr, Bm, Cm], axis=-1), conv_w) + conv_b)
    xs, Bm, Cm = _split(xbc, (GROUP_W, SSD_BC_W, SSD_BC_W))
    rep = SSD_HEADS // SSD_GROUPS
    xh = xs.reshape(B_, S, SSD_HEADS, SSD_HEAD).astype(F32)
    Bh = jnp.repeat(Bm.reshape(B_, S, SSD_GROUPS, SSD_STATE).astype(F32), rep, axis=2)
    Ch = jnp.repeat(Cm.reshape(B_, S, SSD_GROUPS, SSD_STATE).astype(F32), rep, axis=2)
    dt = jax.nn.softplus(jnp.stack([dt_f, dt_b], axis=0).astype(F32) + dt_bias.astype(F32)[:, None, None, :])
    A = -jnp.exp(A_log.astype(F32))
    log_a = jnp.swapaxes(dt * A[:, None, None, :], 2, 3)
    q, v = _heads_major(Ch), _heads_major(xh)
    k_f = _heads_major(Bh * dt[0][..., None])
    k_b = _heads_major(Bh * dt[1][..., None])
    y_f = _chunk_decay_attn(q, k_f, v, log_a[0], SSD_CHUNK, False)
    y_b = _flip_t(_chunk_decay_attn(_flip_t(q), _flip_t(k_b), _flip_t(v), _flip_t(log_a[1]), SSD_CHUNK, True))
    y = _heads_major(y_f + y_b) + xh * D.astype(F32)[:, None]
    y = y.reshape(B_, S, GROUP_W) * jax.nn.silu(z.astype(F32))
    y = _rms_scale(y.reshape(B_, S, SSD_GROUPS, GROUP_W // SSD_GROUPS)) * norm_w.astype(F32).reshape(SSD_GROUPS, -1)
    return y.reshape(B_, S, GROUP_W).astype(p.dtype)


def _dt_bias_init(key, shape):
    dt = jnp.exp(jax.random.uniform(key, shape, F32, math.log(1e-3), math.log(1e-1)))
    return dt + jnp.log(-jnp.expm1(-dt))


def setup_inputs(seed: int = 0) -> dict:
    key = jax.random.key(seed)
    ks = jax.random.split(key, 32)
    L = DEPTH

    def nrm(k, shape, scale):
        return jax.random.normal(k, shape, F32) * scale

    return {
        'x': nrm(ks[0], (BATCH, SEQ, D_MODEL), 1.0),
        'attn_norm_g': 1.0 + nrm(ks[1], (L, D_MODEL), 0.02),
        'w_in': nrm(ks[2], (L, D_MODEL, N_IN), D_MODEL ** -0.5),
        'rwkv_mu': jax.random.uniform(ks[3], (L, N_RWKV), F32),
        'rwkv_w0': jnp.linspace(-6.0, -1.0, GROUP_W, dtype=F32)[None, None, :] + nrm(ks[4], (L, 2, GROUP_W), 0.3),
        'rwkv_w2': nrm(ks[5], (L, 2, RWKV_DECAY_RANK, GROUP_W), 0.1 * RWKV_DECAY_RANK ** -0.5),
        'rwkv_a0': nrm(ks[6], (L, GROUP_W), 0.1),
        'rwkv_a2': nrm(ks[7], (L, RWKV_A_RANK, GROUP_W), RWKV_A_RANK ** -0.5),
        'rwkv_g2': nrm(ks[8], (L, RWKV_GATE_RANK, GROUP_W), RWKV_GATE_RANK ** -0.5),
        'rwkv_k_k': 0.85 + nrm(ks[9], (L, GROUP_W), 0.02),
        'rwkv_k_a': 1.0 + nrm(ks[10], (L, GROUP_W), 0.02),
        'rwkv_r_k': nrm(ks[11], (L, RWKV_HEADS, RWKV_HEAD), 0.1),
        'rwkv_ln_w': 1.0 + nrm(ks[12], (L, GROUP_W), 0.02),
        'rwkv_ln_b': nrm(ks[13], (L, GROUP_W), 0.01),
        'gdn_conv_w': nrm(ks[14], (L, CONV_W, GDN_CONV_CH), CONV_W ** -0.5),
        'gdn_A_log': jnp.log(jax.random.uniform(ks[15], (L, 2, GDN_V_HEADS), F32, 1.0, 16.0)),
        'gdn_dt_bias': _dt_bias_init(ks[16], (L, 2, GDN_V_HEADS)),
        'gdn_norm_w': 1.0 + nrm(ks[17], (L, GDN_HEAD), 0.02),
        'ssd_conv_w': nrm(ks[18], (L, CONV_W, SSD_CONV_CH), CONV_W ** -0.5),
        'ssd_conv_b': nrm(ks[19], (L, SSD_CONV_CH), 0.01),
        'ssd_dt_bias': _dt_bias_init(ks[20], (L, 2, SSD_HEADS)),
        'ssd_A_log': jnp.log(jax.random.uniform(ks[21], (L, 2, SSD_HEADS), F32, 1.0, 16.0)),
        'ssd_D': 1.0 + nrm(ks[22], (L, SSD_HEADS), 0.1),
        'ssd_norm_w': 1.0 + nrm(ks[23], (L, GROUP_W), 0.02),
        'w_out': nrm(ks[24], (L, MIX_W, D_MODEL), MIX_W ** -0.5),
        'ffn_norm_g': 1.0 + nrm(ks[25], (L, D_MODEL), 0.02),
        'w_gate_up': nrm(ks[26], (L, D_MODEL, 2 * D_FF), D_MODEL ** -0.5),
        'w_down': nrm(ks[27], (L, D_FF, D_MODEL), D_FF ** -0.5),
        'final_norm_g': 1.0 + nrm(ks[28], (D_MODEL,), 0.02),
    }


def reference(x, attn_norm_g, w_in, rwkv_mu, rwkv_w0, rwkv_w2, rwkv_a0, rwkv_a2, rwkv_g2,
              rwkv_k_k, rwkv_k_a, rwkv_r_k, rwkv_ln_w, rwkv_ln_b, gdn_conv_w, gdn_A_log,
              gdn_dt_bias, gdn_norm_w, ssd_conv_w, ssd_conv_b, ssd_dt_bias, ssd_A_log, ssd_D,
              ssd_norm_w, w_out, ffn_norm_g, w_gate_up, w_down, final_norm_g):
    for l in range(DEPTH):
        h = _rmsnorm(x, attn_norm_g[l])
        p = h @ w_in[l]
        p_a, p_b, p_c, p_d = _split(p, (N_RWKV, N_GDN, N_RET, N_SSD))
        y_a = _rwkv7(p_a, rwkv_mu[l], rwkv_w0[l], rwkv_w2[l], rwkv_a0[l], rwkv_a2[l], rwkv_g2[l],
                     rwkv_k_k[l], rwkv_k_a[l], rwkv_r_k[l], rwkv_ln_w[l], rwkv_ln_b[l])
        y_b = _gated_deltanet(p_b, gdn_conv_w[l], gdn_A_log[l], gdn_dt_bias[l], gdn_norm_w[l])
        y_c = _retention(p_c)
        y_d = _ssd(p_d, ssd_conv_w[l], ssd_conv_b[l], ssd_dt_bias[l], ssd_A_log[l], ssd_D[l], ssd_norm_w[l])
        y = jnp.concatenate([y_a, y_b, y_c, y_d], axis=-1)
        x = x + (y @ w_out[l]).astype(x.dtype)
        h = _rmsnorm(x, ffn_norm_g[l])
        gate, up = jnp.split(h @ w_gate_up[l], 2, axis=-1)
        x = x + ((jax.nn.silu(gate) * up) @ w_down[l]).astype(x.dtype)
    return _rmsnorm(x, final_norm_g)
```

```python
import math
import numpy as np
from contextlib import ExitStack
import concourse.bass as bass
import concourse.mybir as mybir
from concourse.bass_utils import run_bass_kernel_spmd

F32 = mybir.dt.float32
BF16 = mybir.dt.bfloat16
F32R = mybir.dt.float32r


def fr(ap):
    return ap.bitcast(F32R)
AF = mybir.ActivationFunctionType
ALU = mybir.AluOpType
AX = mybir.AxisListType

D = 4096
T = 2048
NB = T // 128
DEPTH = 2
N_RWKV, N_GDN, N_RET, N_SSD = 3424, 3096, 3072, 3104
N_IN = N_RWKV + N_GDN + N_RET + N_SSD
O_RWKV, O_GDN, O_RET, O_SSD = 0, N_RWKV, N_RWKV + N_GDN, N_RWKV + N_GDN + N_RET
DFF = 11008
NEG = -30000.0


class Buf:
    __slots__ = ("w", "r")

    def __init__(s):
        s.w = {}
        s.r = {}


class TB:
    __slots__ = ("t", "b")

    def __init__(s, t):
        s.t = t
        s.b = Buf()

    def __getitem__(s, key):
        return s.t[key]


class TBQ:
    __slots__ = ("t", "b", "c0")

    def __init__(s, base, c0):
        s.t = base.t
        s.c0 = c0
        s.b = base.b

    def __getitem__(s, key):
        rows, cols = key
        a = cols.start or 0
        b = 128 if cols.stop is None else cols.stop
        return s.t[rows, s.c0 + a:s.c0 + b]


class Eng:
    def __init__(s, name, eng, sem):
        s.name = name
        s.eng = eng
        s.sem = sem
        s.count = 0
        s.seen = {}


class KB:
    NDMA = 32

    def __init__(s, nc, es):
        s.nc = nc
        s.es = es

        def mk(name, eng):
            return Eng(name, eng, es.enter_context(nc.semaphore("sem_" + name)))

        s.pe = mk("pe", nc.tensor)
        s.dve = mk("dve", nc.vector)
        s.act = mk("act", nc.scalar)
        s.pool = mk("pool", nc.gpsimd)
        s.sp = mk("sp", nc.sync)
        s.engs = [s.pe, s.dve, s.act, s.pool, s.sp]
        s.dsems = [es.enter_context(nc.semaphore("dsem%d" % i)) for i in range(s.NDMA)]
        s.dcount = 0
        s.dlast = [0] * s.NDMA
        s.nname = 0

    def sbuf(s, es, shape, dt=F32, name=None):
        s.nname += 1
        return TB(es.enter_context(s.nc.sbuf_tensor("%s_%d" % (name or "sb", s.nname), list(shape), dt)))

    def psum(s, es, shape, dt=F32, name=None):
        s.nname += 1
        return TB(es.enter_context(s.nc.psum_tensor("%s_%d" % (name or "ps", s.nname), list(shape), dt)))

    def dram(s, name, shape, dt=F32, kind="Internal"):
        return TB(s.nc.dram_tensor(name, list(shape), dt, kind=kind).ap())

    def _wait(s, E, sem, val):
        k = id(sem)
        if E.seen.get(k, 0) >= val:
            return
        E.eng.wait_ge(sem, val)
        E.seen[k] = val

    def _deps(s, E, reads, writes):
        need = {}

        def add(d, raw):
            for k, (sem, val) in d.items():
                if sem is E.sem and not raw:
                    continue
                if k not in need or need[k][1] < val:
                    need[k] = (sem, val)

        for tb in reads:
            add(tb.b.w, True)
        for tb in writes:
            add(tb.b.w, False)
            add(tb.b.r, False)
        for k, (sem, val) in need.items():
            s._wait(E, sem, val)

    def _commit(s, tok, reads, writes):
        k = id(tok[0])
        for tb in reads:
            b = tb.b
            if k not in b.r or b.r[k][1] < tok[1]:
                b.r[k] = tok
        for tb in writes:
            tb.b.w = {k: tok}
            tb.b.r = {}

    def op(s, E, fn, r=(), w=()):
        s._deps(E, r, w)
        ins = fn()
        E.count += 1
        ins.then_inc(E.sem, 1)
        s._commit((E.sem, E.count), r, w)
        return ins

    def P(s, fn, r=(), w=()):
        return s.op(s.pe, fn, r, w)

    def V(s, fn, r=(), w=()):
        return s.op(s.dve, fn, r, w)

    def A(s, fn, r=(), w=()):
        return s.op(s.act, fn, r, w)

    def G(s, fn, r=(), w=()):
        return s.op(s.pool, fn, r, w)

    def dma(s, out, in_, r=(), w=(), Q=None, **kw):
        Q = Q or s.sp
        i = s.dcount % s.NDMA
        s.dcount += 1
        sem = s.dsems[i]
        prev = s.dlast[i]
        s._deps(Q, r, w)
        if prev:
            s._wait(Q, sem, prev)
        ins = Q.eng.dma_start(out=out, in_=in_, **kw)
        ins.then_inc(sem, 16)
        val = prev + 16
        s.dlast[i] = val
        s._commit((sem, val), r, w)
        return ins

    def barrier(s):
        toks = [(E.sem, E.count) for E in s.engs if E.count > 0]
        toks += [(s.dsems[i], s.dlast[i]) for i in range(s.NDMA) if s.dlast[i] > 0]
        for E in s.engs:
            for sem, val in toks:
                if sem is E.sem:
                    continue
                s._wait(E, sem, val)

    def finish(s):
        toks = [(E.sem, E.count) for E in s.engs if E.count > 0 and E is not s.sp]
        toks += [(s.dsems[i], s.dlast[i]) for i in range(s.NDMA) if s.dlast[i] > 0]
        for sem, val in toks:
            s._wait(s.sp, sem, val)


NMASK = 15


def make_consts():
    p = np.arange(128)[:, None]
    f = np.arange(128)[None, :]
    cm = np.zeros((128, NMASK, 128), np.float32)
    cm[:, 0] = (p < f)
    cm[:, 1] = (p <= f)
    cm[:, 2] = (p > f)
    cm[:, 3] = (p >= f)
    for i in range(4):
        cm[:, 4 + i] = np.where(cm[:, i] > 0, 0.0, NEG)
    for k in range(1, 8):
        b = 2 ** (k - 1)
        cm[:, 7 + k] = -(((p // (2 * b)) == (f // (2 * b))) & ((p // b) != (f // b))).astype(np.float32)
    ident = np.eye(128, dtype=np.float32)
    ones = np.ones((128, 128), np.float32)
    blk64 = (p // 64 == f // 64).astype(np.float32)
    perm = np.zeros((128, 128), np.float32)
    for i in range(64):
        perm[2 * i + 1, 2 * i] = -1.0
        perm[2 * i, 2 * i + 1] = 1.0
    sq = np.stack([ident, ones, blk64, perm], axis=1)
    pos = np.arange(T, dtype=np.float32)
    theta = (1.0 / (np.float32(10000.0) ** np.linspace(0.0, 1.0, 32, dtype=np.float32))).astype(np.float32)
    ang = (pos[:, None] * theta[None, :]).astype(np.float32)
    c = np.arange(128)
    idx = (c % 64) // 2
    cos = np.cos(ang).astype(np.float32)[:, idx].T.copy()
    sin = np.sin(ang).astype(np.float32)[:, idx].T.copy()
    posrow = np.broadcast_to(pos[None, :], (128, T)).copy()
    reset = np.broadcast_to(((np.arange(T) % 128) != 0).astype(np.float32)[None, :], (128, T)).copy()
    rows = np.stack([cos, sin, posrow, reset], axis=1)
    poscol = (np.arange(NB)[None, :] * 128 + np.arange(128)[:, None]).astype(np.float32)
    return {"c_masks": cm, "c_sq": sq, "c_rows": rows, "c_poscol": poscol}


def colmajor(v):
    v = np.asarray(v, np.float32).reshape(-1)
    n = (len(v) + 127) // 128 * 128
    vp = np.zeros(n, np.float32)
    vp[:len(v)] = v
    return vp.reshape(-1, 128).T.copy()


def make_widemask():
    p = np.arange(128)[:, None]
    u = np.arange(896)[None, :] - 384
    wm = np.zeros((128, 2, 896), np.float32)
    wm[:, 0] = np.where(u >= p, 0.0, NEG)
    wm[:, 1] = np.where(u < p, 0.0, NEG)
    return wm


class Rot:
    def __init__(s, k, es, n, shape, dt=F32, name="rot"):
        s.tiles = [k.sbuf(es, shape, dt, name) for _ in range(n)]
        s.i = 0

    def get(s):
        t = s.tiles[s.i % len(s.tiles)]
        s.i += 1
        return t


class Ctx:
    pass


def quad_attn(k, C, R, kT, qT, dk, pbase, vtok, voff, dv, gf_bc, eb_bc, colf, coltbs, yT, e0, e1):
    nc = k.nc
    wm = C.wm
    L = 2
    for tg in range(T // 512):
        ts = slice(tg * 512, (tg + 1) * 512)
        yps = C.psacc.get()
        kqs = {}
        Ws = {}

        def stageA(i):
            fwd = i <= 4 * tg + 3
            bwd = i >= 4 * tg
            diag = fwd and bwd
            off = i - 4 * tg
            W = R.wt.get()
            if not diag:
                if fwd:
                    k.A(lambda: nc.scalar.activation(out=W[:, :], in_=gf_bc[:, ts], func=AF.Exp, bias=colf(0, i), scale=1.0),
                        r=[gf_bc] + coltbs, w=[W])
                else:
                    k.A(lambda: nc.scalar.activation(out=W[:, :], in_=eb_bc[:, ts], func=AF.Exp, bias=colf(1, i), scale=-1.0),
                        r=[eb_bc] + coltbs, w=[W])
            else:
                msl = wm[:, 0, 384 - off * 128: 384 - off * 128 + 512]
                msl2 = wm[:, 1, 384 - off * 128: 384 - off * 128 + 512]
                a1 = R.arg.get()
                k.V(lambda: nc.vector.scalar_tensor_tensor(out=a1[:, :], in0=gf_bc[:, ts], scalar=colf(0, i), in1=msl,
                                                           op0=ALU.add, op1=ALU.add), r=[gf_bc, wm] + coltbs, w=[a1])
                ef = R.ex.get()
                k.A(lambda: nc.scalar.activation(out=ef[:, :], in_=a1[:, :], func=AF.Exp), r=[a1], w=[ef])
                a2 = R.arg.get()
                k.V(lambda: nc.vector.scalar_tensor_tensor(out=a2[:, :], in0=eb_bc[:, ts], scalar=colf(1, i), in1=msl2,
                                                           op0=ALU.subtract, op1=ALU.subtract), r=[eb_bc, wm] + coltbs, w=[a2])
                eb = R.ex.get()
                k.A(lambda: nc.scalar.activation(out=eb[:, :], in_=a2[:, :], func=AF.Exp, scale=-1.0), r=[a2], w=[eb])
                k.G(lambda: nc.gpsimd.tensor_tensor(out=W[:, :], in0=ef[:, :], in1=eb[:, :], op=ALU.add), r=[ef, eb], w=[W])
            Ws[i] = W

        def stageB(i):
            kq = C.psrot.get()
            k.P(lambda: nc.tensor.matmul(kq[:, :], lhsT=kT[pbase:pbase + dk, i * 128:(i + 1) * 128], rhs=qT[pbase:pbase + dk, ts],
                                         start=True, stop=True), r=[kT, qT], w=[kq])
            kqs[i] = kq

        def stageC(i):
            sc = R.sc.get()
            kq = kqs.pop(i)
            W = Ws.pop(i)
            k.V(lambda: nc.vector.tensor_tensor(out=sc[:, :], in0=kq[:, :], in1=W[:, :], op=ALU.mult), r=[kq, W], w=[sc])
            k.P(lambda: nc.tensor.matmul(yps[0:dv, :], lhsT=vtok[:, i, voff:voff + dv], rhs=sc[:, :], start=(i == 0), stop=(i == NB - 1)),
                r=[vtok, sc], w=[yps])

        for i in range(min(L, NB)):
            stageA(i)
            stageB(i)
        for i in range(NB):
            if i + L < NB:
                stageA(i + L)
                stageB(i + L)
            stageC(i)
        k.A(lambda: nc.scalar.copy(out=yT[e0:e1, ts], in_=yps[e0:e1, :]), r=[yps], w=[yT])


def transpose_blocks(k, C, src, srows, dst, dcol0, nblk=NB):
    nc = k.nc
    for i in range(0, nblk, 4):
        ps = C.psrot.get()
        for j in range(4):
            k.P(lambda: nc.tensor.transpose(ps[:, j * 128:j * 128 + srows], src[0:srows, (i + j) * 128:(i + j + 1) * 128],
                                            C.ident[0:srows, 0:srows]), r=[src, C.sq], w=[ps])
        k.A(lambda: nc.scalar.copy(out=dst[:, i:i + 4, dcol0:dcol0 + srows],
                                   in_=ps[:, :].rearrange("p (j c) -> p j c", j=4)[:, :, 0:srows]), r=[ps], w=[dst])


def load_rows(k, dst, drows, PT, r0, n):
    k.dma(dst[drows:drows + n, :], PT[r0:r0 + n, :], r=[PT], w=[dst])


def conv_silu(k, C, raw, acc, dst, wcol, bcol):
    nc = k.nc
    if bcol is None:
        k.V(lambda: nc.vector.tensor_scalar(out=acc[:, :], in0=raw[:, :], scalar1=wcol[:, 2:3], scalar2=None, op0=ALU.mult),
            r=[raw, C.prm], w=[acc])
    else:
        k.V(lambda: nc.vector.tensor_scalar(out=acc[:, :], in0=raw[:, :], scalar1=wcol[:, 2:3], scalar2=bcol, op0=ALU.mult, op1=ALU.add),
            r=[raw, C.prm], w=[acc])
    for j in (0, 1, 3, 4):
        sh = j - 2
        lo = max(0, -sh)
        hi = T - max(0, sh)
        k.V(lambda: nc.vector.scalar_tensor_tensor(out=acc[:, lo:hi], in0=raw[:, lo + sh:hi + sh], scalar=wcol[:, j:j + 1], in1=acc[:, lo:hi],
                                                   op0=ALU.mult, op1=ALU.add), r=[raw, C.prm, acc], w=[acc])
    k.A(lambda: nc.scalar.activation(out=dst[:, :], in_=acc[:, :], func=AF.Silu), r=[acc], w=[dst])


def sumsq_bcast(k, C, srcs, nparts, onesap, scale, eps, rstd, tmp):
    nc = k.nc
    M = onesap.shape[-1]
    for tg in range(T // 512):
        ts = slice(tg * 512, (tg + 1) * 512)
        ps = C.psrot.get()
        for n, src in enumerate(srcs):
            sq = C.R.arg.get()
            k.G(lambda: nc.gpsimd.tensor_tensor(out=sq[0:nparts, :], in0=src[0:nparts, ts], in1=src[0:nparts, ts], op=ALU.mult), r=[src], w=[sq])
            k.P(lambda: nc.tensor.matmul(ps[0:M, :], lhsT=onesap, rhs=sq[0:nparts, :], start=(n == 0), stop=(n == len(srcs) - 1)),
                r=[sq, C.sq], w=[ps])
        k.A(lambda: nc.scalar.activation(out=tmp[0:M, ts], in_=ps[0:M, :], func=AF.Sqrt, scale=scale, bias=float(eps)), r=[ps], w=[tmp])
    k.V(lambda: nc.vector.reciprocal(out=rstd[0:M, :], in_=tmp[0:M, :]), r=[tmp], w=[rstd])


def mixer_ssd(k, C, PT, YT, yrow0):
    nc = k.nc
    with ExitStack() as es:
        phase_consts(k, C, es, wm=True)
        P = C.pcol
        quad_pools(k, C, es)
        R = C.R
        colsT = [k.sbuf(es, [128, NB, 16], name="ssd_cols") for _ in range(2)]
        es2 = ExitStack()
        dtt = [k.sbuf(es2, [16, T], name="ssd_dt") for _ in range(6)]
        nA = k.sbuf(es2, [16, 2], name="ssd_nA")
        onesr = k.sbuf(es2, [16, T], name="ssd_ones")
        k.G(lambda: nc.gpsimd.memset(onesr[:, :], 1.0), w=[onesr])
        k.A(lambda: nc.scalar.activation(out=nA[:, :], in_=P("ssd_A_log", 16), func=AF.Exp), r=[C.prm], w=[nA])
        for d in range(2):
            raw = dtt[d]
            load_rows(k, raw, 0, PT, O_SSD + 3072 + 16 * d, 16)
            e = dtt[2]
            k.A(lambda: nc.scalar.activation(out=e[:, :], in_=raw[:, :], func=AF.Exp, bias=P("ssd_dt_bias", 16)[:, d:d + 1]), r=[raw, C.prm], w=[e])
            k.A(lambda: nc.scalar.activation(out=raw[:, :], in_=e[:, :], func=AF.Ln, bias=1.0), r=[e], w=[raw])
            la = dtt[3]
            k.V(lambda: nc.vector.tensor_scalar(out=la[:, :], in0=raw[:, :], scalar1=nA[:, d:d + 1], scalar2=-1.0, op0=ALU.mult, op1=ALU.mult),
                r=[raw, nA], w=[la])
            G = dtt[4 + d]
            k.V(lambda: nc.vector.tensor_tensor_scan(out=G[:, :], data0=onesr[:, :], data1=la[:, :], initial=0.0, op0=ALU.mult, op1=ALU.add),
                r=[la, onesr], w=[G])
            if d == 1:
                k.V(lambda: nc.vector.tensor_tensor(out=G[:, :], in0=G[:, :], in1=la[:, :], op=ALU.subtract), r=[G, la], w=[G])
        scr = C.ssd_scr
        k.dma(scr[0, :, :], dtt[4][:, :], r=[dtt[4]], w=[scr])
        k.dma(scr[1, :, :], dtt[5][:, :], r=[dtt[5]], w=[scr])
        for d in range(2):
            k.A(lambda: nc.scalar.activation(out=dtt[d][:, :], in_=dtt[d][:, :], func=AF.Ln), r=[dtt[d]], w=[dtt[d]])
        k.V(lambda: nc.vector.tensor_tensor(out=dtt[0][:, :], in0=dtt[0][:, :], in1=dtt[4][:, :], op=ALU.subtract), r=[dtt[0], dtt[4]], w=[dtt[0]])
        k.V(lambda: nc.vector.tensor_tensor(out=dtt[1][:, :], in0=dtt[1][:, :], in1=dtt[5][:, :], op=ALU.add), r=[dtt[1], dtt[5]], w=[dtt[1]])
        for a in range(2):
            transpose_blocks(k, C, dtt[a], 16, colsT[a], 0)
        k.barrier()
        es2.close()
        raw = k.sbuf(es, [128, T], name="ssd_raw")
        acc = k.sbuf(es, [128, T], name="ssd_acc")
        xs = [k.sbuf(es, [128, T], name="ssd_xs") for _ in range(2)]
        Bm = k.sbuf(es, [128, T], BF16, name="ssd_B")
        Cm = k.sbuf(es, [128, T], BF16, name="ssd_C")
        yc = [k.sbuf(es, [128, T], name="ssd_y") for _ in range(2)]
        vt = k.sbuf(es, [128, NB, 128], BF16, name="ssd_vt")
        gfb = k.sbuf(es, [128, T], name="ssd_gfb")
        ebb = k.sbuf(es, [128, T], name="ssd_ebb")
        rstd = k.sbuf(es, [128, T], name="ssd_rstd")
        ob = k.sbuf(es, [128, T], BF16, name="ssd_ob")
        cw = P("ssd_conv_w", 128)
        cb = P("ssd_conv_b", 128)
        for g in range(4):
            for j in range(2):
                ch = 2 * g + j
                load_rows(k, raw, 0, PT, O_SSD + 1024 + ch * 128, 128)
                conv_silu(k, C, raw, acc, xs[j], cw[:, ch * 5:(ch + 1) * 5], cb[:, ch:ch + 1])
            load_rows(k, raw, 0, PT, O_SSD + 2048 + g * 128, 128)
            conv_silu(k, C, raw, acc, Bm, cw[:, (8 + g) * 5:(9 + g) * 5], cb[:, 8 + g:9 + g])
            load_rows(k, raw, 0, PT, O_SSD + 2560 + g * 128, 128)
            conv_silu(k, C, raw, acc, Cm, cw[:, (12 + g) * 5:(13 + g) * 5], cb[:, 12 + g:13 + g])
            for j in range(2):
                ch = 2 * g + j
                transpose_blocks(k, C, xs[j], 128, vt, 0)
                for hh in range(2):
                    h = ch * 2 + hh
                    k.dma(gfb[:, :], scr[0, h:h + 1, :].partition_broadcast(128), r=[scr], w=[gfb])
                    k.dma(ebb[:, :], scr[1, h:h + 1, :].partition_broadcast(128), r=[scr], w=[ebb])
                    colf = (lambda a, i, h=h: colsT[a][:, i, h:h + 1])
                    quad_attn(k, C, R, Bm, Cm, 128, 0, vt, 0, 128, gfb, ebb, colf, colsT, yc[j], hh * 64, hh * 64 + 64)
                load_rows(k, raw, 0, PT, O_SSD + ch * 128, 128)
                k.A(lambda: nc.scalar.activation(out=acc[:, :], in_=raw[:, :], func=AF.Silu), r=[raw], w=[acc])
                k.V(lambda: nc.vector.scalar_tensor_tensor(out=yc[j][:, :], in0=xs[j][:, :], scalar=P("ssd_D", 128)[:, ch:ch + 1], in1=yc[j][:, :],
                                                           op0=ALU.mult, op1=ALU.add), r=[xs[j], yc[j], C.prm], w=[yc[j]])
                k.V(lambda: nc.vector.tensor_tensor(out=yc[j][:, :], in0=yc[j][:, :], in1=acc[:, :], op=ALU.mult), r=[yc[j], acc], w=[yc[j]])
            sumsq_bcast(k, C, yc, 128, C.ones[:, :], 1.0 / 256, 1e-6, rstd, raw)
            for j in range(2):
                ch = 2 * g + j
                k.V(lambda: nc.vector.scalar_tensor_tensor(out=ob[:, :], in0=yc[j][:, :], scalar=P("ssd_norm_w", 128)[:, ch:ch + 1], in1=rstd[:, :],
                                                           op0=ALU.mult, op1=ALU.mult), r=[yc[j], rstd, C.prm], w=[ob])
                k.dma(YT[yrow0 + ch * 128: yrow0 + (ch + 1) * 128, :], ob[:, :], r=[ob], w=[YT])
        k.barrier()


def mixer_ret(k, C, PT, YT, yrow0):
    nc = k.nc
    with ExitStack() as es:
        phase_consts(k, C, es, wm=True, poscol=True)
        quad_pools(k, C, es)
        R = C.R
        raw = k.sbuf(es, [128, T], name="ret_raw")
        t1 = k.sbuf(es, [128, T], name="ret_t1")
        qr = k.sbuf(es, [128, T], BF16, name="ret_q")
        kr = k.sbuf(es, [128, T], BF16, name="ret_k")
        racc = k.sbuf(es, [128, T], name="ret_racc")
        vt = k.sbuf(es, [128, NB, 128], BF16, name="ret_vt")
        gfb = k.sbuf(es, [128, T], name="ret_gfb")
        cols = k.sbuf(es, [128, 4, NB], name="ret_cols")
        yh = k.sbuf(es, [128, T], name="ret_y")
        rstd = k.sbuf(es, [128, T], name="ret_rstd")
        ob = k.sbuf(es, [128, T], BF16, name="ret_ob")
        k.G(lambda: nc.gpsimd.memset(cols[:, 2:4, :], 1.0), w=[cols])
        rows = k.sbuf(es, [128, 3, T], name="ret_rows")
        k.dma(rows[:, :, :], C.d_rows[:, 0:3, :], r=[C.d_rows], w=[rows])

        def rope(src_row0, dst, scl):
            load_rows(k, raw, 0, PT, src_row0, 128)
            k.G(lambda: nc.gpsimd.tensor_tensor(out=t1[:, :], in0=raw[:, :], in1=rows[:, 0, :], op=ALU.mult), r=[raw, rows], w=[t1])
            for tg in range(T // 512):
                ts = slice(tg * 512, (tg + 1) * 512)
                ps = C.psrot.get()
                k.P(lambda: nc.tensor.matmul(ps[:, :], lhsT=C.perm[:, :], rhs=raw[:, ts], start=True, stop=True), r=[raw, C.sq], w=[ps])
                k.V(lambda: nc.vector.tensor_tensor(out=racc[:, ts], in0=ps[:, :], in1=rows[:, 1, ts], op=ALU.mult), r=[ps, rows], w=[racc])
            k.V(lambda: nc.vector.tensor_tensor(out=racc[:, :], in0=racc[:, :], in1=t1[:, :], op=ALU.add), r=[racc, t1], w=[racc])
            k.V(lambda: nc.vector.tensor_scalar(out=dst[:, :], in0=racc[:, :], scalar1=float(scl), scalar2=None, op0=ALU.mult), r=[racc], w=[dst])

        for h in range(8):
            if h % 2 == 0:
                rope(O_RET + (h // 2) * 128, qr, 1.0)
                rope(O_RET + 512 + (h // 2) * 128, kr, 0.125)
            lg = math.log(1.0 - 2.0 ** (-5.0 - h))
            k.V(lambda: nc.vector.tensor_scalar(out=gfb[:, :], in0=rows[:, 2, :], scalar1=lg, scalar2=None, op0=ALU.mult), r=[rows], w=[gfb])
            k.V(lambda: nc.vector.tensor_scalar(out=cols[:, 0:1, :], in0=C.poscol[:, :].unsqueeze(1), scalar1=-lg, scalar2=None, op0=ALU.mult),
                r=[C.poscol], w=[cols])
            k.V(lambda: nc.vector.tensor_scalar(out=cols[:, 1:2, :], in0=C.poscol[:, :].unsqueeze(1), scalar1=lg, scalar2=None, op0=ALU.mult),
                r=[C.poscol], w=[cols])
            load_rows(k, raw, 0, PT, O_RET + 1024 + h * 128, 128)
            transpose_blocks(k, C, raw, 128, vt, 0)
            colf = (lambda a, i: cols[:, a, i:i + 1])
            quad_attn(k, C, R, kr, qr, 64, (h % 2) * 64, vt, 0, 128, gfb, gfb, colf, [cols], yh, 0, 128)
            sumsq_bcast(k, C, [yh], 128, C.ones[:, :], 1.0 / 128, 1e-6, rstd, t1)
            load_rows(k, raw, 0, PT, O_RET + 2048 + h * 128, 128)
            k.A(lambda: nc.scalar.activation(out=t1[:, :], in_=raw[:, :], func=AF.Silu), r=[raw], w=[t1])
            k.V(lambda: nc.vector.tensor_tensor(out=yh[:, :], in0=yh[:, :], in1=rstd[:, :], op=ALU.mult), r=[yh, rstd], w=[yh])
            k.V(lambda: nc.vector.tensor_tensor(out=ob[:, :], in0=yh[:, :], in1=t1[:, :], op=ALU.mult), r=[yh, t1], w=[ob])
            k.dma(YT[yrow0 + h * 128: yrow0 + (h + 1) * 128, :], ob[:, :], r=[ob], w=[YT])
        k.barrier()


def chunkT(a, nch):
    a = np.asarray(a, np.float32)
    n, Cc = a.shape
    return a.T.reshape(nch, 128, n).transpose(1, 0, 2).reshape(128, nch * n).copy()


def pack_layer_params(inp, l):
    ent = {}
    ent["ssd_A_log"] = np.asarray(inp["ssd_A_log"][l], np.float32).T.copy()
    ent["ssd_dt_bias"] = np.asarray(inp["ssd_dt_bias"][l], np.float32).T.copy()
    ent["ssd_conv_w"] = chunkT(inp["ssd_conv_w"][l], 16)
    ent["ssd_conv_b"] = colmajor(inp["ssd_conv_b"][l])
    ent["ssd_D"] = colmajor(np.repeat(np.asarray(inp["ssd_D"][l], np.float32), 64))
    ent["ssd_norm_w"] = colmajor(inp["ssd_norm_w"][l])
    ent["gdn_conv_w"] = chunkT(inp["gdn_conv_w"][l], 16)
    ent["gdn_A_log"] = np.asarray(inp["gdn_A_log"][l], np.float32).T.copy()
    ent["gdn_dt_bias"] = np.asarray(inp["gdn_dt_bias"][l], np.float32).T.copy()
    ent["gdn_norm_w"] = colmajor(inp["gdn_norm_w"][l])
    mu = np.asarray(inp["rwkv_mu"][l], np.float32)
    hl = lambda v: np.asarray(v, np.float32).reshape(16, 64).T.copy()
    ent["rwkv_mu_r"] = hl(mu[0:1024])
    ent["rwkv_mu_k"] = hl(mu[1024:2048])
    ent["rwkv_mu_v"] = hl(mu[2048:3072])
    ent["rwkv_mu_lo"] = mu[3072:3264].reshape(3, 64).T.copy()
    ent["rwkv_mu_g"] = colmajor(mu[3264:3424])
    ent["rwkv_w0"] = np.concatenate([colmajor(inp["rwkv_w0"][l][0]), colmajor(inp["rwkv_w0"][l][1])], axis=1)
    ent["rwkv_a0"] = colmajor(inp["rwkv_a0"][l])
    ent["rwkv_k_k"] = hl(inp["rwkv_k_k"][l])
    ent["rwkv_k_a"] = hl(inp["rwkv_k_a"][l])
    ent["rwkv_r_k"] = hl(np.asarray(inp["rwkv_r_k"][l]).reshape(-1))
    ent["rwkv_ln_w"] = hl(inp["rwkv_ln_w"][l])
    ent["rwkv_ln_b"] = hl(inp["rwkv_ln_b"][l])
    ent["attn_norm_g"] = colmajor(inp["attn_norm_g"][l])
    ent["ffn_norm_g"] = colmajor(inp["ffn_norm_g"][l])
    ent["final_norm_g"] = colmajor(inp["final_norm_g"])
    return ent


def layout_params(ent):
    off = {}
    c = 0
    for name, a in ent.items():
        off[name] = (c, a.shape[0], a.shape[1])
        c += a.shape[1]
    arr = np.zeros((128, c), np.float32)
    for name, a in ent.items():
        o, n, m = off[name]
        arr[0:n, o:o + m] = a
    return arr, off


def setup_ctx(k, es, nc, dr, prm_off):
    C = Ctx()
    C.dr = dr
    C.d_rows = dr["c_rows"]
    nprm = sum(v[2] for v in prm_off.values())
    C.nprm = nprm
    C.prm_off = prm_off
    return C


def phase_consts(k, C, es, masks=False, wm=False, poscol=False):
    dr = C.dr
    C.sq = k.sbuf(es, [128, 4, 128], name="c_sq")
    k.dma(C.sq[:, :, :], dr["c_sq"][:, :, :], r=[dr["c_sq"]], w=[C.sq])
    C.ident = C.sq[:, 0, :]
    C.ones = C.sq[:, 1, :]
    C.blk64 = C.sq[:, 2, :]
    C.perm = C.sq[:, 3, :]
    if masks:
        C.masks = k.sbuf(es, [128, NMASK, 128], name="c_masks")
        k.dma(C.masks[:, :, :], dr["c_masks"][:, :, :], r=[dr["c_masks"]], w=[C.masks])
    if wm:
        C.wm = k.sbuf(es, [128, 2, 896], name="c_wm")
        k.dma(C.wm[:, :, :], dr["c_wm"][:, :, :], r=[dr["c_wm"]], w=[C.wm])
    if poscol:
        C.poscol = k.sbuf(es, [128, NB], name="c_poscol")
        k.dma(C.poscol[:, :], dr["c_poscol"][:, :], r=[dr["c_poscol"]], w=[C.poscol])
    C.prm = k.sbuf(es, [128, C.nprm], name="prm")
    k.dma(C.prm[:, :], C.prm_dram[:, :], r=[C.prm_dram], w=[C.prm])
    prm_off = C.prm_off
    C.pcol = lambda name, nparts=128: C.prm[0:nparts, prm_off[name][0]:prm_off[name][0] + prm_off[name][2]]


def mkpool(tiles):
    pr = Ctx()
    pr.tiles = tiles
    pr.i = 0

    def get():
        t = pr.tiles[pr.i % len(pr.tiles)]
        pr.i += 1
        return t
    pr.get = get
    return pr


def quad_pools(k, C, es):
    ps = [k.psum(es, [128, 512], name="psb") for _ in range(8)]
    C.psrot = mkpool(ps[0:6])
    C.psacc = mkpool(ps[6:8])
    R = Ctx()
    R.arg = Rot(k, es, 4, [128, 512], name="r_arg")
    R.ex = Rot(k, es, 4, [128, 512], name="r_ex")
    R.wt = Rot(k, es, 5, [128, 512], name="r_wt")
    R.sc = Rot(k, es, 3, [128, 512], BF16, name="r_sc")
    C.R = R


import os
def tri_inverse_batch(k, C, units, pq):
    nc = k.nc
    nm = lambda lv: C.masks[:, 7 + lv, :]
    for u in units:
        k.G(lambda: nc.gpsimd.tensor_tensor(out=u["Ta"][:, :], in0=u["LL"][:, :], in1=nm(1), op=ALU.mult), r=[u["LL"], C.masks], w=[u["Ta"]])
        k.G(lambda: nc.gpsimd.tensor_tensor(out=u["Ta"][:, :], in0=u["Ta"][:, :], in1=C.ident, op=ALU.add), r=[u["Ta"], C.sq], w=[u["Ta"]])
        k.G(lambda: nc.gpsimd.tensor_tensor(out=u["Wa"][:, :], in0=u["LT"][:, :], in1=nm(1), op=ALU.mult), r=[u["LT"], C.masks], w=[u["Wa"]])
        k.G(lambda: nc.gpsimd.tensor_tensor(out=u["Wa"][:, :], in0=u["Wa"][:, :], in1=C.ident, op=ALU.add), r=[u["Wa"], C.sq], w=[u["Wa"]])
        u["T"], u["Tn"], u["W"], u["Wn"] = u["Ta"], u["Tb"], u["Wa"], u["Wb"]
    for lv in range(2, 8 if not os.environ.get("K_SKIP_INV") else 2):
        last = lv == 7
        pz = {}
        for ui, u in enumerate(units):
            if not last:
                p1 = pq.get()
                k.P(lambda: nc.tensor.matmul(p1[:, :], lhsT=u["LT"][:, :], rhs=u["T"][:, :], start=True, stop=True), r=[u["LT"], u["T"]], w=[p1])
                pz[(ui, 0)] = p1
            p2 = pq.get()
            k.P(lambda: nc.tensor.matmul(p2[:, :], lhsT=u["LL"][:, :], rhs=u["W"][:, :], start=True, stop=True), r=[u["LL"], u["W"]], w=[p2])
            pz[(ui, 1)] = p2
        for ui, u in enumerate(units):
            if not last:
                p1 = pz[(ui, 0)]
                k.V(lambda: nc.vector.tensor_tensor(out=u["Z1"][:, :], in0=p1[:, :], in1=nm(lv), op=ALU.mult), r=[p1, C.masks], w=[u["Z1"]])
            p2 = pz[(ui, 1)]
            k.V(lambda: nc.vector.tensor_tensor(out=u["Z2"][:, :], in0=p2[:, :], in1=nm(lv), op=ALU.mult), r=[p2, C.masks], w=[u["Z2"]])
        for ui, u in enumerate(units):
            if not last:
                p1 = pq.get()
                k.P(lambda: nc.tensor.matmul(p1[:, :], lhsT=u["W"][:, :], rhs=u["Z1"][:, :], start=True, stop=True), r=[u["W"], u["Z1"]], w=[p1])
                pz[(ui, 0)] = p1
            p2 = pq.get()
            k.P(lambda: nc.tensor.matmul(p2[:, :], lhsT=u["T"][:, :], rhs=u["Z2"][:, :], start=True, stop=True), r=[u["T"], u["Z2"]], w=[p2])
            pz[(ui, 1)] = p2
        for ui, u in enumerate(units):
            if not last:
                p1 = pz[(ui, 0)]
                k.V(lambda: nc.vector.tensor_tensor(out=u["Tn"][:, :], in0=u["T"][:, :], in1=p1[:, :], op=ALU.add), r=[u["T"], p1], w=[u["Tn"]])
            p2 = pz[(ui, 1)]
            k.V(lambda: nc.vector.tensor_tensor(out=u["Wn"][:, :], in0=u["W"][:, :], in1=p2[:, :], op=ALU.add), r=[u["W"], p2], w=[u["Wn"]])
            u["T"], u["Tn"] = u["Tn"], u["T"]
            u["W"], u["Wn"] = u["Wn"], u["W"]


def alloc_units(k, es, n):
    units = []
    for i in range(n):
        u = {}
        for nm_ in ("LL", "LT", "Ta", "Tb", "Wa", "Wb", "Z1", "Z2"):
            u[nm_] = k.sbuf(es, [128, 128], F32R, name="u_" + nm_)
        units.append(u)
    return units


def softplus_rows(k, src, tmp, dst, n, biascol, r_extra):
    nc = k.nc
    k.A(lambda: nc.scalar.activation(out=tmp[0:n, :], in_=src[0:n, :], func=AF.Exp, bias=biascol), r=[src] + r_extra, w=[tmp])
    k.A(lambda: nc.scalar.activation(out=dst[0:n, :], in_=tmp[0:n, :], func=AF.Ln, bias=1.0), r=[tmp], w=[dst])


def chunk_cumsum(k, C, lg, reset, pref, G, n, backward):
    nc = k.nc
    k.V(lambda: nc.vector.tensor_tensor_scan(out=pref[0:n, :], data0=reset[0:n, :], data1=lg[0:n, :], initial=0.0, op0=ALU.mult, op1=ALU.add),
        r=[lg, reset], w=[pref])
    p3 = pref[0:n, :].rearrange("p (c j) -> p c j", j=128)
    tot = p3[:, :, 127:128]
    if not backward:
        k.V(lambda: nc.vector.tensor_copy(out=G[0:n, :], in_=pref[0:n, :]), r=[pref], w=[G])
    else:
        g3 = G[0:n, :].rearrange("p (c j) -> p c j", j=128)
        l3 = lg[0:n, :].rearrange("p (c j) -> p c j", j=128)
        k.V(lambda: nc.vector.tensor_tensor(out=g3, in0=l3, in1=p3, op=ALU.subtract), r=[lg, pref], w=[G])
        k.V(lambda: nc.vector.tensor_tensor(out=g3, in0=g3, in1=tot.to_broadcast([n, NB, 128]), op=ALU.add), r=[G, pref], w=[G])
    return tot


def mixer_gdn(k, C, PT, YT, yrow0):
    nc = k.nc
    NU = 8
    with ExitStack() as es:
        phase_consts(k, C, es, masks=True)
        P = C.pcol
        psb = [k.psum(es, [128, 512], name="psb") for _ in range(2)]
        C.psrot = mkpool(psb)
        pqb = [k.psum(es, [128, 512], name="psq") for _ in range(6)]
        pq = mkpool([TBQ(b_, c0) for c0 in (0, 128, 256, 384) for b_ in pqb])
        cols = [k.sbuf(es, [128, NB, 8], name="gdn_cols") for _ in range(7)]
        etot_bc = k.sbuf(es, [128, 2, 8, NB], name="gdn_etot")
        scr = C.gdn_scr
        scr2 = C.gdn_scr2
        es2 = ExitStack()
        reset = k.sbuf(es2, [8, T], name="g_reset")
        k.dma(reset[:, :], C.d_rows[0:8, 3, :], r=[C.d_rows], w=[reset])
        beta = k.sbuf(es2, [8, T], name="g_beta")
        tA = k.sbuf(es2, [8, T], name="g_tA")
        tB = k.sbuf(es2, [8, T], name="g_tB")
        lg = k.sbuf(es2, [8, T], name="g_lg")
        G = k.sbuf(es2, [8, T], name="g_G")
        eG = k.sbuf(es2, [8, T], name="g_eG")
        eA = k.sbuf(es2, [8, 2], name="g_eA")
        et = k.sbuf(es2, [8, NB], name="g_et")
        k.A(lambda: nc.scalar.activation(out=eA[:, :], in_=P("gdn_A_log", 8), func=AF.Exp), r=[C.prm], w=[eA])
        load_rows(k, tA, 0, PT, O_GDN + 3072, 8)
        k.A(lambda: nc.scalar.activation(out=beta[:, :], in_=tA[:, :], func=AF.Sigmoid), r=[tA], w=[beta])
        k.dma(scr[4, :, :], beta[:, :], r=[beta], w=[scr])
        transpose_blocks(k, C, beta, 8, cols[6], 0)
        for d in range(2):
            load_rows(k, tA, 0, PT, O_GDN + 3080 + 8 * d, 8)
            softplus_rows(k, tA, tB, tA, 8, P("gdn_dt_bias", 8)[:, d:d + 1], [C.prm])
            k.V(lambda: nc.vector.tensor_scalar(out=lg[:, :], in0=tA[:, :], scalar1=eA[:, d:d + 1], scalar2=-1.0, op0=ALU.mult, op1=ALU.mult),
                r=[tA, eA], w=[lg])
            tot = chunk_cumsum(k, C, lg, reset, tB, G, 8, d == 1)
            k.dma(scr[d, :, :], G[:, :], r=[G], w=[scr])
            transpose_blocks(k, C, G, 8, cols[0 + d], 0)
            k.A(lambda: nc.scalar.activation(out=eG[:, :], in_=G[:, :], func=AF.Exp), r=[G], w=[eG])
            k.dma(scr[2 + d, :, :], eG[:, :], r=[eG], w=[scr])
            k.V(lambda: nc.vector.tensor_tensor(out=eG[:, :], in0=eG[:, :], in1=beta[:, :], op=ALU.mult), r=[eG, beta], w=[eG])
            transpose_blocks(k, C, eG, 8, cols[2 + d], 0)
            g3 = G[:, :].rearrange("p (c j) -> p c j", j=128)
            k.V(lambda: nc.vector.tensor_tensor(out=g3, in0=tot.to_broadcast([8, NB, 128]), in1=g3, op=ALU.subtract), r=[G, tB], w=[G])
            k.A(lambda: nc.scalar.activation(out=G[:, :], in_=G[:, :], func=AF.Exp), r=[G], w=[G])
            transpose_blocks(k, C, G, 8, cols[4 + d], 0)
            k.A(lambda: nc.scalar.activation(out=et[:, :], in_=tot.rearrange("p c o -> p (c o)"), func=AF.Exp), r=[tB], w=[et])
            k.dma(scr2[d, :, :], et[:, :], r=[et], w=[scr2])
        k.dma(etot_bc[:, :, :, :].rearrange("p a h c -> p (a h c)"),
              scr2[:, :, :].rearrange("a h c -> (a h c)").partition_broadcast(128), r=[scr2], w=[etot_bc])
        k.barrier()
        es2.close()
        tX = k.sbuf(es, [128, T], name="gdn_tX")
        tY = k.sbuf(es, [128, T], name="gdn_tY")
        qT = k.sbuf(es, [128, T], name="gdn_q")
        kT = k.sbuf(es, [128, T], name="gdn_k")
        vT = k.sbuf(es, [128, T], name="gdn_v")
        rs = k.sbuf(es, [128, T], name="gdn_rs")
        ktok = k.sbuf(es, [128, NB, 128], name="gdn_ktok")
        vb = k.sbuf(es, [128, NB, 128], name="gdn_vb")
        kbg = k.sbuf(es, [128, NB, 128], name="gdn_kbg")
        kdec = k.sbuf(es, [128, NB, 128], name="gdn_kdec")
        U = k.sbuf(es, [128, NB, 128], name="gdn_U")
        WT = k.sbuf(es, [128, T], name="gdn_WT")
        attnT = k.sbuf(es, [128, NB, 128], name="gdn_attnT")
        yT = k.sbuf(es, [128, T], name="gdn_yT")
        Gbc = k.sbuf(es, [128, T], name="gdn_Gbc")
        Bbc = k.sbuf(es, [128, T], name="gdn_Bbc")
        Ebc = k.sbuf(es, [128, T], name="gdn_Ebc")
        ob = k.sbuf(es, [128, T], BF16, name="gdn_ob")
        Ss = [k.sbuf(es, [128, 128], name="gdn_S") for _ in range(2)]
        vn = [k.sbuf(es, [128, 128], name="gdn_vn") for _ in range(2)]
        dec = [k.sbuf(es, [128, 128], name="gdn_dec") for _ in range(6)]
        units = alloc_units(k, es, NU)
        cw = P("gdn_conv_w", 128)
        deci = [0]

        def decay_tile(c, d, h, transposed, mask_idx):
            cs = slice(c * 128, (c + 1) * 128)
            a = dec[deci[0] % 6]
            deci[0] += 1
            gcol = cols[0 + d][:, c, h:h + 1]
            if not transposed:
                k.V(lambda: nc.vector.scalar_tensor_tensor(out=a[:, :], in0=Gbc[:, cs], scalar=gcol, in1=C.masks[:, mask_idx, :],
                                                           op0=ALU.subtract, op1=ALU.add), r=[Gbc, cols[0 + d], C.masks], w=[a])
                k.A(lambda: nc.scalar.activation(out=a[:, :], in_=a[:, :], func=AF.Exp), r=[a], w=[a])
            else:
                k.V(lambda: nc.vector.scalar_tensor_tensor(out=a[:, :], in0=Gbc[:, cs], scalar=gcol, in1=C.masks[:, mask_idx, :],
                                                           op0=ALU.subtract, op1=ALU.subtract), r=[Gbc, cols[0 + d], C.masks], w=[a])
                k.A(lambda: nc.scalar.activation(out=a[:, :], in_=a[:, :], func=AF.Exp, scale=-1.0), r=[a], w=[a])
            return a

        for m in range(4):
            for (dst, row0, chn, scl) in ((qT, O_GDN + m * 128, m, 128.0 ** -0.5), (kT, O_GDN + 512 + m * 128, 4 + m, 1.0)):
                load_rows(k, tX, 0, PT, row0, 128)
                conv_silu(k, C, tX, tY, dst, cw[:, chn * 5:(chn + 1) * 5], None)
                for tg in range(T // 512):
                    ts = slice(tg * 512, (tg + 1) * 512)
                    ps = C.psrot.get()
                    k.G(lambda: nc.gpsimd.tensor_tensor(out=tX[:, ts], in0=dst[:, ts], in1=dst[:, ts], op=ALU.mult), r=[dst], w=[tX])
                    k.P(lambda: nc.tensor.matmul(ps[:, :], lhsT=C.ones, rhs=tX[:, ts], start=True, stop=True), r=[tX, C.sq], w=[ps])
                    k.A(lambda: nc.scalar.activation(out=tY[:, ts], in_=ps[:, :], func=AF.Sqrt, bias=1e-6), r=[ps], w=[tY])
                k.V(lambda: nc.vector.reciprocal(out=rs[:, :], in_=tY[:, :]), r=[tY], w=[rs])
                k.V(lambda: nc.vector.scalar_tensor_tensor(out=dst[:, :], in0=dst[:, :], scalar=float(scl), in1=rs[:, :], op0=ALU.mult, op1=ALU.mult),
                    r=[dst, rs], w=[dst])
            transpose_blocks(k, C, kT, 128, ktok, 0)
            for h in (2 * m, 2 * m + 1):
                load_rows(k, tX, 0, PT, O_GDN + 1024 + h * 128, 128)
                conv_silu(k, C, tX, tY, vT, cw[:, (8 + h) * 5:(9 + h) * 5], None)
                transpose_blocks(k, C, vT, 128, vb, 0)
                bcol = cols[6][:, :, h:h + 1]
                k.V(lambda: nc.vector.tensor_tensor(out=vb[:, :, :], in0=vb[:, :, :], in1=bcol.to_broadcast([128, NB, 128]), op=ALU.mult),
                    r=[vb, cols[6]], w=[vb])
                k.dma(Bbc[:, :], scr[4, h:h + 1, :].partition_broadcast(128), r=[scr], w=[Bbc])
                k.G(lambda: nc.gpsimd.tensor_tensor(out=tX[:, :], in0=kT[:, :], in1=Bbc[:, :], op=ALU.mult), r=[kT, Bbc], w=[tX])
                for d in range(2):
                    k.dma(Gbc[:, :], scr[d, h:h + 1, :].partition_broadcast(128), r=[scr], w=[Gbc])
                    k.dma(Ebc[:, :], scr[2 + d, h:h + 1, :].partition_broadcast(128), r=[scr], w=[Ebc])
                    k.G(lambda: nc.gpsimd.tensor_tensor(out=tY[:, :], in0=qT[:, :], in1=Ebc[:, :], op=ALU.mult), r=[qT, Ebc], w=[tY])
                    c1 = cols[2 + d][:, :, h:h + 1]
                    c2 = cols[4 + d][:, :, h:h + 1]
                    k.V(lambda: nc.vector.tensor_tensor(out=kbg[:, :, :], in0=ktok[:, :, :], in1=c1.to_broadcast([128, NB, 128]), op=ALU.mult),
                        r=[ktok, cols[2 + d]], w=[kbg])
                    k.V(lambda: nc.vector.tensor_tensor(out=kdec[:, :, :], in0=ktok[:, :, :], in1=c2.to_broadcast([128, NB, 128]), op=ALU.mult),
                        r=[ktok, cols[4 + d]], w=[kdec])
                    m_st, m_in, m_ts = (4, 5, 6) if d == 0 else (6, 7, 4)
                    for c0 in range(0, NB, NU):
                        for ui in range(NU):
                            c = c0 + ui
                            u = units[ui]
                            cs = slice(c * 128, (c + 1) * 128)
                            dts = decay_tile(c, d, h, False, m_st)
                            dti = decay_tile(c, d, h, False, m_in)
                            dtt_ = decay_tile(c, d, h, True, m_ts)
                            p1 = pq.get()
                            k.P(lambda: nc.tensor.matmul(p1[:, :], lhsT=kT[:, cs], rhs=tX[:, cs], start=True, stop=True), r=[kT, tX], w=[p1])
                            k.V(lambda: nc.vector.tensor_tensor(out=u["LT"][:, :], in0=p1[:, :], in1=dts[:, :], op=ALU.mult), r=[p1, dts], w=[u["LT"]])
                            p2 = pq.get()
                            k.P(lambda: nc.tensor.matmul(p2[:, :], lhsT=tX[:, cs], rhs=kT[:, cs], start=True, stop=True), r=[kT, tX], w=[p2])
                            k.V(lambda: nc.vector.tensor_tensor(out=u["LL"][:, :], in0=p2[:, :], in1=dtt_[:, :], op=ALU.mult), r=[p2, dtt_], w=[u["LL"]])
                            p3 = pq.get()
                            k.P(lambda: nc.tensor.matmul(p3[:, :], lhsT=kT[:, cs], rhs=qT[:, cs], start=True, stop=True), r=[kT, qT], w=[p3])
                            k.V(lambda: nc.vector.tensor_tensor(out=attnT[:, c, :], in0=p3[:, :], in1=dti[:, :], op=ALU.mult), r=[p3, dti], w=[attnT])
                        tri_inverse_batch(k, C, units, pq)
                        for ui in range(NU):
                            c = c0 + ui
                            u = units[ui]
                            cs = slice(c * 128, (c + 1) * 128)
                            p1 = pq.get()
                            k.P(lambda: nc.tensor.matmul(p1[:, :], lhsT=u["W"][:, :].bitcast(F32), rhs=vb[:, c, :], start=True, stop=True), r=[u["W"], vb], w=[p1])
                            k.A(lambda: nc.scalar.copy(out=U[:, c, :], in_=p1[:, :]), r=[p1], w=[U])
                            p2 = pq.get()
                            k.P(lambda: nc.tensor.matmul(p2[:, :], lhsT=kbg[:, c, :], rhs=u["W"][:, :].bitcast(F32), start=True, stop=True), r=[u["W"], kbg], w=[p2])
                            k.A(lambda: nc.scalar.copy(out=WT[:, cs], in_=p2[:, :]), r=[p2], w=[WT])
                    S = Ss[0]
                    k.G(lambda: nc.gpsimd.memset(S[:, :], 0.0), w=[S])
                    order = range(NB) if d == 0 else range(NB - 1, -1, -1)
                    for n_, c in enumerate(order):
                        cs = slice(c * 128, (c + 1) * 128)
                        Sn = Ss[(n_ + 1) % 2]
                        v_ = vn[n_ % 2]
                        p1 = pq.get()
                        k.P(lambda: nc.tensor.matmul(p1[:, :], lhsT=WT[:, cs], rhs=S[:, :], start=True, stop=True), r=[WT, S], w=[p1])
                        k.V(lambda: nc.vector.tensor_tensor(out=v_[:, :], in0=U[:, c, :], in1=p1[:, :], op=ALU.subtract), r=[U, p1], w=[v_])
                        p3 = pq.get()
                        k.P(lambda: nc.tensor.matmul(p3[:, :], lhsT=kdec[:, c, :], rhs=v_[:, :], start=True, stop=True), r=[kdec, v_], w=[p3])
                        p2 = pq.get()
                        k.P(lambda: nc.tensor.matmul(p2[:, :], lhsT=S[:, :], rhs=tY[:, cs], start=True, stop=False), r=[S, tY], w=[p2])
                        k.P(lambda: nc.tensor.matmul(p2[:, :], lhsT=v_[:, :], rhs=attnT[:, c, :], start=False, stop=True), r=[v_, attnT], w=[p2])
                        k.V(lambda: nc.vector.scalar_tensor_tensor(out=Sn[:, :], in0=S[:, :], scalar=etot_bc[:, d, h, c:c + 1], in1=p3[:, :],
                                                                   op0=ALU.mult, op1=ALU.add), r=[S, etot_bc, p3], w=[Sn])
                        if d == 0:
                            k.A(lambda: nc.scalar.copy(out=yT[:, cs], in_=p2[:, :]), r=[p2], w=[yT])
                        else:
                            k.V(lambda: nc.vector.tensor_tensor(out=yT[:, cs], in0=yT[:, cs], in1=p2[:, :], op=ALU.add), r=[yT, p2], w=[yT])
                        S = Sn
                for tg in range(T // 512):
                    ts = slice(tg * 512, (tg + 1) * 512)
                    ps = C.psrot.get()
                    k.G(lambda: nc.gpsimd.tensor_tensor(out=tX[:, ts], in0=yT[:, ts], in1=yT[:, ts], op=ALU.mult), r=[yT], w=[tX])
                    k.P(lambda: nc.tensor.matmul(ps[:, :], lhsT=C.ones, rhs=tX[:, ts], start=True, stop=True), r=[tX, C.sq], w=[ps])
                    k.A(lambda: nc.scalar.activation(out=tY[:, ts], in_=ps[:, :], func=AF.Sqrt, scale=1.0 / 128, bias=1e-6), r=[ps], w=[tY])
                k.V(lambda: nc.vector.reciprocal(out=rs[:, :], in_=tY[:, :]), r=[tY], w=[rs])
                load_rows(k, tX, 0, PT, O_GDN + 2048 + h * 128, 128)
                k.A(lambda: nc.scalar.activation(out=tY[:, :], in_=tX[:, :], func=AF.Silu), r=[tX], w=[tY])
                k.V(lambda: nc.vector.scalar_tensor_tensor(out=yT[:, :], in0=yT[:, :], scalar=P("gdn_norm_w", 128)[:, 0:1], in1=rs[:, :],
                                                           op0=ALU.mult, op1=ALU.mult), r=[yT, rs, C.prm], w=[yT])
                k.V(lambda: nc.vector.tensor_tensor(out=ob[:, :], in0=yT[:, :], in1=tY[:, :], op=ALU.mult), r=[yT, tY], w=[ob])
                k.dma(YT[yrow0 + h * 128: yrow0 + (h + 1) * 128, :], ob[:, :], r=[ob], w=[YT])
        k.barrier()


def tshift(k, raw, tmp, dst, n, mucol, rdeps):
    nc = k.nc
    k.G(lambda: nc.gpsimd.tensor_tensor(out=tmp[0:n, 1:T - 1], in0=raw[0:n, 0:T - 2], in1=raw[0:n, 2:T], op=ALU.add), r=[raw], w=[tmp])
    k.G(lambda: nc.gpsimd.tensor_copy(out=tmp[0:n, 0:1], in_=raw[0:n, 1:2]), r=[raw], w=[tmp])
    k.G(lambda: nc.gpsimd.tensor_copy(out=tmp[0:n, T - 1:T], in_=raw[0:n, T - 2:T - 1]), r=[raw], w=[tmp])
    k.V(lambda: nc.vector.scalar_tensor_tensor(out=tmp[0:n, :], in0=tmp[0:n, :], scalar=0.5, in1=raw[0:n, :], op0=ALU.mult, op1=ALU.subtract),
        r=[tmp, raw], w=[tmp])
    k.V(lambda: nc.vector.scalar_tensor_tensor(out=dst[0:n, :], in0=tmp[0:n, :], scalar=mucol, in1=raw[0:n, :], op0=ALU.mult, op1=ALU.add),
        r=[tmp, raw] + rdeps, w=[dst])


def rwkv_lora(k, C, PT, dW):
    nc = k.nc
    scr = C.rw_scr
    with ExitStack() as es:
        phase_consts(k, C, es)
        P = C.pcol
        psb = [k.psum(es, [128, 512], name="psb") for _ in range(4)]
        C.psrot = mkpool(psb)
        w2 = [k.sbuf(es, [64, 1024], name="rl_w2") for _ in range(3)]
        g2a = k.sbuf(es, [128, 1024], name="rl_g2a")
        g2b = k.sbuf(es, [32, 1024], name="rl_g2b")
        k.dma(w2[0][:, :], dW["rwkv_w2"][0], r=[dW["rwkv_w2_tb"]], w=[w2[0]])
        k.dma(w2[1][:, :], dW["rwkv_w2"][1], r=[dW["rwkv_w2_tb"]], w=[w2[1]])
        k.dma(w2[2][:, :], dW["rwkv_a2"], r=[dW["rwkv_a2_tb"]], w=[w2[2]])
        k.dma(g2a[:, :], dW["rwkv_g2"][0:128, :], r=[dW["rwkv_g2_tb"]], w=[g2a])
        k.dma(g2b[:, :], dW["rwkv_g2"][128:160, :], r=[dW["rwkv_g2_tb"]], w=[g2b])
        raw = k.sbuf(es, [128, T], name="rl_raw")
        tmp = k.sbuf(es, [128, T], name="rl_tmp")
        lo = [k.sbuf(es, [64, T], name="rl_lo") for _ in range(3)]
        sg0 = k.sbuf(es, [128, T], name="rl_sg0")
        sg1 = k.sbuf(es, [32, T], name="rl_sg1")
        ob = [k.sbuf(es, [128, T], name="rl_ob") for _ in range(2)]
        for i in range(3):
            load_rows(k, raw, 0, PT, O_RWKV + 3072 + 64 * i, 64)
            tshift(k, raw, tmp, lo[i], 64, P("rwkv_mu_lo", 64)[:, i:i + 1], [C.prm])
            if i < 2:
                k.A(lambda: nc.scalar.activation(out=lo[i][:, :], in_=lo[i][:, :], func=AF.Tanh), r=[lo[i]], w=[lo[i]])
        load_rows(k, raw, 0, PT, O_RWKV + 3264, 128)
        tshift(k, raw, tmp, sg0, 128, P("rwkv_mu_g", 128)[:, 0:1], [C.prm])
        k.A(lambda: nc.scalar.activation(out=sg0[:, :], in_=sg0[:, :], func=AF.Sigmoid), r=[sg0], w=[sg0])
        load_rows(k, raw, 0, PT, O_RWKV + 3264 + 128, 32)
        tshift(k, raw, tmp, sg1, 32, P("rwkv_mu_g", 32)[:, 1:2], [C.prm])
        k.A(lambda: nc.scalar.activation(out=sg1[:, :], in_=sg1[:, :], func=AF.Sigmoid), r=[sg1], w=[sg1])
        cnt = 0
        for cc in range(8):
            ccs = slice(cc * 128, (cc + 1) * 128)
            for arr in range(4):
                o = ob[cnt % 2]
                cnt += 1
                for tg in range(T // 512):
                    ts = slice(tg * 512, (tg + 1) * 512)
                    ps = C.psrot.get()
                    if arr < 3:
                        k.P(lambda: nc.tensor.matmul(ps[:, :], lhsT=w2[arr][:, ccs], rhs=lo[arr][:, ts], start=True, stop=True), r=[w2[arr], lo[arr]], w=[ps])
                    else:
                        k.P(lambda: nc.tensor.matmul(ps[:, :], lhsT=g2a[:, ccs], rhs=sg0[:, ts], start=True, stop=False), r=[g2a, sg0], w=[ps])
                        k.P(lambda: nc.tensor.matmul(ps[:, :], lhsT=g2b[:, ccs], rhs=sg1[:, ts], start=False, stop=True), r=[g2b, sg1], w=[ps])
                    if arr < 2:
                        bcol = P("rwkv_w0", 128)[:, arr * 8 + cc: arr * 8 + cc + 1]
                        k.A(lambda: nc.scalar.activation(out=o[:, ts], in_=ps[:, :], func=AF.Sigmoid, bias=bcol), r=[ps, C.prm], w=[o])
                    elif arr == 2:
                        bcol = P("rwkv_a0", 128)[:, cc:cc + 1]
                        k.A(lambda: nc.scalar.activation(out=o[:, ts], in_=ps[:, :], func=AF.Sigmoid, bias=bcol), r=[ps, C.prm], w=[o])
                    else:
                        k.A(lambda: nc.scalar.copy(out=o[:, ts], in_=ps[:, :]), r=[ps], w=[o])
                if arr < 2:
                    k.V(lambda: nc.vector.tensor_scalar(out=o[:, :], in0=o[:, :], scalar1=-math.exp(-0.5), scalar2=None, op0=ALU.mult), r=[o], w=[o])
                k.dma(scr[arr, ccs, :], o[:, :], r=[o], w=[scr])
    k.barrier()


def mixer_rwkv(k, C, PT, YT, yrow0, dW):
    nc = k.nc
    NU = 8
    rwkv_lora(k, C, PT, dW)
    scr = C.rw_scr
    with ExitStack() as es:
        phase_consts(k, C, es, masks=True)
        P = C.pcol
        psb = [k.psum(es, [128, 512], name="psb") for _ in range(2)]
        C.psrot = mkpool(psb)
        pqb = [k.psum(es, [128, 512], name="psq") for _ in range(6)]
        pq = mkpool([TBQ(b_, c0) for c0 in (0, 128, 256, 384) for b_ in pqb])
        H = 64
        rT, k2T, vT, kkT, bT, yT = [k.sbuf(es, [H, T], name="rw_p%d" % i) for i in range(6)]
        Tt = [k.sbuf(es, [H, T], name="rw_t%d" % i) for i in range(6)]
        reset = k.sbuf(es, [H, T], name="rw_reset")
        k.dma(reset[:, :], C.d_rows[0:H, 3, :], r=[C.d_rows], w=[reset])
        vtok = k.sbuf(es, [128, NB, H], name="rw_vtok")
        kbh = k.sbuf(es, [128, NB, H], name="rw_kbh")
        kkh = k.sbuf(es, [128, NB, H], name="rw_kkh")
        etot = k.sbuf(es, [H, NB], name="rw_etot")
        aak = k.sbuf(es, [128, NU, 128], name="rw_aak")
        arb = k.sbuf(es, [128, NU, 128], name="rw_arb")
        ark = k.sbuf(es, [128, NU, 128], name="rw_ark")
        rhs_sb = [k.sbuf(es, [128, H], name="rw_rhs") for _ in range(2)]
        u_sb = [k.sbuf(es, [128, H], name="rw_u") for _ in range(2)]
        Ss = [k.sbuf(es, [H, H], name="rw_S") for _ in range(2)]
        ob = k.sbuf(es, [H, T], BF16, name="rw_ob")
        units = alloc_units(k, es, NU)
        ones64 = C.sq[0:H, 1, 0:H]
        for h in range(16):
            hc = slice(h, h + 1)
            raw, tmp = Tt[0], Tt[1]
            load_rows(k, raw, 0, PT, O_RWKV + h * H, H)
            tshift(k, raw, tmp, rT, H, P("rwkv_mu_r", H)[:, hc], [C.prm])
            load_rows(k, raw, 0, PT, O_RWKV + 2048 + h * H, H)
            tshift(k, raw, tmp, vT, H, P("rwkv_mu_v", H)[:, hc], [C.prm])
            load_rows(k, raw, 0, PT, O_RWKV + 1024 + h * H, H)
            kT = Tt[2]
            tshift(k, raw, tmp, kT, H, P("rwkv_mu_k", H)[:, hc], [C.prm])
            aT = Tt[3]
            k.dma(aT[:, :], scr[2, h * H:(h + 1) * H, :], r=[scr], w=[aT])
            k.V(lambda: nc.vector.tensor_scalar(out=kkT[:, :], in0=kT[:, :], scalar1=P("rwkv_k_k", H)[:, hc], scalar2=None, op0=ALU.mult),
                r=[kT, C.prm], w=[kkT])
            for tg in range(T // 512):
                ts = slice(tg * 512, (tg + 1) * 512)
                ps = C.psrot.get()
                k.G(lambda: nc.gpsimd.tensor_tensor(out=raw[:, ts], in0=kkT[:, ts], in1=kkT[:, ts], op=ALU.mult), r=[kkT], w=[raw])
                k.P(lambda: nc.tensor.matmul(ps[0:H, :], lhsT=ones64, rhs=raw[:, ts], start=True, stop=True), r=[raw, C.sq], w=[ps])
                k.A(lambda: nc.scalar.activation(out=tmp[:, ts], in_=ps[0:H, :], func=AF.Sqrt, bias=1e-6), r=[ps], w=[tmp])
            k.V(lambda: nc.vector.reciprocal(out=tmp[:, :], in_=tmp[:, :]), r=[tmp], w=[tmp])
            k.V(lambda: nc.vector.tensor_tensor(out=kkT[:, :], in0=kkT[:, :], in1=tmp[:, :], op=ALU.mult), r=[kkT, tmp], w=[kkT])
            k.V(lambda: nc.vector.tensor_scalar(out=tmp[:, :], in0=aT[:, :], scalar1=-1.0, scalar2=P("rwkv_k_a", H)[:, hc], op0=ALU.add, op1=ALU.mult),
                r=[aT, C.prm], w=[tmp])
            k.V(lambda: nc.vector.scalar_tensor_tensor(out=k2T[:, :], in0=tmp[:, :], scalar=1.0, in1=kT[:, :], op0=ALU.add, op1=ALU.mult),
                r=[tmp, kT], w=[k2T])
            k.G(lambda: nc.gpsimd.tensor_tensor(out=bT[:, :], in0=kkT[:, :], in1=aT[:, :], op=ALU.mult), r=[kkT, aT], w=[bT])
            transpose_blocks(k, C, vT, H, vtok, 0)
            for d in range(2):
                lw, G, pref, e4, e5, e6 = Tt
                k.dma(lw[:, :], scr[d, h * H:(h + 1) * H, :], r=[scr], w=[lw])
                tot = chunk_cumsum(k, C, lw, reset, pref, G, H, d == 1)
                k.A(lambda: nc.scalar.activation(out=e4[:, :], in_=G[:, :], func=AF.Exp), r=[G], w=[e4])
                k.V(lambda: nc.vector.tensor_tensor(out=e4[:, :], in0=e4[:, :], in1=rT[:, :], op=ALU.mult), r=[e4, rT], w=[e4])
                k.G(lambda: nc.gpsimd.tensor_tensor(out=e5[:, :], in0=G[:, :], in1=lw[:, :], op=ALU.subtract), r=[G, lw], w=[e5])
                k.A(lambda: nc.scalar.activation(out=e5[:, :], in_=e5[:, :], func=AF.Exp), r=[e5], w=[e5])
                k.V(lambda: nc.vector.tensor_tensor(out=e5[:, :], in0=e5[:, :], in1=kkT[:, :], op=ALU.mult), r=[e5, kkT], w=[e5])
                k.A(lambda: nc.scalar.activation(out=lw[:, :], in_=G[:, :], func=AF.Exp, scale=-1.0), r=[G], w=[lw])
                k.V(lambda: nc.vector.tensor_tensor(out=e6[:, :], in0=lw[:, :], in1=bT[:, :], op=ALU.mult), r=[lw, bT], w=[e6])
                k.G(lambda: nc.gpsimd.tensor_tensor(out=lw[:, :], in0=lw[:, :], in1=k2T[:, :], op=ALU.mult), r=[lw, k2T], w=[lw])
                QR, QA, KB, KK2 = e4, e5, e6, lw
                g3 = G[:, :].rearrange("p (c j) -> p c j", j=128)
                k.V(lambda: nc.vector.tensor_tensor(out=g3, in0=tot.to_broadcast([H, NB, 128]), in1=g3, op=ALU.subtract), r=[G, pref], w=[G])
                k.A(lambda: nc.scalar.activation(out=G[:, :], in_=G[:, :], func=AF.Exp), r=[G], w=[G])
                k.A(lambda: nc.scalar.activation(out=etot[:, :], in_=tot.rearrange("p c o -> p (c o)"), func=AF.Exp), r=[pref], w=[etot])
                k.V(lambda: nc.vector.scalar_tensor_tensor(out=pref[:, :], in0=G[:, :], scalar=-1.0, in1=bT[:, :], op0=ALU.mult, op1=ALU.mult),
                    r=[G, bT], w=[pref])
                k.G(lambda: nc.gpsimd.tensor_tensor(out=G[:, :], in0=G[:, :], in1=k2T[:, :], op=ALU.mult), r=[G, k2T], w=[G])
                transpose_blocks(k, C, pref, H, kbh, 0)
                transpose_blocks(k, C, G, H, kkh, 0)
                m_st, m_in, m_ts = (0, 1, 2) if d == 0 else (2, 3, 0)
                S = Ss[0]
                k.G(lambda: nc.gpsimd.memset(S[:, :], 0.0), w=[S])
                nstep = 0
                batches = range(0, NB, NU) if d == 0 else range(NB - NU, -1, -NU)
                for c0 in batches:
                    for ui in range(NU):
                        c = c0 + ui
                        u = units[ui]
                        cs = slice(c * 128, (c + 1) * 128)
                        p1 = pq.get()
                        k.P(lambda: nc.tensor.matmul(p1[:, :], lhsT=KB[:, cs], rhs=QA[:, cs], start=True, stop=True), r=[KB, QA], w=[p1])
                        k.V(lambda: nc.vector.tensor_tensor(out=u["LT"][:, :], in0=p1[:, :], in1=C.masks[:, m_st, :], op=ALU.mult), r=[p1, C.masks], w=[u["LT"]])
                        p2 = pq.get()
                        k.P(lambda: nc.tensor.matmul(p2[:, :], lhsT=QA[:, cs], rhs=KB[:, cs], start=True, stop=True), r=[KB, QA], w=[p2])
                        k.V(lambda: nc.vector.tensor_tensor(out=u["LL"][:, :], in0=p2[:, :], in1=C.masks[:, m_ts, :], op=ALU.mult), r=[p2, C.masks], w=[u["LL"]])
                        p3 = pq.get()
                        k.P(lambda: nc.tensor.matmul(p3[:, :], lhsT=KK2[:, cs], rhs=QA[:, cs], start=True, stop=True), r=[KK2, QA], w=[p3])
                        k.V(lambda: nc.vector.tensor_tensor(out=aak[:, ui, :], in0=p3[:, :], in1=C.masks[:, m_st, :], op=ALU.mult), r=[p3, C.masks], w=[aak])
                        p4 = pq.get()
                        k.P(lambda: nc.tensor.matmul(p4[:, :], lhsT=KB[:, cs], rhs=QR[:, cs], start=True, stop=True), r=[KB, QR], w=[p4])
                        k.V(lambda: nc.vector.scalar_tensor_tensor(out=arb[:, ui, :], in0=p4[:, :], scalar=-1.0, in1=C.masks[:, m_in, :],
                                                                   op0=ALU.mult, op1=ALU.mult), r=[p4, C.masks], w=[arb])
                        p5 = pq.get()
                        k.P(lambda: nc.tensor.matmul(p5[:, :], lhsT=KK2[:, cs], rhs=QR[:, cs], start=True, stop=True), r=[KK2, QR], w=[p5])
                        k.V(lambda: nc.vector.tensor_tensor(out=ark[:, ui, :], in0=p5[:, :], in1=C.masks[:, m_in, :], op=ALU.mult), r=[p5, C.masks], w=[ark])
                    tri_inverse_batch(k, C, units, pq)
                    uis = range(NU) if d == 0 else range(NU - 1, -1, -1)
                    if os.environ.get("K_SKIP_SEQ"):
                        uis = []
                    for ui in uis:
                        c = c0 + ui
                        u = units[ui]
                        cs = slice(c * 128, (c + 1) * 128)
                        Sn = Ss[(nstep + 1) % 2]
                        rh = rhs_sb[nstep % 2]
                        us = u_sb[nstep % 2]
                        nstep += 1
                        p1 = pq.get()
                        k.P(lambda: nc.tensor.matmul(p1[:, 0:H], lhsT=QA[:, cs], rhs=S[:, :], start=True, stop=False), r=[QA, S], w=[p1])
                        k.P(lambda: nc.tensor.matmul(p1[:, 0:H], lhsT=aak[:, ui, :], rhs=vtok[:, c, :], start=False, stop=True), r=[aak, vtok], w=[p1])
                        k.A(lambda: nc.scalar.copy(out=rh[:, :], in_=p1[:, 0:H]), r=[p1], w=[rh])
                        p2 = pq.get()
                        k.P(lambda: nc.tensor.matmul(p2[:, 0:H], lhsT=u["W"][:, :].bitcast(F32), rhs=rh[:, :], start=True, stop=True), r=[u["W"], rh], w=[p2])
                        k.A(lambda: nc.scalar.copy(out=us[:, :], in_=p2[:, 0:H]), r=[p2], w=[us])
                        p4 = pq.get()
                        k.P(lambda: nc.tensor.matmul(p4[0:H, 0:H], lhsT=kbh[:, c, :], rhs=us[:, :], start=True, stop=False), r=[kbh, us], w=[p4])
                        k.P(lambda: nc.tensor.matmul(p4[0:H, 0:H], lhsT=kkh[:, c, :], rhs=vtok[:, c, :], start=False, stop=True), r=[kkh, vtok], w=[p4])
                        p3 = pq.get()
                        k.P(lambda: nc.tensor.matmul(p3[0:H, :], lhsT=S[:, :], rhs=QR[:, cs], start=True, stop=False), r=[S, QR], w=[p3])
                        k.P(lambda: nc.tensor.matmul(p3[0:H, :], lhsT=us[:, :], rhs=arb[:, ui, :], start=False, stop=False), r=[us, arb], w=[p3])
                        k.P(lambda: nc.tensor.matmul(p3[0:H, :], lhsT=vtok[:, c, :], rhs=ark[:, ui, :], start=False, stop=True), r=[vtok, ark], w=[p3])
                        k.V(lambda: nc.vector.scalar_tensor_tensor(out=Sn[:, :], in0=S[:, :], scalar=etot[:, c:c + 1], in1=p4[0:H, 0:H],
                                                                   op0=ALU.mult, op1=ALU.add), r=[S, etot, p4], w=[Sn])
                        if d == 0:
                            k.A(lambda: nc.scalar.copy(out=yT[:, cs], in_=p3[0:H, :]), r=[p3], w=[yT])
                        else:
                            k.V(lambda: nc.vector.tensor_tensor(out=yT[:, cs], in0=yT[:, cs], in1=p3[0:H, :], op=ALU.add), r=[yT, p3], w=[yT])
                        S = Sn
            t0, t1, t2, t3 = Tt[0], Tt[1], Tt[2], Tt[3]
            for tg in range(T // 512):
                ts = slice(tg * 512, (tg + 1) * 512)
                ps = C.psrot.get()
                k.P(lambda: nc.tensor.matmul(ps[0:H, :], lhsT=ones64, rhs=yT[:, ts], start=True, stop=True), r=[yT, C.sq], w=[ps])
                k.V(lambda: nc.vector.scalar_tensor_tensor(out=t0[:, ts], in0=ps[0:H, :], scalar=-1.0 / H, in1=yT[:, ts], op0=ALU.mult, op1=ALU.add),
                    r=[ps, yT], w=[t0])
                k.G(lambda: nc.gpsimd.tensor_tensor(out=t1[:, ts], in0=t0[:, ts], in1=t0[:, ts], op=ALU.mult), r=[t0], w=[t1])
                ps2 = C.psrot.get()
                k.P(lambda: nc.tensor.matmul(ps2[0:H, :], lhsT=ones64, rhs=t1[:, ts], start=True, stop=True), r=[t1, C.sq], w=[ps2])
                k.A(lambda: nc.scalar.activation(out=t2[:, ts], in_=ps2[0:H, :], func=AF.Sqrt, scale=1.0 / H, bias=64e-5), r=[ps2], w=[t2])
            k.V(lambda: nc.vector.reciprocal(out=t2[:, :], in_=t2[:, :]), r=[t2], w=[t2])
            k.V(lambda: nc.vector.scalar_tensor_tensor(out=t0[:, :], in0=t0[:, :], scalar=P("rwkv_ln_w", H)[:, hc], in1=t2[:, :], op0=ALU.mult, op1=ALU.mult),
                r=[t0, t2, C.prm], w=[t0])
            k.V(lambda: nc.vector.scalar_tensor_tensor(out=t1[:, :], in0=rT[:, :], scalar=P("rwkv_r_k", H)[:, hc], in1=k2T[:, :], op0=ALU.mult, op1=ALU.mult),
                r=[rT, k2T, C.prm], w=[t1])
            for tg in range(T // 512):
                ts = slice(tg * 512, (tg + 1) * 512)
                ps = C.psrot.get()
                k.P(lambda: nc.tensor.matmul(ps[0:H, :], lhsT=ones64, rhs=t1[:, ts], start=True, stop=True), r=[t1, C.sq], w=[ps])
                k.V(lambda: nc.vector.tensor_tensor(out=t2[:, ts], in0=ps[0:H, :], in1=vT[:, ts], op=ALU.mult), r=[ps, vT], w=[t2])
            k.V(lambda: nc.vector.scalar_tensor_tensor(out=t0[:, :], in0=t0[:, :], scalar=P("rwkv_ln_b", H)[:, hc], in1=t2[:, :], op0=ALU.add, op1=ALU.add),
                r=[t0, t2, C.prm], w=[t0])
            k.dma(t3[:, :], scr[3, h * H:(h + 1) * H, :], r=[scr], w=[t3])
            k.V(lambda: nc.vector.tensor_tensor(out=ob[:, :], in0=t0[:, :], in1=t3[:, :], op=ALU.mult), r=[t0, t3], w=[ob])
            k.dma(YT[yrow0 + h * H: yrow0 + (h + 1) * H, :], ob[:, :], r=[ob], w=[YT])
        k.barrier()


def norm_transpose(k, C, es, X, tc0, ntc, gname, hT, psrot):
    nc = k.nc
    xt = [k.sbuf(es, [128, D], name="nt_x") for _ in range(2)]
    junk = k.sbuf(es, [128, D], BF16, name="nt_junk")
    st = [k.sbuf(es, [128, 4], name="nt_st") for _ in range(2)]
    g = C.pcol(gname)
    for j in range(ntc):
        tc = tc0 + j
        x = xt[j % 2]
        s_ = st[j % 2]
        k.dma(x[:, :], X[tc * 128:(tc + 1) * 128, :], r=[X], w=[x])
        k.V(lambda: nc.vector.memset(s_[:, :], 0.0), w=[s_])
        k.A(lambda: nc.scalar.activation(out=junk[:, :], in_=x[:, :], func=AF.Square, accum_out=s_[:, 0:1]), r=[x, s_], w=[junk, s_])
        k.A(lambda: nc.scalar.activation(out=s_[:, 1:2], in_=s_[:, 0:1], func=AF.Sqrt, scale=1.0 / D, bias=1e-6), r=[s_], w=[s_])
        k.V(lambda: nc.vector.reciprocal(out=s_[:, 2:3], in_=s_[:, 1:2]), r=[s_], w=[s_])
        k.G(lambda: nc.gpsimd.tensor_scalar(out=x[:, :], in0=x[:, :], scalar1=s_[:, 2:3], scalar2=None, op0=ALU.mult), r=[x, s_], w=[x])
        for dc0 in range(0, 32, 4):
            ps = psrot.get()
            for q in range(4):
                dc = dc0 + q
                k.P(lambda: nc.tensor.transpose(ps[:, q * 128:(q + 1) * 128], x[:, dc * 128:(dc + 1) * 128], C.ident), r=[x, C.sq], w=[ps])
            k.V(lambda: nc.vector.tensor_tensor(out=hT[:, dc0:dc0 + 4, j * 128:(j + 1) * 128],
                                                in0=ps[:, :].rearrange("p (q c) -> p q c", q=4),
                                                in1=g[:, dc0:dc0 + 4].unsqueeze(2).to_broadcast([128, 4, 128]), op=ALU.mult),
                r=[ps, C.prm], w=[hT])


def cast_alt(k, n, out_ap, in_ap, r, w):
    nc = k.nc
    if n % 2 == 0:
        k.G(lambda: nc.gpsimd.tensor_copy(out=out_ap, in_=in_ap), r=r, w=w)
    else:
        k.A(lambda: nc.scalar.copy(out=out_ap, in_=in_ap), r=r, w=w)


def phase_inproj(k, C, X, gname, Win, PT):
    nc = k.nc
    with ExitStack() as es:
        phase_consts(k, C, es)
        ps = [k.psum(es, [128, 512], name="psb") for _ in range(8)]
        psrot = mkpool(ps)
        hT = k.sbuf(es, [128, 32, T], BF16, name="in_hT")
        with ExitStack() as es2:
            norm_transpose(k, C, es2, X, 0, NB, gname, hT, psrot)
            k.barrier()
        stg = [k.sbuf(es, [128, 32, 128], name="in_stg") for _ in range(2)]
        wb = [k.sbuf(es, [128, 32, 128], BF16, name="in_wb") for _ in range(2)]
        ost = [k.sbuf(es, [128, T], name="in_ost") for _ in range(2)]
        ncc = (N_IN + 127) // 128
        Wv = Win.t.rearrange("(kc p) c -> p kc c", p=128)

        def load(cc):
            cw = min(128, N_IN - cc * 128)
            sg = stg[cc % 2]
            for hf in range(2):
                k.dma(sg[:, hf * 16:(hf + 1) * 16, 0:cw], Wv[:, hf * 16:(hf + 1) * 16, cc * 128:cc * 128 + cw], r=[Win], w=[sg])

        load(0)
        for cc in range(ncc):
            cw = min(128, N_IN - cc * 128)
            sg = stg[cc % 2]
            w_ = wb[cc % 2]
            o = ost[cc % 2]
            cast_alt(k, cc, w_[:, :, 0:cw], sg[:, :, 0:cw], [sg], [w_])
            if cc + 1 < ncc:
                load(cc + 1)
            for tg in range(T // 512):
                ts = slice(tg * 512, (tg + 1) * 512)
                p = psrot.get()
                for kc in range(32):
                    k.P(lambda: nc.tensor.matmul(p[0:cw, :], lhsT=w_[:, kc, 0:cw], rhs=hT[:, kc, ts], start=(kc == 0), stop=(kc == 31)),
                        r=[w_, hT], w=[p])
                if tg % 2 == 0:
                    k.V(lambda: nc.vector.tensor_copy(out=o[0:cw, ts], in_=p[0:cw, :]), r=[p], w=[o])
                else:
                    k.A(lambda: nc.scalar.copy(out=o[0:cw, ts], in_=p[0:cw, :]), r=[p], w=[o])
            k.dma(PT[cc * 128:cc * 128 + cw, :], o[0:cw, :], r=[o], w=[PT], Q=k.act)
        k.barrier()


def phase_outproj(k, C, YT, Wout, Xin, Xout):
    nc = k.nc
    TH = T // 2
    with ExitStack() as es:
        phase_consts(k, C, es)
        ps = [k.psum(es, [128, 512], name="psb") for _ in range(8)]
        psrot = mkpool(ps)
        yT = k.sbuf(es, [128, 32, TH], BF16, name="op_yT")
        stg = [k.sbuf(es, [128, 4, 512], name="op_stg") for _ in range(3)]
        wb = [k.sbuf(es, [128, 32, 512], BF16, name="op_wb") for _ in range(2)]
        xt = [k.sbuf(es, [128, 512], name="op_x") for _ in range(4)]
        YTv = YT.t.rearrange("(kc p) t -> p kc t", p=128)
        Wv = Wout.t.rearrange("(kc p) c -> p kc c", p=128)
        n = 0
        nx = 0
        for half in range(2):
            for q in range(4):
                k.dma(yT[:, q * 8:(q + 1) * 8, :], YTv[:, q * 8:(q + 1) * 8, half * TH:(half + 1) * TH], r=[YT], w=[yT])
            for cg in range(8):
                cs = slice(cg * 512, (cg + 1) * 512)
                w_ = wb[cg % 2]
                for pc in range(8):
                    sg = stg[n % 3]
                    k.dma(sg[:, :, :], Wv[:, pc * 4:(pc + 1) * 4, cs], r=[Wout], w=[sg])
                    cast_alt(k, n, w_[:, pc * 4:(pc + 1) * 4, :], sg[:, :, :], [sg], [w_])
                    n += 1
                for tcl in range(TH // 128):
                    tc = half * (TH // 128) + tcl
                    x = xt[nx % 4]
                    nx += 1
                    k.dma(x[:, :], Xin[tc * 128:(tc + 1) * 128, cs], r=[Xin], w=[x])
                    p = psrot.get()
                    for kc in range(32):
                        k.P(lambda: nc.tensor.matmul(p[:, :], lhsT=yT[:, kc, tcl * 128:(tcl + 1) * 128], rhs=w_[:, kc, :], start=(kc == 0), stop=(kc == 31)),
                            r=[yT, w_], w=[p])
                    k.V(lambda: nc.vector.tensor_tensor(out=x[:, :], in0=x[:, :], in1=p[:, :], op=ALU.add), r=[x, p], w=[x])
                    k.dma(Xout[tc * 128:(tc + 1) * 128, cs], x[:, :], r=[x], w=[Xout], Q=k.act)
        k.barrier()


def phase_ffn(k, C, Xin, Xout, gname, Wgu, Wdn):
    nc = k.nc
    NF = DFF // 128
    with ExitStack() as es:
        phase_consts(k, C, es)
        ps = [k.psum(es, [128, 512], name="psb") for _ in range(8)]
        psrot = mkpool(ps)
        hT = k.sbuf(es, [128, 32, 512], BF16, name="ff_hT")
        actT = k.sbuf(es, [128, NF, 512], BF16, name="ff_act")
        Wgv = Wgu.t.rearrange("(kc p) c -> p kc c", p=128)
        for qt in range(T // 512):
            with ExitStack() as es2:
                norm_transpose(k, C, es2, Xin, qt * 4, 4, gname, hT, psrot)
                k.barrier()
            with ExitStack() as es3:
                stg = [k.sbuf(es3, [128, 16, 128], name="ff_stg") for _ in range(4)]
                wb = [k.sbuf(es3, [128, 32, 128], BF16, name="ff_wb") for _ in range(4)]
                sgt = [k.sbuf(es3, [128, 512], name="ff_sg") for _ in range(2)]
                nld = [0]

                def load_cast(fc, which, w_):
                    c0 = which * DFF + fc * 128
                    for hf in range(2):
                        sg = stg[nld[0] % 4]
                        k.dma(sg[:, :, :], Wgv[:, hf * 16:(hf + 1) * 16, c0:c0 + 128], r=[Wgu], w=[sg])
                        cast_alt(k, nld[0], w_[:, hf * 16:(hf + 1) * 16, :], sg[:, :, :], [sg], [w_])
                        nld[0] += 1

                def prep(fc):
                    wg = wb[(2 * fc) % 4]
                    wu = wb[(2 * fc + 1) % 4]
                    load_cast(fc, 0, wg)
                    load_cast(fc, 1, wu)

                prep(0)
                for fc in range(NF):
                    wg = wb[(2 * fc) % 4]
                    wu = wb[(2 * fc + 1) % 4]
                    if fc + 1 < NF:
                        prep(fc + 1)
                    pg = psrot.get()
                    for kc in range(32):
                        k.P(lambda: nc.tensor.matmul(pg[:, :], lhsT=wg[:, kc, :], rhs=hT[:, kc, :], start=(kc == 0), stop=(kc == 31)), r=[wg, hT], w=[pg])
                    pu = psrot.get()
                    for kc in range(32):
                        k.P(lambda: nc.tensor.matmul(pu[:, :], lhsT=wu[:, kc, :], rhs=hT[:, kc, :], start=(kc == 0), stop=(kc == 31)), r=[wu, hT], w=[pu])
                    sg_ = sgt[fc % 2]
                    k.A(lambda: nc.scalar.activation(out=sg_[:, :], in_=pg[:, :], func=AF.Silu), r=[pg], w=[sg_])
                    k.V(lambda: nc.vector.tensor_tensor(out=actT[:, fc, :], in0=sg_[:, :], in1=pu[:, :], op=ALU.mult), r=[sg_, pu], w=[actT])
                k.barrier()
            with ExitStack() as es4:
                stg2 = [k.sbuf(es4, [128, 512], name="ff_stg2") for _ in range(6)]
                wd = [k.sbuf(es4, [128, 512], BF16, name="ff_wd") for _ in range(6)]
                xt = [k.sbuf(es4, [128, 512], name="ff_x") for _ in range(4)]
                nd = [0]

                def loadd(i):
                    cg, fc = divmod(i, NF)
                    sg = stg2[i % 6]
                    k.dma(sg[:, :], Wdn.t[fc * 128:(fc + 1) * 128, cg * 512:(cg + 1) * 512], r=[Wdn], w=[sg])

                tot = 8 * NF
                for i in range(3):
                    loadd(i)
                nx = 0
                for cg in range(8):
                    cs = slice(cg * 512, (cg + 1) * 512)
                    pb = [psrot.get() for _ in range(4)]
                    for fc in range(NF):
                        i = cg * NF + fc
                        w_ = wd[i % 6]
                        cast_alt(k, i, w_[:, :], stg2[i % 6][:, :], [stg2[i % 6]], [w_])
                        if i + 3 < tot:
                            loadd(i + 3)
                        for tcl in range(4):
                            k.P(lambda: nc.tensor.matmul(pb[tcl][:, :], lhsT=actT[:, fc, tcl * 128:(tcl + 1) * 128], rhs=w_[:, :],
                                                         start=(fc == 0), stop=(fc == NF - 1)), r=[actT, w_], w=[pb[tcl]])
                    for tcl in range(4):
                        tc = qt * 4 + tcl
                        x = xt[nx % 4]
                        nx += 1
                        k.dma(x[:, :], Xin[tc * 128:(tc + 1) * 128, cs], r=[Xin], w=[x])
                        k.V(lambda: nc.vector.tensor_tensor(out=x[:, :], in0=x[:, :], in1=pb[tcl][:, :], op=ALU.add), r=[x, pb[tcl]], w=[x])
                        k.dma(Xout[tc * 128:(tc + 1) * 128, cs], x[:, :], r=[x], w=[Xout], Q=k.act)
                k.barrier()


def phase_final_norm(k, C, Xin, OUT, gname):
    nc = k.nc
    with ExitStack() as es:
        phase_consts(k, C, es)
        xt = [k.sbuf(es, [128, D], name="fn_x") for _ in range(2)]
        junk = k.sbuf(es, [128, D], BF16, name="fn_junk")
        st = [k.sbuf(es, [128, 4], name="fn_st") for _ in range(2)]
        gb = k.sbuf(es, [128, D], name="fn_g")
        k.dma(gb[:, :], C.final_g_dram.t.partition_broadcast(128), r=[C.final_g_dram], w=[gb])
        for tc in range(NB):
            x = xt[tc % 2]
            s_ = st[tc % 2]
            k.dma(x[:, :], Xin[tc * 128:(tc + 1) * 128, :], r=[Xin], w=[x])
            k.V(lambda: nc.vector.memset(s_[:, :], 0.0), w=[s_])
            k.A(lambda: nc.scalar.activation(out=junk[:, :], in_=x[:, :], func=AF.Square, accum_out=s_[:, 0:1]), r=[x, s_], w=[junk, s_])
            k.A(lambda: nc.scalar.activation(out=s_[:, 1:2], in_=s_[:, 0:1], func=AF.Sqrt, scale=1.0 / D, bias=1e-6), r=[s_], w=[s_])
            k.V(lambda: nc.vector.reciprocal(out=s_[:, 2:3], in_=s_[:, 1:2]), r=[s_], w=[s_])
            k.V(lambda: nc.vector.scalar_tensor_tensor(out=x[:, :], in0=x[:, :], scalar=s_[:, 2:3], in1=gb[:, :], op0=ALU.mult, op1=ALU.mult),
                r=[x, s_, gb], w=[x])
            k.dma(OUT[tc * 128:(tc + 1) * 128, :], x[:, :], r=[x], w=[OUT], Q=k.act)
        k.barrier()


def build_program(prm_off, layers=(0, 1), do_final=True):
    nc = bass.Bass("TRN2", target_bir_lowering=False)
    with ExitStack() as es:
        k = KB(nc, es)
        dr = {}
        for n_, shp in (("c_masks", [128, NMASK, 128]), ("c_sq", [128, 4, 128]), ("c_rows", [128, 4, T]), ("c_poscol", [128, NB]),
                        ("c_wm", [128, 2, 896])):
            dr[n_] = k.dram(n_, shp, F32, kind="ExternalInput")
        nprm = sum(v[2] for v in prm_off.values())
        prm = [k.dram("prm%d" % l, [128, nprm], F32, kind="ExternalInput") for l in range(DEPTH)]
        X = k.dram("x", [T, D], F32, kind="ExternalInput")
        OUT = k.dram("out", [T, D], F32, kind="ExternalOutput")
        w_in = k.dram("w_in", [DEPTH, D, N_IN], F32, kind="ExternalInput")
        w_out = k.dram("w_out", [DEPTH, D, D], F32, kind="ExternalInput")
        w_gu = k.dram("w_gate_up", [DEPTH, D, 2 * DFF], F32, kind="ExternalInput")
        w_dn = k.dram("w_down", [DEPTH, DFF, D], F32, kind="ExternalInput")
        rw2 = k.dram("rwkv_w2", [DEPTH, 2, 64, 1024], F32, kind="ExternalInput")
        ra2 = k.dram("rwkv_a2", [DEPTH, 64, 1024], F32, kind="ExternalInput")
        rg2 = k.dram("rwkv_g2", [DEPTH, 160, 1024], F32, kind="ExternalInput")
        fg = k.dram("final_norm_g", [D], F32, kind="ExternalInput")
        PT = k.dram("PT", [N_IN, T], F32)
        YT = k.dram("YT", [D, T], BF16)
        XA = k.dram("XA", [T, D], F32)
        XB = k.dram("XB", [T, D], F32)
        C = setup_ctx(k, es, nc, dr, prm_off)
        C.final_g_dram = fg
        C.ssd_scr = k.dram("ssd_scr", [4, 16, T], F32)
        C.gdn_scr = k.dram("gdn_scr", [5, 8, T], F32)
        C.gdn_scr2 = k.dram("gdn_scr2", [2, 8, NB], F32)
        C.rw_scr = k.dram("rw_scr", [4, 1024, T], F32)

        def sub(tb, l):
            v = TB(tb.t[l])
            v.b = tb.b
            return v

        xcur = X
        for l in layers:
            C.prm_dram = prm[l]
            phase_inproj(k, C, xcur, "attn_norm_g", sub(w_in, l), PT)
            dW = {"rwkv_w2": rw2.t[l], "rwkv_w2_tb": rw2, "rwkv_a2": ra2.t[l], "rwkv_a2_tb": ra2, "rwkv_g2": rg2.t[l], "rwkv_g2_tb": rg2}
            if not os.environ.get("K_SKIP_MIX"):
                mixer_rwkv(k, C, PT, YT, 0, dW)
                mixer_gdn(k, C, PT, YT, 1024)
                mixer_ret(k, C, PT, YT, 2048)
                mixer_ssd(k, C, PT, YT, 3072)
            phase_outproj(k, C, YT, sub(w_out, l), xcur, XB)
            phase_ffn(k, C, XB, XA, "ffn_norm_g", sub(w_gu, l), sub(w_dn, l))
            xcur = XA
        if do_final:
            phase_final_norm(k, C, xcur, OUT, "final_norm_g")
        k.finish()
    return nc


_CACHE = {}


def kernel(**inputs):
    inp = {n: np.asarray(v) for n, v in inputs.items()}
    ents = [pack_layer_params(inp, l) for l in range(DEPTH)]
    prms = []
    prm_off = None
    for e in ents:
        arr, prm_off = layout_params(e)
        prms.append(arr)
    if "nc" not in _CACHE:
        _CACHE["nc"] = build_program(prm_off)
    nc = _CACHE["nc"]
    cs = make_consts()
    cs["c_wm"] = make_widemask()
    shared = dict(cs)
    for l in range(DEPTH):
        shared["prm%d" % l] = prms[l]
    for n_ in ("w_in", "w_out", "w_gate_up", "w_down", "rwkv_w2", "rwkv_a2", "rwkv_g2", "final_norm_g"):
        shared[n_] = np.ascontiguousarray(inp[n_], dtype=np.float32)
    x = np.ascontiguousarray(inp["x"], dtype=np.float32)
    in_maps = []
    for b in range(8):
        m = dict(shared)
        m["x"] = x[b]
        in_maps.append(m)
    res = run_bass_kernel_spmd(nc, in_maps, core_ids=list(range(8)))
    return np.stack([np.asarray(r["out"]) for r in res.results], axis=0).astype(np.float32)
```

```python
import math
import numpy as np
from contextlib import ExitStack
import concourse.bass as bass
import concourse.mybir as mybir
from concourse.bass_utils import run_bass_kernel_spmd

F32 = mybir.dt.float32
BF16 = mybir.dt.bfloat16
F32R = mybir.dt.float32r


def fr(ap):
    return ap.bitcast(F32R)
AF = mybir.ActivationFunctionType
ALU = mybir.AluOpType
AX = mybir.AxisListType

D = 4096
T = 2048
NB = T // 128
DEPTH = 2
N_RWKV, N_GDN, N_RET, N_SSD = 3424, 3096, 3072, 3104
N_IN = N_RWKV + N_GDN + N_RET + N_SSD
O_RWKV, O_GDN, O_RET, O_SSD = 0, N_RWKV, N_RWKV + N_GDN, N_RWKV + N_GDN + N_RET
DFF = 11008
NEG = -30000.0


class Buf:
    __slots__ = ("w", "r")

    def __init__(s):
        s.w = {}
        s.r = {}


class TB:
    __slots__ = ("t", "b", "ps")

    def __init__(s, t):
        s.t = t
        s.b = Buf()
        s.ps = False

    def __getitem__(s, key):
        return s.t[key]


class TBQ:
    __slots__ = ("t", "b", "c0", "ps")

    def __init__(s, base, c0):
        s.t = base.t
        s.ps = True
        s.c0 = c0
        s.b = base.b

    def __getitem__(s, key):
        rows, cols = key
        a = cols.start or 0
        b = 128 if cols.stop is None else cols.stop
        return s.t[rows, s.c0 + a:s.c0 + b]


class Eng:
    def __init__(s, name, eng, sem):
        s.name = name
        s.eng = eng
        s.sem = sem
        s.count = 0
        s.seen = {}


class KB:
    NDMA = 32

    def __init__(s, nc, es):
        s.nc = nc
        s.es = es

        def mk(name, eng):
            return Eng(name, eng, es.enter_context(nc.semaphore("sem_" + name)))

        s.pe = mk("pe", nc.tensor)
        s.dve = mk("dve", nc.vector)
        s.act = mk("act", nc.scalar)
        s.pool = mk("pool", nc.gpsimd)
        s.sp = mk("sp", nc.sync)
        s.engs = [s.pe, s.dve, s.act, s.pool, s.sp]
        s.dsems = [es.enter_context(nc.semaphore("dsem%d" % i)) for i in range(s.NDMA)]
        s.dcount = 0
        s.dlast = [0] * s.NDMA
        s.nname = 0

    def sbuf(s, es, shape, dt=F32, name=None):
        s.nname += 1
        return TB(es.enter_context(s.nc.sbuf_tensor("%s_%d" % (name or "sb", s.nname), list(shape), dt)))

    def psum(s, es, shape, dt=F32, name=None):
        s.nname += 1
        tb = TB(es.enter_context(s.nc.psum_tensor("%s_%d" % (name or "ps", s.nname), list(shape), dt)))
        tb.ps = True
        return tb

    def dram(s, name, shape, dt=F32, kind="Internal"):
        return TB(s.nc.dram_tensor(name, list(shape), dt, kind=kind).ap())

    def _wait(s, E, sem, val):
        k = id(sem)
        if E.seen.get(k, 0) >= val:
            return
        E.eng.wait_ge(sem, val)
        E.seen[k] = val

    def _deps(s, E, reads, writes):
        need = {}

        def add(d, raw):
            for k, (sem, val) in d.items():
                if sem is E.sem and not raw:
                    continue
                if k not in need or need[k][1] < val:
                    need[k] = (sem, val)

        for tb in reads:
            add(tb.b.w, True)
        for tb in writes:
            add(tb.b.w, False)
            add(tb.b.r, False)
        for k, (sem, val) in need.items():
            s._wait(E, sem, val)

    def _commit(s, tok, reads, writes):
        k = id(tok[0])
        for tb in reads:
            b = tb.b
            if k not in b.r or b.r[k][1] < tok[1]:
                b.r[k] = tok
        for tb in writes:
            tb.b.w = {k: tok}
            tb.b.r = {}

    def op(s, E, fn, r=(), w=()):
        if any(tb.ps for tb in r):
            w = list(w) + [tb for tb in r if tb.ps]
        s._deps(E, r, w)
        ins = fn()
        E.count += 1
        ins.then_inc(E.sem, 1)
        s._commit((E.sem, E.count), r, w)
        return ins

    def P(s, fn, r=(), w=()):
        return s.op(s.pe, fn, r, w)

    def V(s, fn, r=(), w=()):
        return s.op(s.dve, fn, r, w)

    def A(s, fn, r=(), w=()):
        return s.op(s.act, fn, r, w)

    def G(s, fn, r=(), w=()):
        return s.op(s.pool, fn, r, w)

    def dma(s, out, in_, r=(), w=(), Q=None, **kw):
        Q = Q or s.sp
        i = s.dcount % s.NDMA
        s.dcount += 1
        sem = s.dsems[i]
        prev = s.dlast[i]
        s._deps(Q, r, w)
        if prev:
            s._wait(Q, sem, prev)
        ins = Q.eng.dma_start(out=out, in_=in_, **kw)
        ins.then_inc(sem, 16)
        val = prev + 16
        s.dlast[i] = val
        s._commit((sem, val), r, w)
        return ins

    def barrier(s):
        toks = [(E.sem, E.count) for E in s.engs if E.count > 0]
        toks += [(s.dsems[i], s.dlast[i]) for i in range(s.NDMA) if s.dlast[i] > 0]
        for E in s.engs:
            for sem, val in toks:
                if sem is E.sem:
                    continue
                s._wait(E, sem, val)

    def finish(s):
        toks = [(E.sem, E.count) for E in s.engs if E.count > 0 and E is not s.sp]
        toks += [(s.dsems[i], s.dlast[i]) for i in range(s.NDMA) if s.dlast[i] > 0]
        for sem, val in toks:
            s._wait(s.sp, sem, val)


NMASK = 15


def make_consts():
    p = np.arange(128)[:, None]
    f = np.arange(128)[None, :]
    cm = np.zeros((128, NMASK, 128), np.float32)
    cm[:, 0] = (p < f)
    cm[:, 1] = (p <= f)
    cm[:, 2] = (p > f)
    cm[:, 3] = (p >= f)
    for i in range(4):
        cm[:, 4 + i] = np.where(cm[:, i] > 0, 0.0, NEG)
    for k in range(1, 8):
        b = 2 ** (k - 1)
        cm[:, 7 + k] = -(((p // (2 * b)) == (f // (2 * b))) & ((p // b) != (f // b))).astype(np.float32)
    ident = np.eye(128, dtype=np.float32)
    ones = np.ones((128, 128), np.float32)
    blk64 = (p // 64 == f // 64).astype(np.float32)
    perm = np.zeros((128, 128), np.float32)
    for i in range(64):
        perm[2 * i + 1, 2 * i] = -1.0
        perm[2 * i, 2 * i + 1] = 1.0
    sq = np.stack([ident, ones, blk64, perm], axis=1)
    pos = np.arange(T, dtype=np.float32)
    theta = (1.0 / (np.float32(10000.0) ** np.linspace(0.0, 1.0, 32, dtype=np.float32))).astype(np.float32)
    ang = (pos[:, None] * theta[None, :]).astype(np.float32)
    c = np.arange(128)
    idx = (c % 64) // 2
    cos = np.cos(ang).astype(np.float32)[:, idx].T.copy()
    sin = np.sin(ang).astype(np.float32)[:, idx].T.copy()
    posrow = np.broadcast_to(pos[None, :], (128, T)).copy()
    reset = np.broadcast_to(((np.arange(T) % 128) != 0).astype(np.float32)[None, :], (128, T)).copy()
    rows = np.stack([cos, sin, posrow, reset], axis=1)
    poscol = (np.arange(NB)[None, :] * 128 + np.arange(128)[:, None]).astype(np.float32)
    return {"c_masks": cm, "c_sq": sq, "c_rows": rows, "c_poscol": poscol}


def colmajor(v):
    v = np.asarray(v, np.float32).reshape(-1)
    n = (len(v) + 127) // 128 * 128
    vp = np.zeros(n, np.float32)
    vp[:len(v)] = v
    return vp.reshape(-1, 128).T.copy()


def make_widemask():
    p = np.arange(128)[:, None]
    u = np.arange(896)[None, :] - 384
    wm = np.zeros((128, 2, 896), np.float32)
    wm[:, 0] = np.where(u >= p, 0.0, NEG)
    wm[:, 1] = np.where(u < p, 0.0, NEG)
    return wm


class Rot:
    def __init__(s, k, es, n, shape, dt=F32, name="rot"):
        s.tiles = [k.sbuf(es, shape, dt, name) for _ in range(n)]
        s.i = 0

    def get(s):
        t = s.tiles[s.i % len(s.tiles)]
        s.i += 1
        return t


class Ctx:
    pass


def quad_attn(k, C, R, kT, qT, dk, pbase, vtok, voff, dv, gf_bc, eb_bc, colf, coltbs, yT, e0, e1):
    nc = k.nc
    wm = C.wm
    L = 3
    for tg in range(T // 512):
        ts = slice(tg * 512, (tg + 1) * 512)
        yps = C.psacc.get()
        kqs = {}
        Ws = {}

        def stageA(i):
            fwd = i <= 4 * tg + 3
            bwd = i >= 4 * tg
            diag = fwd and bwd
            off = i - 4 * tg
            W = R.wt.get()
            if not diag:
                if fwd:
                    k.A(lambda: nc.scalar.activation(out=W[:, :], in_=gf_bc[:, ts], func=AF.Exp, bias=colf(0, i), scale=1.0),
                        r=[gf_bc] + coltbs, w=[W])
                else:
                    k.A(lambda: nc.scalar.activation(out=W[:, :], in_=eb_bc[:, ts], func=AF.Exp, bias=colf(1, i), scale=-1.0),
                        r=[eb_bc] + coltbs, w=[W])
            else:
                msl = wm[:, 0, 384 - off * 128: 384 - off * 128 + 512]
                msl2 = wm[:, 1, 384 - off * 128: 384 - off * 128 + 512]
                a1 = R.arg.get()
                k.V(lambda: nc.vector.scalar_tensor_tensor(out=a1[:, :], in0=gf_bc[:, ts], scalar=colf(0, i), in1=msl,
                                                           op0=ALU.add, op1=ALU.add), r=[gf_bc, wm] + coltbs, w=[a1])
                ef = R.ex.get()
                k.A(lambda: nc.scalar.activation(out=ef[:, :], in_=a1[:, :], func=AF.Exp), r=[a1], w=[ef])
                a2 = R.arg.get()
                k.V(lambda: nc.vector.scalar_tensor_tensor(out=a2[:, :], in0=eb_bc[:, ts], scalar=colf(1, i), in1=msl2,
                                                           op0=ALU.subtract, op1=ALU.subtract), r=[eb_bc, wm] + coltbs, w=[a2])
                eb = R.ex.get()
                k.A(lambda: nc.scalar.activation(out=eb[:, :], in_=a2[:, :], func=AF.Exp, scale=-1.0), r=[a2], w=[eb])
                k.G(lambda: nc.gpsimd.tensor_tensor(out=W[:, :], in0=ef[:, :], in1=eb[:, :], op=ALU.add), r=[ef, eb], w=[W])
            Ws[i] = W

        def stageB(i):
            kq = C.psrot.get()
            k.P(lambda: nc.tensor.matmul(kq[:, :], lhsT=kT[pbase:pbase + dk, i * 128:(i + 1) * 128], rhs=qT[pbase:pbase + dk, ts],
                                         start=True, stop=True), r=[kT, qT], w=[kq])
            kqs[i] = kq

        def stageC(i):
            sc = R.sc.get()
            kq = kqs.pop(i)
            W = Ws.pop(i)
            k.V(lambda: nc.vector.tensor_tensor(out=sc[:, :], in0=kq[:, :], in1=W[:, :], op=ALU.mult), r=[kq, W], w=[sc])
            k.P(lambda: nc.tensor.matmul(yps[0:dv, :], lhsT=vtok[:, i, voff:voff + dv], rhs=sc[:, :], start=(i == 0), stop=(i == NB - 1)),
                r=[vtok, sc], w=[yps])

        for i in range(min(L, NB)):
            stageA(i)
            stageB(i)
        for i in range(NB):
            if i + L < NB:
                stageA(i + L)
                stageB(i + L)
            stageC(i)
        k.A(lambda: nc.scalar.copy(out=yT[e0:e1, ts], in_=yps[e0:e1, :]), r=[yps], w=[yT])


def transpose_blocks(k, C, src, srows, dst, dcol0, nblk=NB):
    nc = k.nc
    for i in range(0, nblk, 4):
        ps = C.psrot.get()
        for j in range(4):
            k.P(lambda: nc.tensor.transpose(ps[:, j * 128:j * 128 + srows], src[0:srows, (i + j) * 128:(i + j + 1) * 128],
                                            C.ident[0:srows, 0:srows]), r=[src, C.sq], w=[ps])
        k.A(lambda: nc.scalar.copy(out=dst[:, i:i + 4, dcol0:dcol0 + srows],
                                   in_=ps[:, :].rearrange("p (j c) -> p j c", j=4)[:, :, 0:srows]), r=[ps], w=[dst])


def load_rows(k, dst, drows, PT, r0, n):
    k.dma(dst[drows:drows + n, :], PT[r0:r0 + n, :], r=[PT], w=[dst])


def conv_silu(k, C, raw, acc, dst, wcol, bcol):
    nc = k.nc
    if bcol is None:
        k.V(lambda: nc.vector.tensor_scalar(out=acc[:, :], in0=raw[:, :], scalar1=wcol[:, 2:3], scalar2=None, op0=ALU.mult),
            r=[raw, C.prm], w=[acc])
    else:
        k.V(lambda: nc.vector.tensor_scalar(out=acc[:, :], in0=raw[:, :], scalar1=wcol[:, 2:3], scalar2=bcol, op0=ALU.mult, op1=ALU.add),
            r=[raw, C.prm], w=[acc])
    for j in (0, 1, 3, 4):
        sh = j - 2
        lo = max(0, -sh)
        hi = T - max(0, sh)
        k.V(lambda: nc.vector.scalar_tensor_tensor(out=acc[:, lo:hi], in0=raw[:, lo + sh:hi + sh], scalar=wcol[:, j:j + 1], in1=acc[:, lo:hi],
                                                   op0=ALU.mult, op1=ALU.add), r=[raw, C.prm, acc], w=[acc])
    k.A(lambda: nc.scalar.activation(out=dst[:, :], in_=acc[:, :], func=AF.Silu), r=[acc], w=[dst])


def sumsq_bcast(k, C, srcs, nparts, onesap, scale, eps, rstd, tmp):
    nc = k.nc
    M = onesap.shape[-1]
    for tg in range(T // 512):
        ts = slice(tg * 512, (tg + 1) * 512)
        ps = C.psrot.get()
        for n, src in enumerate(srcs):
            sq = C.R.arg.get()
            k.G(lambda: nc.gpsimd.tensor_tensor(out=sq[0:nparts, :], in0=src[0:nparts, ts], in1=src[0:nparts, ts], op=ALU.mult), r=[src], w=[sq])
            k.P(lambda: nc.tensor.matmul(ps[0:M, :], lhsT=onesap, rhs=sq[0:nparts, :], start=(n == 0), stop=(n == len(srcs) - 1)),
                r=[sq, C.sq], w=[ps])
        k.A(lambda: nc.scalar.activation(out=tmp[0:M, ts], in_=ps[0:M, :], func=AF.Sqrt, scale=scale, bias=float(eps)), r=[ps], w=[tmp])
    k.V(lambda: nc.vector.reciprocal(out=rstd[0:M, :], in_=tmp[0:M, :]), r=[tmp], w=[rstd])


def mixer_ssd(k, C, PT, YT, yrow0):
    nc = k.nc
    with ExitStack() as es:
        phase_consts(k, C, es, wm=True)
        P = C.pcol
        quad_pools(k, C, es)
        R = C.R
        colsT = [k.sbuf(es, [128, NB, 16], name="ssd_cols") for _ in range(2)]
        es2 = ExitStack()
        dtt = [k.sbuf(es2, [16, T], name="ssd_dt") for _ in range(6)]
        nA = k.sbuf(es2, [16, 2], name="ssd_nA")
        onesr = k.sbuf(es2, [16, T], name="ssd_ones")
        k.G(lambda: nc.gpsimd.memset(onesr[:, :], 1.0), w=[onesr])
        k.A(lambda: nc.scalar.activation(out=nA[:, :], in_=P("ssd_A_log", 16), func=AF.Exp), r=[C.prm], w=[nA])
        for d in range(2):
            raw = dtt[d]
            load_rows(k, raw, 0, PT, O_SSD + 3072 + 16 * d, 16)
            e = dtt[2]
            k.A(lambda: nc.scalar.activation(out=e[:, :], in_=raw[:, :], func=AF.Exp, bias=P("ssd_dt_bias", 16)[:, d:d + 1]), r=[raw, C.prm], w=[e])
            k.A(lambda: nc.scalar.activation(out=raw[:, :], in_=e[:, :], func=AF.Ln, bias=1.0), r=[e], w=[raw])
            la = dtt[3]
            k.V(lambda: nc.vector.tensor_scalar(out=la[:, :], in0=raw[:, :], scalar1=nA[:, d:d + 1], scalar2=-1.0, op0=ALU.mult, op1=ALU.mult),
                r=[raw, nA], w=[la])
            G = dtt[4 + d]
            k.V(lambda: nc.vector.tensor_tensor_scan(out=G[:, :], data0=onesr[:, :], data1=la[:, :], initial=0.0, op0=ALU.mult, op1=ALU.add),
                r=[la, onesr], w=[G])
            if d == 1:
                k.V(lambda: nc.vector.tensor_tensor(out=G[:, :], in0=G[:, :], in1=la[:, :], op=ALU.subtract), r=[G, la], w=[G])
        scr = C.ssd_scr
        k.dma(scr[0, :, :], dtt[4][:, :], r=[dtt[4]], w=[scr])
        k.dma(scr[1, :, :], dtt[5][:, :], r=[dtt[5]], w=[scr])
        for d in range(2):
            k.A(lambda: nc.scalar.activation(out=dtt[d][:, :], in_=dtt[d][:, :], func=AF.Ln), r=[dtt[d]], w=[dtt[d]])
        k.V(lambda: nc.vector.tensor_tensor(out=dtt[0][:, :], in0=dtt[0][:, :], in1=dtt[4][:, :], op=ALU.subtract), r=[dtt[0], dtt[4]], w=[dtt[0]])
        k.V(lambda: nc.vector.tensor_tensor(out=dtt[1][:, :], in0=dtt[1][:, :], in1=dtt[5][:, :], op=ALU.add), r=[dtt[1], dtt[5]], w=[dtt[1]])
        for a in range(2):
            transpose_blocks(k, C, dtt[a], 16, colsT[a], 0)
        k.barrier()
        es2.close()
        raw = k.sbuf(es, [128, T], name="ssd_raw")
        acc = k.sbuf(es, [128, T], name="ssd_acc")
        xs = [k.sbuf(es, [128, T], name="ssd_xs") for _ in range(2)]
        Bm = k.sbuf(es, [128, T], BF16, name="ssd_B")
        Cm = k.sbuf(es, [128, T], BF16, name="ssd_C")
        yc = [k.sbuf(es, [128, T], name="ssd_y") for _ in range(2)]
        vt = k.sbuf(es, [128, NB, 128], BF16, name="ssd_vt")
        gfb = k.sbuf(es, [128, T], name="ssd_gfb")
        ebb = k.sbuf(es, [128, T], name="ssd_ebb")
        rstd = k.sbuf(es, [128, T], name="ssd_rstd")
        ob = k.sbuf(es, [128, T], BF16, name="ssd_ob")
        cw = P("ssd_conv_w", 128)
        cb = P("ssd_conv_b", 128)
        for g in range(4):
            for j in range(2):
                ch = 2 * g + j
                load_rows(k, raw, 0, PT, O_SSD + 1024 + ch * 128, 128)
                conv_silu(k, C, raw, acc, xs[j], cw[:, ch * 5:(ch + 1) * 5], cb[:, ch:ch + 1])
            load_rows(k, raw, 0, PT, O_SSD + 2048 + g * 128, 128)
            conv_silu(k, C, raw, acc, Bm, cw[:, (8 + g) * 5:(9 + g) * 5], cb[:, 8 + g:9 + g])
            load_rows(k, raw, 0, PT, O_SSD + 2560 + g * 128, 128)
            conv_silu(k, C, raw, acc, Cm, cw[:, (12 + g) * 5:(13 + g) * 5], cb[:, 12 + g:13 + g])
            for j in range(2):
                ch = 2 * g + j
                transpose_blocks(k, C, xs[j], 128, vt, 0)
                for hh in range(2):
                    h = ch * 2 + hh
                    k.dma(gfb[:, :], scr[0, h:h + 1, :].partition_broadcast(128), r=[scr], w=[gfb])
                    k.dma(ebb[:, :], scr[1, h:h + 1, :].partition_broadcast(128), r=[scr], w=[ebb])
                    colf = (lambda a, i, h=h: colsT[a][:, i, h:h + 1])
                    quad_attn(k, C, R, Bm, Cm, 128, 0, vt, 0, 128, gfb, ebb, colf, colsT, yc[j], hh * 64, hh * 64 + 64)
                load_rows(k, raw, 0, PT, O_SSD + ch * 128, 128)
                k.A(lambda: nc.scalar.activation(out=acc[:, :], in_=raw[:, :], func=AF.Silu), r=[raw], w=[acc])
                k.V(lambda: nc.vector.scalar_tensor_tensor(out=yc[j][:, :], in0=xs[j][:, :], scalar=P("ssd_D", 128)[:, ch:ch + 1], in1=yc[j][:, :],
                                                           op0=ALU.mult, op1=ALU.add), r=[xs[j], yc[j], C.prm], w=[yc[j]])
                k.V(lambda: nc.vector.tensor_tensor(out=yc[j][:, :], in0=yc[j][:, :], in1=acc[:, :], op=ALU.mult), r=[yc[j], acc], w=[yc[j]])
            sumsq_bcast(k, C, yc, 128, C.ones[:, :], 1.0 / 256, 1e-6, rstd, raw)
            for j in range(2):
                ch = 2 * g + j
                k.V(lambda: nc.vector.scalar_tensor_tensor(out=ob[:, :], in0=yc[j][:, :], scalar=P("ssd_norm_w", 128)[:, ch:ch + 1], in1=rstd[:, :],
                                                           op0=ALU.mult, op1=ALU.mult), r=[yc[j], rstd, C.prm], w=[ob])
                k.dma(YT[yrow0 + ch * 128: yrow0 + (ch + 1) * 128, :], ob[:, :], r=[ob], w=[YT])
        k.barrier()


def mixer_ret(k, C, PT, YT, yrow0):
    nc = k.nc
    with ExitStack() as es:
        phase_consts(k, C, es, wm=True, poscol=True)
        quad_pools(k, C, es)
        R = C.R
        raw = k.sbuf(es, [128, T], name="ret_raw")
        t1 = k.sbuf(es, [128, T], name="ret_t1")
        qr = k.sbuf(es, [128, T], BF16, name="ret_q")
        kr = k.sbuf(es, [128, T], BF16, name="ret_k")
        racc = k.sbuf(es, [128, T], name="ret_racc")
        vt = k.sbuf(es, [128, NB, 128], BF16, name="ret_vt")
        gfb = k.sbuf(es, [128, T], name="ret_gfb")
        cols = k.sbuf(es, [128, 4, NB], name="ret_cols")
        yh = k.sbuf(es, [128, T], name="ret_y")
        rstd = k.sbuf(es, [128, T], name="ret_rstd")
        ob = k.sbuf(es, [128, T], BF16, name="ret_ob")
        k.G(lambda: nc.gpsimd.memset(cols[:, 2:4, :], 1.0), w=[cols])
        rows = k.sbuf(es, [128, 3, T], name="ret_rows")
        k.dma(rows[:, :, :], C.d_rows[:, 0:3, :], r=[C.d_rows], w=[rows])

        def rope(src_row0, dst, scl):
            load_rows(k, raw, 0, PT, src_row0, 128)
            k.G(lambda: nc.gpsimd.tensor_tensor(out=t1[:, :], in0=raw[:, :], in1=rows[:, 0, :], op=ALU.mult), r=[raw, rows], w=[t1])
            for tg in range(T // 512):
                ts = slice(tg * 512, (tg + 1) * 512)
                ps = C.psrot.get()
                k.P(lambda: nc.tensor.matmul(ps[:, :], lhsT=C.perm[:, :], rhs=raw[:, ts], start=True, stop=True), r=[raw, C.sq], w=[ps])
                k.V(lambda: nc.vector.tensor_tensor(out=racc[:, ts], in0=ps[:, :], in1=rows[:, 1, ts], op=ALU.mult), r=[ps, rows], w=[racc])
            k.V(lambda: nc.vector.tensor_tensor(out=racc[:, :], in0=racc[:, :], in1=t1[:, :], op=ALU.add), r=[racc, t1], w=[racc])
            k.V(lambda: nc.vector.tensor_scalar(out=dst[:, :], in0=racc[:, :], scalar1=float(scl), scalar2=None, op0=ALU.mult), r=[racc], w=[dst])

        for h in range(8):
            if h % 2 == 0:
                rope(O_RET + (h // 2) * 128, qr, 1.0)
                rope(O_RET + 512 + (h // 2) * 128, kr, 0.125)
            lg = math.log(1.0 - 2.0 ** (-5.0 - h))
            k.V(lambda: nc.vector.tensor_scalar(out=gfb[:, :], in0=rows[:, 2, :], scalar1=lg, scalar2=None, op0=ALU.mult), r=[rows], w=[gfb])
            k.V(lambda: nc.vector.tensor_scalar(out=cols[:, 0:1, :], in0=C.poscol[:, :].unsqueeze(1), scalar1=-lg, scalar2=None, op0=ALU.mult),
                r=[C.poscol], w=[cols])
            k.V(lambda: nc.vector.tensor_scalar(out=cols[:, 1:2, :], in0=C.poscol[:, :].unsqueeze(1), scalar1=lg, scalar2=None, op0=ALU.mult),
                r=[C.poscol], w=[cols])
            load_rows(k, raw, 0, PT, O_RET + 1024 + h * 128, 128)
            transpose_blocks(k, C, raw, 128, vt, 0)
            colf = (lambda a, i: cols[:, a, i:i + 1])
            quad_attn(k, C, R, kr, qr, 64, (h % 2) * 64, vt, 0, 128, gfb, gfb, colf, [cols], yh, 0, 128)
            sumsq_bcast(k, C, [yh], 128, C.ones[:, :], 1.0 / 128, 1e-6, rstd, t1)
            load_rows(k, raw, 0, PT, O_RET + 2048 + h * 128, 128)
            k.A(lambda: nc.scalar.activation(out=t1[:, :], in_=raw[:, :], func=AF.Silu), r=[raw], w=[t1])
            k.V(lambda: nc.vector.tensor_tensor(out=yh[:, :], in0=yh[:, :], in1=rstd[:, :], op=ALU.mult), r=[yh, rstd], w=[yh])
            k.V(lambda: nc.vector.tensor_tensor(out=ob[:, :], in0=yh[:, :], in1=t1[:, :], op=ALU.mult), r=[yh, t1], w=[ob])
            k.dma(YT[yrow0 + h * 128: yrow0 + (h + 1) * 128, :], ob[:, :], r=[ob], w=[YT])
        k.barrier()


def chunkT(a, nch):
    a = np.asarray(a, np.float32)
    n, Cc = a.shape
    return a.T.reshape(nch, 128, n).transpose(1, 0, 2).reshape(128, nch * n).copy()


def pack_layer_params(inp, l):
    ent = {}
    ent["ssd_A_log"] = np.asarray(inp["ssd_A_log"][l], np.float32).T.copy()
    ent["ssd_dt_bias"] = np.asarray(inp["ssd_dt_bias"][l], np.float32).T.copy()
    ent["ssd_conv_w"] = chunkT(inp["ssd_conv_w"][l], 16)
    ent["ssd_conv_b"] = colmajor(inp["ssd_conv_b"][l])
    ent["ssd_D"] = colmajor(np.repeat(np.asarray(inp["ssd_D"][l], np.float32), 64))
    ent["ssd_norm_w"] = colmajor(inp["ssd_norm_w"][l])
    ent["gdn_conv_w"] = chunkT(inp["gdn_conv_w"][l], 16)
    ent["gdn_A_log"] = np.asarray(inp["gdn_A_log"][l], np.float32).T.copy()
    ent["gdn_dt_bias"] = np.asarray(inp["gdn_dt_bias"][l], np.float32).T.copy()
    ent["gdn_norm_w"] = colmajor(inp["gdn_norm_w"][l])
    mu = np.asarray(inp["rwkv_mu"][l], np.float32)
    hl = lambda v: np.asarray(v, np.float32).reshape(16, 64).T.copy()
    ent["rwkv_mu_r"] = hl(mu[0:1024])
    ent["rwkv_mu_k"] = hl(mu[1024:2048])
    ent["rwkv_mu_v"] = hl(mu[2048:3072])
    ent["rwkv_mu_lo"] = mu[3072:3264].reshape(3, 64).T.copy()
    ent["rwkv_mu_g"] = colmajor(mu[3264:3424])
    ent["rwkv_w0"] = np.concatenate([colmajor(inp["rwkv_w0"][l][0]), colmajor(inp["rwkv_w0"][l][1])], axis=1)
    ent["rwkv_a0"] = colmajor(inp["rwkv_a0"][l])
    ent["rwkv_k_k"] = hl(inp["rwkv_k_k"][l])
    ent["rwkv_k_a"] = hl(inp["rwkv_k_a"][l])
    ent["rwkv_r_k"] = hl(np.asarray(inp["rwkv_r_k"][l]).reshape(-1))
    ent["rwkv_ln_w"] = hl(inp["rwkv_ln_w"][l])
    ent["rwkv_ln_b"] = hl(inp["rwkv_ln_b"][l])
    ent["attn_norm_g"] = colmajor(inp["attn_norm_g"][l])
    ent["ffn_norm_g"] = colmajor(inp["ffn_norm_g"][l])
    ent["final_norm_g"] = colmajor(inp["final_norm_g"])
    return ent


def layout_params(ent):
    off = {}
    c = 0
    for name, a in ent.items():
        off[name] = (c, a.shape[0], a.shape[1])
        c += a.shape[1]
    arr = np.zeros((128, c), np.float32)
    for name, a in ent.items():
        o, n, m = off[name]
        arr[0:n, o:o + m] = a
    return arr, off


def setup_ctx(k, es, nc, dr, prm_off):
    C = Ctx()
    C.dr = dr
    C.d_rows = dr["c_rows"]
    nprm = sum(v[2] for v in prm_off.values())
    C.nprm = nprm
    C.prm_off = prm_off
    return C


def phase_consts(k, C, es, masks=False, wm=False, poscol=False):
    dr = C.dr
    C.sq = k.sbuf(es, [128, 4, 128], name="c_sq")
    k.dma(C.sq[:, :, :], dr["c_sq"][:, :, :], r=[dr["c_sq"]], w=[C.sq])
    C.ident = C.sq[:, 0, :]
    C.ones = C.sq[:, 1, :]
    C.blk64 = C.sq[:, 2, :]
    C.perm = C.sq[:, 3, :]
    if masks:
        C.masks = k.sbuf(es, [128, NMASK, 128], name="c_masks")
        k.dma(C.masks[:, :, :], dr["c_masks"][:, :, :], r=[dr["c_masks"]], w=[C.masks])
    if wm:
        C.wm = k.sbuf(es, [128, 2, 896], name="c_wm")
        k.dma(C.wm[:, :, :], dr["c_wm"][:, :, :], r=[dr["c_wm"]], w=[C.wm])
    if poscol:
        C.poscol = k.sbuf(es, [128, NB], name="c_poscol")
        k.dma(C.poscol[:, :], dr["c_poscol"][:, :], r=[dr["c_poscol"]], w=[C.poscol])
    C.prm = k.sbuf(es, [128, C.nprm], name="prm")
    k.dma(C.prm[:, :], C.prm_dram[:, :], r=[C.prm_dram], w=[C.prm])
    prm_off = C.prm_off
    C.pcol = lambda name, nparts=128: C.prm[0:nparts, prm_off[name][0]:prm_off[name][0] + prm_off[name][2]]


def mkpool(tiles):
    pr = Ctx()
    pr.tiles = tiles
    pr.i = 0

    def get():
        t = pr.tiles[pr.i % len(pr.tiles)]
        pr.i += 1
        return t
    pr.get = get
    return pr


def quad_pools(k, C, es):
    ps = [k.psum(es, [128, 512], name="psb") for _ in range(8)]
    C.psrot = mkpool(ps[0:6])
    C.psacc = mkpool(ps[6:8])
    R = Ctx()
    R.arg = Rot(k, es, 4, [128, 512], name="r_arg")
    R.ex = Rot(k, es, 4, [128, 512], name="r_ex")
    R.wt = Rot(k, es, 6, [128, 512], name="r_wt")
    R.sc = Rot(k, es, 3, [128, 512], BF16, name="r_sc")
    C.R = R


import os
def tri_inverse_batch(k, C, units, pq, filler=None):
    nc = k.nc
    nm = lambda lv: C.masks[:, 7 + lv, :]
    for u in units:
        k.G(lambda: nc.gpsimd.tensor_tensor(out=u["Ta"][:, :], in0=u["LL"][:, :], in1=nm(1), op=ALU.mult), r=[u["LL"], C.masks], w=[u["Ta"]])
        k.G(lambda: nc.gpsimd.tensor_tensor(out=u["Ta"][:, :], in0=u["Ta"][:, :], in1=C.ident, op=ALU.add), r=[u["Ta"], C.sq], w=[u["Ta"]])
        k.G(lambda: nc.gpsimd.tensor_tensor(out=u["Wa"][:, :], in0=u["LT"][:, :], in1=nm(1), op=ALU.mult), r=[u["LT"], C.masks], w=[u["Wa"]])
        k.G(lambda: nc.gpsimd.tensor_tensor(out=u["Wa"][:, :], in0=u["Wa"][:, :], in1=C.ident, op=ALU.add), r=[u["Wa"], C.sq], w=[u["Wa"]])
        u["T"], u["Tn"], u["W"], u["Wn"] = u["Ta"], u["Tb"], u["Wa"], u["Wb"]
    for lv in range(2, 8 if not os.environ.get("K_SKIP_INV") else 2):
        last = lv == 7
        pz = {}
        for ui, u in enumerate(units):
            if not last:
                p1 = pq.get()
                k.P(lambda: nc.tensor.matmul(p1[:, :], lhsT=u["LT"][:, :], rhs=u["T"][:, :], start=True, stop=True), r=[u["LT"], u["T"]], w=[p1])
                pz[(ui, 0)] = p1
            p2 = pq.get()
            k.P(lambda: nc.tensor.matmul(p2[:, :], lhsT=u["LL"][:, :], rhs=u["W"][:, :], start=True, stop=True), r=[u["LL"], u["W"]], w=[p2])
            pz[(ui, 1)] = p2
        if filler is not None:
            next(filler, None)
        for ui, u in enumerate(units):
            if not last:
                p1 = pz[(ui, 0)]
                k.V(lambda: nc.vector.tensor_tensor(out=u["Z1"][:, :], in0=p1[:, :], in1=nm(lv), op=ALU.mult), r=[p1, C.masks], w=[u["Z1"]])
            p2 = pz[(ui, 1)]
            k.V(lambda: nc.vector.tensor_tensor(out=u["Z2"][:, :], in0=p2[:, :], in1=nm(lv), op=ALU.mult), r=[p2, C.masks], w=[u["Z2"]])
        if filler is not None:
            next(filler, None)
        for ui, u in enumerate(units):
            if not last:
                p1 = pq.get()
                k.P(lambda: nc.tensor.matmul(p1[:, :], lhsT=u["W"][:, :], rhs=u["Z1"][:, :], start=True, stop=True), r=[u["W"], u["Z1"]], w=[p1])
                pz[(ui, 0)] = p1
            p2 = pq.get()
            k.P(lambda: nc.tensor.matmul(p2[:, :], lhsT=u["T"][:, :], rhs=u["Z2"][:, :], start=True, stop=True), r=[u["T"], u["Z2"]], w=[p2])
            pz[(ui, 1)] = p2
        if filler is not None:
            next(filler, None)
        for ui, u in enumerate(units):
            if not last:
                p1 = pz[(ui, 0)]
                k.V(lambda: nc.vector.tensor_tensor(out=u["Tn"][:, :], in0=u["T"][:, :], in1=p1[:, :], op=ALU.add), r=[u["T"], p1], w=[u["Tn"]])
            p2 = pz[(ui, 1)]
            k.V(lambda: nc.vector.tensor_tensor(out=u["Wn"][:, :], in0=u["W"][:, :], in1=p2[:, :], op=ALU.add), r=[u["W"], p2], w=[u["Wn"]])
            u["T"], u["Tn"] = u["Tn"], u["T"]
            u["W"], u["Wn"] = u["Wn"], u["W"]
        if filler is not None:
            next(filler, None)


def alloc_units(k, es, n):
    units = []
    for i in range(n):
        u = {}
        for nm_ in ("LL", "LT", "Ta", "Tb", "Wa", "Wb", "Z1", "Z2"):
            u[nm_] = k.sbuf(es, [128, 128], F32R, name="u_" + nm_)
        units.append(u)
    return units


def softplus_rows(k, src, tmp, dst, n, biascol, r_extra):
    nc = k.nc
    k.A(lambda: nc.scalar.activation(out=tmp[0:n, :], in_=src[0:n, :], func=AF.Exp, bias=biascol), r=[src] + r_extra, w=[tmp])
    k.A(lambda: nc.scalar.activation(out=dst[0:n, :], in_=tmp[0:n, :], func=AF.Ln, bias=1.0), r=[tmp], w=[dst])


def chunk_cumsum(k, C, lg, reset, pref, G, n, backward):
    nc = k.nc
    k.V(lambda: nc.vector.tensor_tensor_scan(out=pref[0:n, :], data0=reset[0:n, :], data1=lg[0:n, :], initial=0.0, op0=ALU.mult, op1=ALU.add),
        r=[lg, reset], w=[pref])
    p3 = pref[0:n, :].rearrange("p (c j) -> p c j", j=128)
    tot = p3[:, :, 127:128]
    if not backward:
        k.V(lambda: nc.vector.tensor_copy(out=G[0:n, :], in_=pref[0:n, :]), r=[pref], w=[G])
    else:
        g3 = G[0:n, :].rearrange("p (c j) -> p c j", j=128)
        l3 = lg[0:n, :].rearrange("p (c j) -> p c j", j=128)
        k.V(lambda: nc.vector.tensor_tensor(out=g3, in0=l3, in1=p3, op=ALU.subtract), r=[lg, pref], w=[G])
        k.V(lambda: nc.vector.tensor_tensor(out=g3, in0=g3, in1=tot.to_broadcast([n, NB, 128]), op=ALU.add), r=[G, pref], w=[G])
    return tot


def mixer_gdn(k, C, PT, YT, yrow0):
    nc = k.nc
    NU = 8
    with ExitStack() as es:
        phase_consts(k, C, es, masks=True)
        P = C.pcol
        psb = [k.psum(es, [128, 512], name="psb") for _ in range(2)]
        C.psrot = mkpool(psb)
        pqb = [k.psum(es, [128, 512], name="psq") for _ in range(6)]
        pq = mkpool([TBQ(b_, c0) for c0 in (0, 128, 256, 384) for b_ in pqb])
        cols = [k.sbuf(es, [128, NB, 8], name="gdn_cols") for _ in range(7)]
        etot_bc = k.sbuf(es, [128, 2, 8, NB], name="gdn_etot")
        scr = C.gdn_scr
        scr2 = C.gdn_scr2
        es2 = ExitStack()
        reset = k.sbuf(es2, [8, T], name="g_reset")
        k.dma(reset[:, :], C.d_rows[0:8, 3, :], r=[C.d_rows], w=[reset])
        beta = k.sbuf(es2, [8, T], name="g_beta")
        tA = k.sbuf(es2, [8, T], name="g_tA")
        tB = k.sbuf(es2, [8, T], name="g_tB")
        lg = k.sbuf(es2, [8, T], name="g_lg")
        G = k.sbuf(es2, [8, T], name="g_G")
        eG = k.sbuf(es2, [8, T], name="g_eG")
        eA = k.sbuf(es2, [8, 2], name="g_eA")
        et = k.sbuf(es2, [8, NB], name="g_et")
        k.A(lambda: nc.scalar.activation(out=eA[:, :], in_=P("gdn_A_log", 8), func=AF.Exp), r=[C.prm], w=[eA])
        load_rows(k, tA, 0, PT, O_GDN + 3072, 8)
        k.A(lambda: nc.scalar.activation(out=beta[:, :], in_=tA[:, :], func=AF.Sigmoid), r=[tA], w=[beta])
        k.dma(scr[4, :, :], beta[:, :], r=[beta], w=[scr])
        transpose_blocks(k, C, beta, 8, cols[6], 0)
        for d in range(2):
            load_rows(k, tA, 0, PT, O_GDN + 3080 + 8 * d, 8)
            softplus_rows(k, tA, tB, tA, 8, P("gdn_dt_bias", 8)[:, d:d + 1], [C.prm])
            k.V(lambda: nc.vector.tensor_scalar(out=lg[:, :], in0=tA[:, :], scalar1=eA[:, d:d + 1], scalar2=-1.0, op0=ALU.mult, op1=ALU.mult),
                r=[tA, eA], w=[lg])
            tot = chunk_cumsum(k, C, lg, reset, tB, G, 8, d == 1)
            k.dma(scr[d, :, :], G[:, :], r=[G], w=[scr])
            transpose_blocks(k, C, G, 8, cols[0 + d], 0)
            k.A(lambda: nc.scalar.activation(out=eG[:, :], in_=G[:, :], func=AF.Exp), r=[G], w=[eG])
            k.dma(scr[2 + d, :, :], eG[:, :], r=[eG], w=[scr])
            k.V(lambda: nc.vector.tensor_tensor(out=eG[:, :], in0=eG[:, :], in1=beta[:, :], op=ALU.mult), r=[eG, beta], w=[eG])
            transpose_blocks(k, C, eG, 8, cols[2 + d], 0)
            g3 = G[:, :].rearrange("p (c j) -> p c j", j=128)
            k.V(lambda: nc.vector.tensor_tensor(out=g3, in0=tot.to_broadcast([8, NB, 128]), in1=g3, op=ALU.subtract), r=[G, tB], w=[G])
            k.A(lambda: nc.scalar.activation(out=G[:, :], in_=G[:, :], func=AF.Exp), r=[G], w=[G])
            transpose_blocks(k, C, G, 8, cols[4 + d], 0)
            k.A(lambda: nc.scalar.activation(out=et[:, :], in_=tot.rearrange("p c o -> p (c o)"), func=AF.Exp), r=[tB], w=[et])
            k.dma(scr2[d, :, :], et[:, :], r=[et], w=[scr2])
        k.dma(etot_bc[:, :, :, :].rearrange("p a h c -> p (a h c)"),
              scr2[:, :, :].rearrange("a h c -> (a h c)").partition_broadcast(128), r=[scr2], w=[etot_bc])
        k.barrier()
        es2.close()
        tX = k.sbuf(es, [128, T], name="gdn_tX")
        tY = k.sbuf(es, [128, T], name="gdn_tY")
        qT = k.sbuf(es, [128, T], name="gdn_q")
        kT = k.sbuf(es, [128, T], name="gdn_k")
        vT = k.sbuf(es, [128, T], name="gdn_v")
        rs = k.sbuf(es, [128, T], name="gdn_rs")
        ktok = k.sbuf(es, [128, NB, 128], name="gdn_ktok")
        vb = k.sbuf(es, [128, NB, 128], name="gdn_vb")
        kbg = k.sbuf(es, [128, NB, 128], name="gdn_kbg")
        kdec = k.sbuf(es, [128, NB, 128], name="gdn_kdec")
        U = k.sbuf(es, [128, NB, 128], name="gdn_U")
        WT = k.sbuf(es, [128, T], name="gdn_WT")
        attnT = k.sbuf(es, [128, NB, 128], name="gdn_attnT")
        yT = k.sbuf(es, [128, T], name="gdn_yT")
        Gbc = k.sbuf(es, [128, T], name="gdn_Gbc")
        Bbc = k.sbuf(es, [128, T], name="gdn_Bbc")
        Ebc = k.sbuf(es, [128, T], name="gdn_Ebc")
        ob = k.sbuf(es, [128, T], BF16, name="gdn_ob")
        Ss = [k.sbuf(es, [128, 128], name="gdn_S") for _ in range(2)]
        vn = [k.sbuf(es, [128, 128], name="gdn_vn") for _ in range(2)]
        dec = [k.sbuf(es, [128, 128], name="gdn_dec") for _ in range(6)]
        units = alloc_units(k, es, NU)
        cw = P("gdn_conv_w", 128)
        deci = [0]

        def decay_tile(c, d, h, transposed, mask_idx):
            cs = slice(c * 128, (c + 1) * 128)
            a = dec[deci[0] % 6]
            deci[0] += 1
            gcol = cols[0 + d][:, c, h:h + 1]
            if not transposed:
                k.V(lambda: nc.vector.scalar_tensor_tensor(out=a[:, :], in0=Gbc[:, cs], scalar=gcol, in1=C.masks[:, mask_idx, :],
                                                           op0=ALU.subtract, op1=ALU.add), r=[Gbc, cols[0 + d], C.masks], w=[a])
                k.A(lambda: nc.scalar.activation(out=a[:, :], in_=a[:, :], func=AF.Exp), r=[a], w=[a])
            else:
                k.V(lambda: nc.vector.scalar_tensor_tensor(out=a[:, :], in0=Gbc[:, cs], scalar=gcol, in1=C.masks[:, mask_idx, :],
                                                           op0=ALU.subtract, op1=ALU.subtract), r=[Gbc, cols[0 + d], C.masks], w=[a])
                k.A(lambda: nc.scalar.activation(out=a[:, :], in_=a[:, :], func=AF.Exp, scale=-1.0), r=[a], w=[a])
            return a

        for m in range(4):
            for (dst, row0, chn, scl) in ((qT, O_GDN + m * 128, m, 128.0 ** -0.5), (kT, O_GDN + 512 + m * 128, 4 + m, 1.0)):
                load_rows(k, tX, 0, PT, row0, 128)
                conv_silu(k, C, tX, tY, dst, cw[:, chn * 5:(chn + 1) * 5], None)
                for tg in range(T // 512):
                    ts = slice(tg * 512, (tg + 1) * 512)
                    ps = C.psrot.get()
                    k.G(lambda: nc.gpsimd.tensor_tensor(out=tX[:, ts], in0=dst[:, ts], in1=dst[:, ts], op=ALU.mult), r=[dst], w=[tX])
                    k.P(lambda: nc.tensor.matmul(ps[:, :], lhsT=C.ones, rhs=tX[:, ts], start=True, stop=True), r=[tX, C.sq], w=[ps])
                    k.A(lambda: nc.scalar.activation(out=tY[:, ts], in_=ps[:, :], func=AF.Sqrt, bias=1e-6), r=[ps], w=[tY])
                k.V(lambda: nc.vector.reciprocal(out=rs[:, :], in_=tY[:, :]), r=[tY], w=[rs])
                k.V(lambda: nc.vector.scalar_tensor_tensor(out=dst[:, :], in0=dst[:, :], scalar=float(scl), in1=rs[:, :], op0=ALU.mult, op1=ALU.mult),
                    r=[dst, rs], w=[dst])
            transpose_blocks(k, C, kT, 128, ktok, 0)
            for h in (2 * m, 2 * m + 1):
                load_rows(k, tX, 0, PT, O_GDN + 1024 + h * 128, 128)
                conv_silu(k, C, tX, tY, vT, cw[:, (8 + h) * 5:(9 + h) * 5], None)
                transpose_blocks(k, C, vT, 128, vb, 0)
                bcol = cols[6][:, :, h:h + 1]
                k.V(lambda: nc.vector.tensor_tensor(out=vb[:, :, :], in0=vb[:, :, :], in1=bcol.to_broadcast([128, NB, 128]), op=ALU.mult),
                    r=[vb, cols[6]], w=[vb])
                k.dma(Bbc[:, :], scr[4, h:h + 1, :].partition_broadcast(128), r=[scr], w=[Bbc])
                k.G(lambda: nc.gpsimd.tensor_tensor(out=tX[:, :], in0=kT[:, :], in1=Bbc[:, :], op=ALU.mult), r=[kT, Bbc], w=[tX])
                for d in range(2):
                    k.dma(Gbc[:, :], scr[d, h:h + 1, :].partition_broadcast(128), r=[scr], w=[Gbc])
                    k.dma(Ebc[:, :], scr[2 + d, h:h + 1, :].partition_broadcast(128), r=[scr], w=[Ebc])
                    k.G(lambda: nc.gpsimd.tensor_tensor(out=tY[:, :], in0=qT[:, :], in1=Ebc[:, :], op=ALU.mult), r=[qT, Ebc], w=[tY])
                    c1 = cols[2 + d][:, :, h:h + 1]
                    c2 = cols[4 + d][:, :, h:h + 1]
                    k.V(lambda: nc.vector.tensor_tensor(out=kbg[:, :, :], in0=ktok[:, :, :], in1=c1.to_broadcast([128, NB, 128]), op=ALU.mult),
                        r=[ktok, cols[2 + d]], w=[kbg])
                    k.V(lambda: nc.vector.tensor_tensor(out=kdec[:, :, :], in0=ktok[:, :, :], in1=c2.to_broadcast([128, NB, 128]), op=ALU.mult),
                        r=[ktok, cols[4 + d]], w=[kdec])
                    m_st, m_in, m_ts = (4, 5, 6) if d == 0 else (6, 7, 4)
                    for c0 in range(0, NB, NU):
                        for ui in range(NU):
                            c = c0 + ui
                            u = units[ui]
                            cs = slice(c * 128, (c + 1) * 128)
                            dts = decay_tile(c, d, h, False, m_st)
                            dti = decay_tile(c, d, h, False, m_in)
                            dtt_ = decay_tile(c, d, h, True, m_ts)
                            p1 = pq.get()
                            k.P(lambda: nc.tensor.matmul(p1[:, :], lhsT=kT[:, cs], rhs=tX[:, cs], start=True, stop=True), r=[kT, tX], w=[p1])
                            k.V(lambda: nc.vector.tensor_tensor(out=u["LT"][:, :], in0=p1[:, :], in1=dts[:, :], op=ALU.mult), r=[p1, dts], w=[u["LT"]])
                            p2 = pq.get()
                            k.P(lambda: nc.tensor.matmul(p2[:, :], lhsT=tX[:, cs], rhs=kT[:, cs], start=True, stop=True), r=[kT, tX], w=[p2])
                            k.V(lambda: nc.vector.tensor_tensor(out=u["LL"][:, :], in0=p2[:, :], in1=dtt_[:, :], op=ALU.mult), r=[p2, dtt_], w=[u["LL"]])
                            p3 = pq.get()
                            k.P(lambda: nc.tensor.matmul(p3[:, :], lhsT=kT[:, cs], rhs=qT[:, cs], start=True, stop=True), r=[kT, qT], w=[p3])
                            k.V(lambda: nc.vector.tensor_tensor(out=attnT[:, c, :], in0=p3[:, :], in1=dti[:, :], op=ALU.mult), r=[p3, dti], w=[attnT])
                        tri_inverse_batch(k, C, units, pq)
                        for ui in range(NU):
                            c = c0 + ui
                            u = units[ui]
                            cs = slice(c * 128, (c + 1) * 128)
                            p1 = pq.get()
                            k.P(lambda: nc.tensor.matmul(p1[:, :], lhsT=u["W"][:, :].bitcast(F32), rhs=vb[:, c, :], start=True, stop=True), r=[u["W"], vb], w=[p1])
                            k.A(lambda: nc.scalar.copy(out=U[:, c, :], in_=p1[:, :]), r=[p1], w=[U])
                            p2 = pq.get()
                            k.P(lambda: nc.tensor.matmul(p2[:, :], lhsT=kbg[:, c, :], rhs=u["W"][:, :].bitcast(F32), start=True, stop=True), r=[u["W"], kbg], w=[p2])
                            k.A(lambda: nc.scalar.copy(out=WT[:, cs], in_=p2[:, :]), r=[p2], w=[WT])
                    S = Ss[0]
                    k.G(lambda: nc.gpsimd.memset(S[:, :], 0.0), w=[S])
                    order = range(NB) if d == 0 else range(NB - 1, -1, -1)
                    for n_, c in enumerate(order):
                        cs = slice(c * 128, (c + 1) * 128)
                        Sn = Ss[(n_ + 1) % 2]
                        v_ = vn[n_ % 2]
                        p1 = pq.get()
                        k.P(lambda: nc.tensor.matmul(p1[:, :], lhsT=WT[:, cs], rhs=S[:, :], start=True, stop=True), r=[WT, S], w=[p1])
                        k.V(lambda: nc.vector.tensor_tensor(out=v_[:, :], in0=U[:, c, :], in1=p1[:, :], op=ALU.subtract), r=[U, p1], w=[v_])
                        p3 = pq.get()
                        k.P(lambda: nc.tensor.matmul(p3[:, :], lhsT=kdec[:, c, :], rhs=v_[:, :], start=True, stop=True), r=[kdec, v_], w=[p3])
                        p2 = pq.get()
                        k.P(lambda: nc.tensor.matmul(p2[:, :], lhsT=S[:, :], rhs=tY[:, cs], start=True, stop=False), r=[S, tY], w=[p2])
                        k.P(lambda: nc.tensor.matmul(p2[:, :], lhsT=v_[:, :], rhs=attnT[:, c, :], start=False, stop=True), r=[v_, attnT], w=[p2])
                        k.V(lambda: nc.vector.scalar_tensor_tensor(out=Sn[:, :], in0=S[:, :], scalar=etot_bc[:, d, h, c:c + 1], in1=p3[:, :],
                                                                   op0=ALU.mult, op1=ALU.add), r=[S, etot_bc, p3], w=[Sn])
                        if d == 0:
                            k.A(lambda: nc.scalar.copy(out=yT[:, cs], in_=p2[:, :]), r=[p2], w=[yT])
                        else:
                            k.V(lambda: nc.vector.tensor_tensor(out=yT[:, cs], in0=yT[:, cs], in1=p2[:, :], op=ALU.add), r=[yT, p2], w=[yT])
                        S = Sn
                for tg in range(T // 512):
                    ts = slice(tg * 512, (tg + 1) * 512)
                    ps = C.psrot.get()
                    k.G(lambda: nc.gpsimd.tensor_tensor(out=tX[:, ts], in0=yT[:, ts], in1=yT[:, ts], op=ALU.mult), r=[yT], w=[tX])
                    k.P(lambda: nc.tensor.matmul(ps[:, :], lhsT=C.ones, rhs=tX[:, ts], start=True, stop=True), r=[tX, C.sq], w=[ps])
                    k.A(lambda: nc.scalar.activation(out=tY[:, ts], in_=ps[:, :], func=AF.Sqrt, scale=1.0 / 128, bias=1e-6), r=[ps], w=[tY])
                k.V(lambda: nc.vector.reciprocal(out=rs[:, :], in_=tY[:, :]), r=[tY], w=[rs])
                load_rows(k, tX, 0, PT, O_GDN + 2048 + h * 128, 128)
                k.A(lambda: nc.scalar.activation(out=tY[:, :], in_=tX[:, :], func=AF.Silu), r=[tX], w=[tY])
                k.V(lambda: nc.vector.scalar_tensor_tensor(out=yT[:, :], in0=yT[:, :], scalar=P("gdn_norm_w", 128)[:, 0:1], in1=rs[:, :],
                                                           op0=ALU.mult, op1=ALU.mult), r=[yT, rs, C.prm], w=[yT])
                k.V(lambda: nc.vector.tensor_tensor(out=ob[:, :], in0=yT[:, :], in1=tY[:, :], op=ALU.mult), r=[yT, tY], w=[ob])
                k.dma(YT[yrow0 + h * 128: yrow0 + (h + 1) * 128, :], ob[:, :], r=[ob], w=[YT])
        k.barrier()


def tshift(k, raw, tmp, dst, n, mucol, rdeps):
    nc = k.nc
    k.V(lambda: nc.vector.tensor_tensor(out=tmp[0:n, 1:T - 1], in0=raw[0:n, 0:T - 2], in1=raw[0:n, 2:T], op=ALU.add), r=[raw], w=[tmp])
    k.V(lambda: nc.vector.tensor_copy(out=tmp[0:n, 0:1], in_=raw[0:n, 1:2]), r=[raw], w=[tmp])
    k.V(lambda: nc.vector.tensor_copy(out=tmp[0:n, T - 1:T], in_=raw[0:n, T - 2:T - 1]), r=[raw], w=[tmp])
    k.V(lambda: nc.vector.scalar_tensor_tensor(out=tmp[0:n, :], in0=tmp[0:n, :], scalar=0.5, in1=raw[0:n, :], op0=ALU.mult, op1=ALU.subtract),
        r=[tmp, raw], w=[tmp])
    k.V(lambda: nc.vector.scalar_tensor_tensor(out=dst[0:n, :], in0=tmp[0:n, :], scalar=mucol, in1=raw[0:n, :], op0=ALU.mult, op1=ALU.add),
        r=[tmp, raw] + rdeps, w=[dst])


def rwkv_lora(k, C, PT, dW):
    nc = k.nc
    scr = C.rw_scr
    with ExitStack() as es:
        phase_consts(k, C, es)
        P = C.pcol
        psb = [k.psum(es, [128, 512], name="psb") for _ in range(4)]
        C.psrot = mkpool(psb)
        w2 = [k.sbuf(es, [64, 1024], name="rl_w2") for _ in range(3)]
        g2a = k.sbuf(es, [128, 1024], name="rl_g2a")
        g2b = k.sbuf(es, [32, 1024], name="rl_g2b")
        k.dma(w2[0][:, :], dW["rwkv_w2"][0], r=[dW["rwkv_w2_tb"]], w=[w2[0]])
        k.dma(w2[1][:, :], dW["rwkv_w2"][1], r=[dW["rwkv_w2_tb"]], w=[w2[1]])
        k.dma(w2[2][:, :], dW["rwkv_a2"], r=[dW["rwkv_a2_tb"]], w=[w2[2]])
        k.dma(g2a[:, :], dW["rwkv_g2"][0:128, :], r=[dW["rwkv_g2_tb"]], w=[g2a])
        k.dma(g2b[:, :], dW["rwkv_g2"][128:160, :], r=[dW["rwkv_g2_tb"]], w=[g2b])
        raw = k.sbuf(es, [128, T], name="rl_raw")
        tmp = k.sbuf(es, [128, T], name="rl_tmp")
        lo = [k.sbuf(es, [64, T], name="rl_lo") for _ in range(3)]
        sg0 = k.sbuf(es, [128, T], name="rl_sg0")
        sg1 = k.sbuf(es, [32, T], name="rl_sg1")
        ob = [k.sbuf(es, [128, T], name="rl_ob") for _ in range(2)]
        for i in range(3):
            load_rows(k, raw, 0, PT, O_RWKV + 3072 + 64 * i, 64)
            tshift(k, raw, tmp, lo[i], 64, P("rwkv_mu_lo", 64)[:, i:i + 1], [C.prm])
            if i < 2:
                k.A(lambda: nc.scalar.activation(out=lo[i][:, :], in_=lo[i][:, :], func=AF.Tanh), r=[lo[i]], w=[lo[i]])
        load_rows(k, raw, 0, PT, O_RWKV + 3264, 128)
        tshift(k, raw, tmp, sg0, 128, P("rwkv_mu_g", 128)[:, 0:1], [C.prm])
        k.A(lambda: nc.scalar.activation(out=sg0[:, :], in_=sg0[:, :], func=AF.Sigmoid), r=[sg0], w=[sg0])
        load_rows(k, raw, 0, PT, O_RWKV + 3264 + 128, 32)
        tshift(k, raw, tmp, sg1, 32, P("rwkv_mu_g", 32)[:, 1:2], [C.prm])
        k.A(lambda: nc.scalar.activation(out=sg1[:, :], in_=sg1[:, :], func=AF.Sigmoid), r=[sg1], w=[sg1])
        cnt = 0
        for cc in range(8):
            ccs = slice(cc * 128, (cc + 1) * 128)
            for arr in range(4):
                o = ob[cnt % 2]
                cnt += 1
                for tg in range(T // 512):
                    ts = slice(tg * 512, (tg + 1) * 512)
                    ps = C.psrot.get()
                    if arr < 3:
                        k.P(lambda: nc.tensor.matmul(ps[:, :], lhsT=w2[arr][:, ccs], rhs=lo[arr][:, ts], start=True, stop=True), r=[w2[arr], lo[arr]], w=[ps])
                    else:
                        k.P(lambda: nc.tensor.matmul(ps[:, :], lhsT=g2a[:, ccs], rhs=sg0[:, ts], start=True, stop=False), r=[g2a, sg0], w=[ps])
                        k.P(lambda: nc.tensor.matmul(ps[:, :], lhsT=g2b[:, ccs], rhs=sg1[:, ts], start=False, stop=True), r=[g2b, sg1], w=[ps])
                    if arr < 2:
                        bcol = P("rwkv_w0", 128)[:, arr * 8 + cc: arr * 8 + cc + 1]
                        k.A(lambda: nc.scalar.activation(out=o[:, ts], in_=ps[:, :], func=AF.Sigmoid, bias=bcol), r=[ps, C.prm], w=[o])
                    elif arr == 2:
                        bcol = P("rwkv_a0", 128)[:, cc:cc + 1]
                        k.A(lambda: nc.scalar.activation(out=o[:, ts], in_=ps[:, :], func=AF.Sigmoid, bias=bcol), r=[ps, C.prm], w=[o])
                    else:
                        k.A(lambda: nc.scalar.copy(out=o[:, ts], in_=ps[:, :]), r=[ps], w=[o])
                if arr < 2:
                    k.V(lambda: nc.vector.tensor_scalar(out=o[:, :], in0=o[:, :], scalar1=-math.exp(-0.5), scalar2=None, op0=ALU.mult), r=[o], w=[o])
                k.dma(scr[arr, ccs, :], o[:, :], r=[o], w=[scr])
    k.barrier()


def mixer_rwkv(k, C, PT, YT, yrow0, dW):
    nc = k.nc
    NU = 4
    rwkv_lora(k, C, PT, dW)
    scr = C.rw_scr
    with ExitStack() as es:
        phase_consts(k, C, es, masks=True)
        P = C.pcol
        psb = [k.psum(es, [128, 512], name="psb") for _ in range(2)]
        C.psrot = mkpool(psb)
        pqb = [k.psum(es, [128, 512], name="psq") for _ in range(6)]
        pq = mkpool([TBQ(b_, c0) for c0 in (0, 128, 256, 384) for b_ in pqb])
        H = 64
        rT, k2T, vT, kkT, bT, yT = [k.sbuf(es, [H, T], name="rw_p%d" % i) for i in range(6)]
        Tt = [k.sbuf(es, [H, T], name="rw_t%d" % i) for i in range(6)]
        reset = k.sbuf(es, [H, T], name="rw_reset")
        k.dma(reset[:, :], C.d_rows[0:H, 3, :], r=[C.d_rows], w=[reset])
        vtok = k.sbuf(es, [128, NB, H], name="rw_vtok")
        kbh = k.sbuf(es, [128, NB, H], name="rw_kbh")
        kkh = k.sbuf(es, [128, NB, H], name="rw_kkh")
        etot = k.sbuf(es, [H, NB], name="rw_etot")
        aak = k.sbuf(es, [128, 2 * NU, 128], name="rw_aak")
        arb = k.sbuf(es, [128, 2 * NU, 128], name="rw_arb")
        ark = k.sbuf(es, [128, 2 * NU, 128], name="rw_ark")
        rhs_sb = [k.sbuf(es, [128, H], name="rw_rhs") for _ in range(2)]
        u_sb = [k.sbuf(es, [128, H], name="rw_u") for _ in range(2)]
        Ss = [k.sbuf(es, [H, H], name="rw_S") for _ in range(2)]
        ob = k.sbuf(es, [H, T], BF16, name="rw_ob")
        units_all = alloc_units(k, es, 2 * NU)
        ones64 = C.sq[0:H, 1, 0:H]
        for h in range(16):
            hc = slice(h, h + 1)
            raw, tmp = Tt[0], Tt[1]
            load_rows(k, raw, 0, PT, O_RWKV + h * H, H)
            tshift(k, raw, tmp, rT, H, P("rwkv_mu_r", H)[:, hc], [C.prm])
            load_rows(k, raw, 0, PT, O_RWKV + 2048 + h * H, H)
            tshift(k, raw, tmp, vT, H, P("rwkv_mu_v", H)[:, hc], [C.prm])
            load_rows(k, raw, 0, PT, O_RWKV + 1024 + h * H, H)
            kT = Tt[2]
            tshift(k, raw, tmp, kT, H, P("rwkv_mu_k", H)[:, hc], [C.prm])
            aT = Tt[3]
            k.dma(aT[:, :], scr[2, h * H:(h + 1) * H, :], r=[scr], w=[aT])
            k.V(lambda: nc.vector.tensor_scalar(out=kkT[:, :], in0=kT[:, :], scalar1=P("rwkv_k_k", H)[:, hc], scalar2=None, op0=ALU.mult),
                r=[kT, C.prm], w=[kkT])
            for tg in range(T // 512):
                ts = slice(tg * 512, (tg + 1) * 512)
                ps = C.psrot.get()
                k.V(lambda: nc.vector.tensor_tensor(out=raw[:, ts], in0=kkT[:, ts], in1=kkT[:, ts], op=ALU.mult), r=[kkT], w=[raw])
                k.P(lambda: nc.tensor.matmul(ps[0:H, :], lhsT=ones64, rhs=raw[:, ts], start=True, stop=True), r=[raw, C.sq], w=[ps])
                k.A(lambda: nc.scalar.activation(out=tmp[:, ts], in_=ps[0:H, :], func=AF.Sqrt, bias=1e-6), r=[ps], w=[tmp])
            k.V(lambda: nc.vector.reciprocal(out=tmp[:, :], in_=tmp[:, :]), r=[tmp], w=[tmp])
            k.V(lambda: nc.vector.tensor_tensor(out=kkT[:, :], in0=kkT[:, :], in1=tmp[:, :], op=ALU.mult), r=[kkT, tmp], w=[kkT])
            k.V(lambda: nc.vector.tensor_scalar(out=tmp[:, :], in0=aT[:, :], scalar1=-1.0, scalar2=P("rwkv_k_a", H)[:, hc], op0=ALU.add, op1=ALU.mult),
                r=[aT, C.prm], w=[tmp])
            k.V(lambda: nc.vector.scalar_tensor_tensor(out=k2T[:, :], in0=tmp[:, :], scalar=1.0, in1=kT[:, :], op0=ALU.add, op1=ALU.mult),
                r=[tmp, kT], w=[k2T])
            k.V(lambda: nc.vector.tensor_tensor(out=bT[:, :], in0=kkT[:, :], in1=aT[:, :], op=ALU.mult), r=[kkT, aT], w=[bT])
            transpose_blocks(k, C, vT, H, vtok, 0)
            for d in range(2):
                lw, G, pref, e4, e5, e6 = Tt
                k.dma(lw[:, :], scr[d, h * H:(h + 1) * H, :], r=[scr], w=[lw])
                tot = chunk_cumsum(k, C, lw, reset, pref, G, H, d == 1)
                k.A(lambda: nc.scalar.activation(out=e4[:, :], in_=G[:, :], func=AF.Exp), r=[G], w=[e4])
                k.V(lambda: nc.vector.tensor_tensor(out=e4[:, :], in0=e4[:, :], in1=rT[:, :], op=ALU.mult), r=[e4, rT], w=[e4])
                k.V(lambda: nc.vector.tensor_tensor(out=e5[:, :], in0=G[:, :], in1=lw[:, :], op=ALU.subtract), r=[G, lw], w=[e5])
                k.A(lambda: nc.scalar.activation(out=e5[:, :], in_=e5[:, :], func=AF.Exp), r=[e5], w=[e5])
                k.V(lambda: nc.vector.tensor_tensor(out=e5[:, :], in0=e5[:, :], in1=kkT[:, :], op=ALU.mult), r=[e5, kkT], w=[e5])
                k.A(lambda: nc.scalar.activation(out=lw[:, :], in_=G[:, :], func=AF.Exp, scale=-1.0), r=[G], w=[lw])
                k.V(lambda: nc.vector.tensor_tensor(out=e6[:, :], in0=lw[:, :], in1=bT[:, :], op=ALU.mult), r=[lw, bT], w=[e6])
                k.V(lambda: nc.vector.tensor_tensor(out=lw[:, :], in0=lw[:, :], in1=k2T[:, :], op=ALU.mult), r=[lw, k2T], w=[lw])
                QR, QA, KB, KK2 = e4, e5, e6, lw
                g3 = G[:, :].rearrange("p (c j) -> p c j", j=128)
                k.V(lambda: nc.vector.tensor_tensor(out=g3, in0=tot.to_broadcast([H, NB, 128]), in1=g3, op=ALU.subtract), r=[G, pref], w=[G])
                k.A(lambda: nc.scalar.activation(out=G[:, :], in_=G[:, :], func=AF.Exp), r=[G], w=[G])
                k.A(lambda: nc.scalar.activation(out=etot[:, :], in_=tot.rearrange("p c o -> p (c o)"), func=AF.Exp), r=[pref], w=[etot])
                k.V(lambda: nc.vector.scalar_tensor_tensor(out=pref[:, :], in0=G[:, :], scalar=-1.0, in1=bT[:, :], op0=ALU.mult, op1=ALU.mult),
                    r=[G, bT], w=[pref])
                k.V(lambda: nc.vector.tensor_tensor(out=G[:, :], in0=G[:, :], in1=k2T[:, :], op=ALU.mult), r=[G, k2T], w=[G])
                transpose_blocks(k, C, pref, H, kbh, 0)
                transpose_blocks(k, C, G, H, kkh, 0)
                m_st, m_in, m_ts = (0, 1, 2) if d == 0 else (2, 3, 0)
                st = {"S": Ss[0], "n": 0}
                k.G(lambda: nc.gpsimd.memset(Ss[0][:, :], 0.0), w=[Ss[0]])
                batches = list(range(0, NB, NU)) if d == 0 else list(range(NB - NU, -1, -NU))

                def setup(c0, units, ao):
                    for ui in range(NU):
                        c = c0 + ui
                        u = units[ui]
                        ai = ao + ui
                        cs = slice(c * 128, (c + 1) * 128)
                        p1 = pq.get()
                        k.P(lambda: nc.tensor.matmul(p1[:, :], lhsT=KB[:, cs], rhs=QA[:, cs], start=True, stop=True), r=[KB, QA], w=[p1])
                        k.V(lambda: nc.vector.tensor_tensor(out=u["LT"][:, :], in0=p1[:, :], in1=C.masks[:, m_st, :], op=ALU.mult), r=[p1, C.masks], w=[u["LT"]])
                        p2 = pq.get()
                        k.P(lambda: nc.tensor.matmul(p2[:, :], lhsT=QA[:, cs], rhs=KB[:, cs], start=True, stop=True), r=[KB, QA], w=[p2])
                        k.V(lambda: nc.vector.tensor_tensor(out=u["LL"][:, :], in0=p2[:, :], in1=C.masks[:, m_ts, :], op=ALU.mult), r=[p2, C.masks], w=[u["LL"]])
                        p3 = pq.get()
                        k.P(lambda: nc.tensor.matmul(p3[:, :], lhsT=KK2[:, cs], rhs=QA[:, cs], start=True, stop=True), r=[KK2, QA], w=[p3])
                        k.V(lambda: nc.vector.tensor_tensor(out=aak[:, ai, :], in0=p3[:, :], in1=C.masks[:, m_st, :], op=ALU.mult), r=[p3, C.masks], w=[aak])
                        p4 = pq.get()
                        k.P(lambda: nc.tensor.matmul(p4[:, :], lhsT=KB[:, cs], rhs=QR[:, cs], start=True, stop=True), r=[KB, QR], w=[p4])
                        k.V(lambda: nc.vector.scalar_tensor_tensor(out=arb[:, ai, :], in0=p4[:, :], scalar=-1.0, in1=C.masks[:, m_in, :],
                                                                   op0=ALU.mult, op1=ALU.mult), r=[p4, C.masks], w=[arb])
                        p5 = pq.get()
                        k.P(lambda: nc.tensor.matmul(p5[:, :], lhsT=KK2[:, cs], rhs=QR[:, cs], start=True, stop=True), r=[KK2, QR], w=[p5])
                        k.V(lambda: nc.vector.tensor_tensor(out=ark[:, ai, :], in0=p5[:, :], in1=C.masks[:, m_in, :], op=ALU.mult), r=[p5, C.masks], w=[ark])

                def seq_gen(c0, units, ao):
                    uis = range(NU) if d == 0 else range(NU - 1, -1, -1)
                    for ui in uis:
                        c = c0 + ui
                        u = units[ui]
                        ai = ao + ui
                        cs = slice(c * 128, (c + 1) * 128)
                        S = st["S"]
                        nstep = st["n"]
                        Sn = Ss[(nstep + 1) % 2]
                        rh = rhs_sb[nstep % 2]
                        us = u_sb[nstep % 2]
                        st["n"] = nstep + 1
                        p1 = pq.get()
                        k.P(lambda: nc.tensor.matmul(p1[:, 0:H], lhsT=QA[:, cs], rhs=S[:, :], start=True, stop=False), r=[QA, S], w=[p1])
                        k.P(lambda: nc.tensor.matmul(p1[:, 0:H], lhsT=aak[:, ai, :], rhs=vtok[:, c, :], start=False, stop=True), r=[aak, vtok], w=[p1])
                        k.A(lambda: nc.scalar.copy(out=rh[:, :], in_=p1[:, 0:H]), r=[p1], w=[rh])
                        yield
                        p2 = pq.get()
                        k.P(lambda: nc.tensor.matmul(p2[:, 0:H], lhsT=u["W"][:, :].bitcast(F32), rhs=rh[:, :], start=True, stop=True), r=[u["W"], rh], w=[p2])
                        k.A(lambda: nc.scalar.copy(out=us[:, :], in_=p2[:, 0:H]), r=[p2], w=[us])
                        yield
                        p4 = pq.get()
                        k.P(lambda: nc.tensor.matmul(p4[0:H, 0:H], lhsT=kbh[:, c, :], rhs=us[:, :], start=True, stop=False), r=[kbh, us], w=[p4])
                        k.P(lambda: nc.tensor.matmul(p4[0:H, 0:H], lhsT=kkh[:, c, :], rhs=vtok[:, c, :], start=False, stop=True), r=[kkh, vtok], w=[p4])
                        k.V(lambda: nc.vector.scalar_tensor_tensor(out=Sn[:, :], in0=S[:, :], scalar=etot[:, c:c + 1], in1=p4[0:H, 0:H],
                                                                   op0=ALU.mult, op1=ALU.add), r=[S, etot, p4], w=[Sn])
                        p3 = pq.get()
                        k.P(lambda: nc.tensor.matmul(p3[0:H, :], lhsT=S[:, :], rhs=QR[:, cs], start=True, stop=False), r=[S, QR], w=[p3])
                        k.P(lambda: nc.tensor.matmul(p3[0:H, :], lhsT=us[:, :], rhs=arb[:, ai, :], start=False, stop=False), r=[us, arb], w=[p3])
                        k.P(lambda: nc.tensor.matmul(p3[0:H, :], lhsT=vtok[:, c, :], rhs=ark[:, ai, :], start=False, stop=True), r=[vtok, ark], w=[p3])
                        if d == 0:
                            k.A(lambda: nc.scalar.copy(out=yT[:, cs], in_=p3[0:H, :]), r=[p3], w=[yT])
                        else:
                            k.V(lambda: nc.vector.tensor_tensor(out=yT[:, cs], in0=yT[:, cs], in1=p3[0:H, :], op=ALU.add), r=[yT, p3], w=[yT])
                        st["S"] = Sn
                        yield

                prev = None
                for bi, c0 in enumerate(batches):
                    units = units_all[(bi % 2) * NU:(bi % 2 + 1) * NU]
                    ao = (bi % 2) * NU
                    setup(c0, units, ao)
                    tri_inverse_batch(k, C, units, pq, filler=prev)
                    if prev is not None:
                        for _ in prev:
                            pass
                    prev = seq_gen(c0, units, ao)
                for _ in prev:
                    pass
            t0, t1, t2, t3 = Tt[0], Tt[1], Tt[2], Tt[3]
            for tg in range(T // 512):
                ts = slice(tg * 512, (tg + 1) * 512)
                ps = C.psrot.get()
                k.P(lambda: nc.tensor.matmul(ps[0:H, :], lhsT=ones64, rhs=yT[:, ts], start=True, stop=True), r=[yT, C.sq], w=[ps])
                k.V(lambda: nc.vector.scalar_tensor_tensor(out=t0[:, ts], in0=ps[0:H, :], scalar=-1.0 / H, in1=yT[:, ts], op0=ALU.mult, op1=ALU.add),
                    r=[ps, yT], w=[t0])
                k.V(lambda: nc.vector.tensor_tensor(out=t1[:, ts], in0=t0[:, ts], in1=t0[:, ts], op=ALU.mult), r=[t0], w=[t1])
                ps2 = C.psrot.get()
                k.P(lambda: nc.tensor.matmul(ps2[0:H, :], lhsT=ones64, rhs=t1[:, ts], start=True, stop=True), r=[t1, C.sq], w=[ps2])
                k.A(lambda: nc.scalar.activation(out=t2[:, ts], in_=ps2[0:H, :], func=AF.Sqrt, scale=1.0 / H, bias=64e-5), r=[ps2], w=[t2])
            k.V(lambda: nc.vector.reciprocal(out=t2[:, :], in_=t2[:, :]), r=[t2], w=[t2])
            k.V(lambda: nc.vector.scalar_tensor_tensor(out=t0[:, :], in0=t0[:, :], scalar=P("rwkv_ln_w", H)[:, hc], in1=t2[:, :], op0=ALU.mult, op1=ALU.mult),
                r=[t0, t2, C.prm], w=[t0])
            k.V(lambda: nc.vector.scalar_tensor_tensor(out=t1[:, :], in0=rT[:, :], scalar=P("rwkv_r_k", H)[:, hc], in1=k2T[:, :], op0=ALU.mult, op1=ALU.mult),
                r=[rT, k2T, C.prm], w=[t1])
            for tg in range(T // 512):
                ts = slice(tg * 512, (tg + 1) * 512)
                ps = C.psrot.get()
                k.P(lambda: nc.tensor.matmul(ps[0:H, :], lhsT=ones64, rhs=t1[:, ts], start=True, stop=True), r=[t1, C.sq], w=[ps])
                k.V(lambda: nc.vector.tensor_tensor(out=t2[:, ts], in0=ps[0:H, :], in1=vT[:, ts], op=ALU.mult), r=[ps, vT], w=[t2])
            k.V(lambda: nc.vector.scalar_tensor_tensor(out=t0[:, :], in0=t0[:, :], scalar=P("rwkv_ln_b", H)[:, hc], in1=t2[:, :], op0=ALU.add, op1=ALU.add),
                r=[t0, t2, C.prm], w=[t0])
            k.dma(t3[:, :], scr[3, h * H:(h + 1) * H, :], r=[scr], w=[t3])
            k.V(lambda: nc.vector.tensor_tensor(out=ob[:, :], in0=t0[:, :], in1=t3[:, :], op=ALU.mult), r=[t0, t3], w=[ob])
            k.dma(YT[yrow0 + h * H: yrow0 + (h + 1) * H, :], ob[:, :], r=[ob], w=[YT])
        k.barrier()


def norm_transpose(k, C, es, X, tc0, ntc, gname, hT, psrot):
    nc = k.nc
    xt = [k.sbuf(es, [128, D], name="nt_x") for _ in range(2)]
    junk = k.sbuf(es, [128, D], BF16, name="nt_junk")
    st = [k.sbuf(es, [128, 4], name="nt_st") for _ in range(2)]
    g = C.pcol(gname)
    for j in range(ntc):
        tc = tc0 + j
        x = xt[j % 2]
        s_ = st[j % 2]
        k.dma(x[:, :], X[tc * 128:(tc + 1) * 128, :], r=[X], w=[x])
        k.V(lambda: nc.vector.memset(s_[:, :], 0.0), w=[s_])
        k.A(lambda: nc.scalar.activation(out=junk[:, :], in_=x[:, :], func=AF.Square, accum_out=s_[:, 0:1]), r=[x, s_], w=[junk, s_])
        k.A(lambda: nc.scalar.activation(out=s_[:, 1:2], in_=s_[:, 0:1], func=AF.Sqrt, scale=1.0 / D, bias=1e-6), r=[s_], w=[s_])
        k.V(lambda: nc.vector.reciprocal(out=s_[:, 2:3], in_=s_[:, 1:2]), r=[s_], w=[s_])
        k.G(lambda: nc.gpsimd.tensor_scalar(out=x[:, :], in0=x[:, :], scalar1=s_[:, 2:3], scalar2=None, op0=ALU.mult), r=[x, s_], w=[x])
        for dc0 in range(0, 32, 4):
            ps = psrot.get()
            for q in range(4):
                dc = dc0 + q
                k.P(lambda: nc.tensor.transpose(ps[:, q * 128:(q + 1) * 128], x[:, dc * 128:(dc + 1) * 128], C.ident), r=[x, C.sq], w=[ps])
            k.V(lambda: nc.vector.tensor_tensor(out=hT[:, dc0:dc0 + 4, j * 128:(j + 1) * 128],
                                                in0=ps[:, :].rearrange("p (q c) -> p q c", q=4),
                                                in1=g[:, dc0:dc0 + 4].unsqueeze(2).to_broadcast([128, 4, 128]), op=ALU.mult),
                r=[ps, C.prm], w=[hT])


def cast_alt(k, n, out_ap, in_ap, r, w):
    nc = k.nc
    m = n % 4
    if m == 0 or m == 2:
        k.V(lambda: nc.vector.tensor_copy(out=out_ap, in_=in_ap), r=r, w=w)
    elif m == 1:
        k.A(lambda: nc.scalar.copy(out=out_ap, in_=in_ap), r=r, w=w)
    else:
        k.G(lambda: nc.gpsimd.tensor_copy(out=out_ap, in_=in_ap), r=r, w=w)


def phase_inproj(k, C, X, gname, Win, PT):
    nc = k.nc
    with ExitStack() as es:
        phase_consts(k, C, es)
        ps = [k.psum(es, [128, 512], name="psb") for _ in range(8)]
        psrot = mkpool(ps)
        hT = k.sbuf(es, [128, 32, T], BF16, name="in_hT")
        with ExitStack() as es2:
            norm_transpose(k, C, es2, X, 0, NB, gname, hT, psrot)
            k.barrier()
        stg = [k.sbuf(es, [128, 32, 128], name="in_stg") for _ in range(2)]
        wb = [k.sbuf(es, [128, 32, 128], BF16, name="in_wb") for _ in range(2)]
        ost = [k.sbuf(es, [128, T], name="in_ost") for _ in range(2)]
        ncc = (N_IN + 127) // 128
        Wv = Win.t.rearrange("(kc p) c -> p kc c", p=128)

        def load(cc):
            cw = min(128, N_IN - cc * 128)
            sg = stg[cc % 2]
            for hf in range(2):
                k.dma(sg[:, hf * 16:(hf + 1) * 16, 0:cw], Wv[:, hf * 16:(hf + 1) * 16, cc * 128:cc * 128 + cw], r=[Win], w=[sg])

        load(0)
        for cc in range(ncc):
            cw = min(128, N_IN - cc * 128)
            sg = stg[cc % 2]
            w_ = wb[cc % 2]
            o = ost[cc % 2]
            cast_alt(k, cc, w_[:, :, 0:cw], sg[:, :, 0:cw], [sg], [w_])
            if cc + 1 < ncc:
                load(cc + 1)
            for tg in range(T // 512):
                ts = slice(tg * 512, (tg + 1) * 512)
                p = psrot.get()
                for kc in range(32):
                    k.P(lambda: nc.tensor.matmul(p[0:cw, :], lhsT=w_[:, kc, 0:cw], rhs=hT[:, kc, ts], start=(kc == 0), stop=(kc == 31)),
                        r=[w_, hT], w=[p])
                if tg % 2 == 0:
                    k.V(lambda: nc.vector.tensor_copy(out=o[0:cw, ts], in_=p[0:cw, :]), r=[p], w=[o])
                else:
                    k.A(lambda: nc.scalar.copy(out=o[0:cw, ts], in_=p[0:cw, :]), r=[p], w=[o])
            k.dma(PT[cc * 128:cc * 128 + cw, :], o[0:cw, :], r=[o], w=[PT], Q=k.act)
        k.barrier()


def phase_outproj(k, C, YT, Wout, Xin, Xout):
    nc = k.nc
    TH = T // 2
    with ExitStack() as es:
        phase_consts(k, C, es)
        ps = [k.psum(es, [128, 512], name="psb") for _ in range(8)]
        psrot = mkpool(ps)
        yT = k.sbuf(es, [128, 32, TH], BF16, name="op_yT")
        stg = [k.sbuf(es, [128, 4, 512], name="op_stg") for _ in range(3)]
        wb = [k.sbuf(es, [128, 32, 512], BF16, name="op_wb") for _ in range(2)]
        xt = [k.sbuf(es, [128, 512], name="op_x") for _ in range(4)]
        YTv = YT.t.rearrange("(kc p) t -> p kc t", p=128)
        Wv = Wout.t.rearrange("(kc p) c -> p kc c", p=128)
        n = 0
        nx = 0
        for half in range(2):
            for q in range(4):
                k.dma(yT[:, q * 8:(q + 1) * 8, :], YTv[:, q * 8:(q + 1) * 8, half * TH:(half + 1) * TH], r=[YT], w=[yT])
            for cg in range(8):
                cs = slice(cg * 512, (cg + 1) * 512)
                w_ = wb[cg % 2]
                for pc in range(8):
                    sg = stg[n % 3]
                    k.dma(sg[:, :, :], Wv[:, pc * 4:(pc + 1) * 4, cs], r=[Wout], w=[sg])
                    cast_alt(k, n, w_[:, pc * 4:(pc + 1) * 4, :], sg[:, :, :], [sg], [w_])
                    n += 1
                for tcl in range(TH // 128):
                    tc = half * (TH // 128) + tcl
                    x = xt[nx % 4]
                    nx += 1
                    k.dma(x[:, :], Xin[tc * 128:(tc + 1) * 128, cs], r=[Xin], w=[x])
                    p = psrot.get()
                    for kc in range(32):
                        k.P(lambda: nc.tensor.matmul(p[:, :], lhsT=yT[:, kc, tcl * 128:(tcl + 1) * 128], rhs=w_[:, kc, :], start=(kc == 0), stop=(kc == 31)),
                            r=[yT, w_], w=[p])
                    k.V(lambda: nc.vector.tensor_tensor(out=x[:, :], in0=x[:, :], in1=p[:, :], op=ALU.add), r=[x, p], w=[x])
                    k.dma(Xout[tc * 128:(tc + 1) * 128, cs], x[:, :], r=[x], w=[Xout], Q=k.act)
        k.barrier()


def phase_ffn(k, C, Xin, Xout, gname, Wgu, Wdn):
    nc = k.nc
    NF = DFF // 128
    with ExitStack() as es:
        phase_consts(k, C, es)
        ps = [k.psum(es, [128, 512], name="psb") for _ in range(8)]
        psrot = mkpool(ps)
        hT = k.sbuf(es, [128, 32, 512], BF16, name="ff_hT")
        actT = k.sbuf(es, [128, NF, 512], BF16, name="ff_act")
        Wgv = Wgu.t.rearrange("(kc p) c -> p kc c", p=128)
        for qt in range(T // 512):
            with ExitStack() as es2:
                norm_transpose(k, C, es2, Xin, qt * 4, 4, gname, hT, psrot)
                k.barrier()
            with ExitStack() as es3:
                stg = [k.sbuf(es3, [128, 16, 128], name="ff_stg") for _ in range(4)]
                wb = [k.sbuf(es3, [128, 32, 128], BF16, name="ff_wb") for _ in range(4)]
                sgt = [k.sbuf(es3, [128, 512], name="ff_sg") for _ in range(2)]
                nld = [0]

                def load_cast(fc, which, w_):
                    c0 = which * DFF + fc * 128
                    for hf in range(2):
                        sg = stg[nld[0] % 4]
                        k.dma(sg[:, :, :], Wgv[:, hf * 16:(hf + 1) * 16, c0:c0 + 128], r=[Wgu], w=[sg])
                        cast_alt(k, nld[0], w_[:, hf * 16:(hf + 1) * 16, :], sg[:, :, :], [sg], [w_])
                        nld[0] += 1

                def prep(fc):
                    wg = wb[(2 * fc) % 4]
                    wu = wb[(2 * fc + 1) % 4]
                    load_cast(fc, 0, wg)
                    load_cast(fc, 1, wu)

                prep(0)
                for fc in range(NF):
                    wg = wb[(2 * fc) % 4]
                    wu = wb[(2 * fc + 1) % 4]
                    if fc + 1 < NF:
                        prep(fc + 1)
                    pg = psrot.get()
                    for kc in range(32):
                        k.P(lambda: nc.tensor.matmul(pg[:, :], lhsT=wg[:, kc, :], rhs=hT[:, kc, :], start=(kc == 0), stop=(kc == 31)), r=[wg, hT], w=[pg])
                    pu = psrot.get()
                    for kc in range(32):
                        k.P(lambda: nc.tensor.matmul(pu[:, :], lhsT=wu[:, kc, :], rhs=hT[:, kc, :], start=(kc == 0), stop=(kc == 31)), r=[wu, hT], w=[pu])
                    sg_ = sgt[fc % 2]
                    k.A(lambda: nc.scalar.activation(out=sg_[:, :], in_=pg[:, :], func=AF.Silu), r=[pg], w=[sg_])
                    k.V(lambda: nc.vector.tensor_tensor(out=actT[:, fc, :], in0=sg_[:, :], in1=pu[:, :], op=ALU.mult), r=[sg_, pu], w=[actT])
                k.barrier()
            with ExitStack() as es4:
                stg2 = [k.sbuf(es4, [128, 512], name="ff_stg2") for _ in range(6)]
                wd = [k.sbuf(es4, [128, 512], BF16, name="ff_wd") for _ in range(6)]
                xt = [k.sbuf(es4, [128, 512], name="ff_x") for _ in range(4)]
                nd = [0]

                def loadd(i):
                    cg, fc = divmod(i, NF)
                    sg = stg2[i % 6]
                    k.dma(sg[:, :], Wdn.t[fc * 128:(fc + 1) * 128, cg * 512:(cg + 1) * 512], r=[Wdn], w=[sg])

                tot = 8 * NF
                for i in range(3):
                    loadd(i)
                nx = 0
                for cg in range(8):
                    cs = slice(cg * 512, (cg + 1) * 512)
                    pb = [psrot.get() for _ in range(4)]
                    for fc in range(NF):
                        i = cg * NF + fc
                        w_ = wd[i % 6]
                        cast_alt(k, i, w_[:, :], stg2[i % 6][:, :], [stg2[i % 6]], [w_])
                        if i + 3 < tot:
                            loadd(i + 3)
                        for tcl in range(4):
                            k.P(lambda: nc.tensor.matmul(pb[tcl][:, :], lhsT=actT[:, fc, tcl * 128:(tcl + 1) * 128], rhs=w_[:, :],
                                                         start=(fc == 0), stop=(fc == NF - 1)), r=[actT, w_], w=[pb[tcl]])
                    for tcl in range(4):
                        tc = qt * 4 + tcl
                        x = xt[nx % 4]
                        nx += 1
                        k.dma(x[:, :], Xin[tc * 128:(tc + 1) * 128, cs], r=[Xin], w=[x])
                        k.V(lambda: nc.vector.tensor_tensor(out=x[:, :], in0=x[:, :], in1=pb[tcl][:, :], op=ALU.add), r=[x, pb[tcl]], w=[x])
                        k.dma(Xout[tc * 128:(tc + 1) * 128, cs], x[:, :], r=[x], w=[Xout], Q=k.act)
                k.barrier()


def phase_final_norm(k, C, Xin, OUT, gname):
    nc = k.nc
    with ExitStack() as es:
        phase_consts(k, C, es)
        xt = [k.sbuf(es, [128, D], name="fn_x") for _ in range(2)]
        junk = k.sbuf(es, [128, D], BF16, name="fn_junk")
        st = [k.sbuf(es, [128, 4], name="fn_st") for _ in range(2)]
        gb = k.sbuf(es, [128, D], name="fn_g")
        k.dma(gb[:, :], C.final_g_dram.t.partition_broadcast(128), r=[C.final_g_dram], w=[gb])
        for tc in range(NB):
            x = xt[tc % 2]
            s_ = st[tc % 2]
            k.dma(x[:, :], Xin[tc * 128:(tc + 1) * 128, :], r=[Xin], w=[x])
            k.V(lambda: nc.vector.memset(s_[:, :], 0.0), w=[s_])
            k.A(lambda: nc.scalar.activation(out=junk[:, :], in_=x[:, :], func=AF.Square, accum_out=s_[:, 0:1]), r=[x, s_], w=[junk, s_])
            k.A(lambda: nc.scalar.activation(out=s_[:, 1:2], in_=s_[:, 0:1], func=AF.Sqrt, scale=1.0 / D, bias=1e-6), r=[s_], w=[s_])
            k.V(lambda: nc.vector.reciprocal(out=s_[:, 2:3], in_=s_[:, 1:2]), r=[s_], w=[s_])
            k.V(lambda: nc.vector.scalar_tensor_tensor(out=x[:, :], in0=x[:, :], scalar=s_[:, 2:3], in1=gb[:, :], op0=ALU.mult, op1=ALU.mult),
                r=[x, s_, gb], w=[x])
            k.dma(OUT[tc * 128:(tc + 1) * 128, :], x[:, :], r=[x], w=[OUT], Q=k.act)
        k.barrier()


def build_program(prm_off, layers=(0, 1), do_final=True):
    nc = bass.Bass("TRN2", target_bir_lowering=False)
    with ExitStack() as es:
        k = KB(nc, es)
        dr = {}
        for n_, shp in (("c_masks", [128, NMASK, 128]), ("c_sq", [128, 4, 128]), ("c_rows", [128, 4, T]), ("c_poscol", [128, NB]),
                        ("c_wm", [128, 2, 896])):
            dr[n_] = k.dram(n_, shp, F32, kind="ExternalInput")
        nprm = sum(v[2] for v in prm_off.values())
        prm = [k.dram("prm%d" % l, [128, nprm], F32, kind="ExternalInput") for l in range(DEPTH)]
        X = k.dram("x", [T, D], F32, kind="ExternalInput")
        OUT = k.dram("out", [T, D], F32, kind="ExternalOutput")
        w_in = k.dram("w_in", [DEPTH, D, N_IN], F32, kind="ExternalInput")
        w_out = k.dram("w_out", [DEPTH, D, D], F32, kind="ExternalInput")
        w_gu = k.dram("w_gate_up", [DEPTH, D, 2 * DFF], F32, kind="ExternalInput")
        w_dn = k.dram("w_down", [DEPTH, DFF, D], F32, kind="ExternalInput")
        rw2 = k.dram("rwkv_w2", [DEPTH, 2, 64, 1024], F32, kind="ExternalInput")
        ra2 = k.dram("rwkv_a2", [DEPTH, 64, 1024], F32, kind="ExternalInput")
        rg2 = k.dram("rwkv_g2", [DEPTH, 160, 1024], F32, kind="ExternalInput")
        fg = k.dram("final_norm_g", [D], F32, kind="ExternalInput")
        PT = k.dram("PT", [N_IN, T], F32)
        YT = k.dram("YT", [D, T], BF16)
        XA = k.dram("XA", [T, D], F32)
        XB = k.dram("XB", [T, D], F32)
        C = setup_ctx(k, es, nc, dr, prm_off)
        C.final_g_dram = fg
        C.ssd_scr = k.dram("ssd_scr", [4, 16, T], F32)
        C.gdn_scr = k.dram("gdn_scr", [5, 8, T], F32)
        C.gdn_scr2 = k.dram("gdn_scr2", [2, 8, NB], F32)
        C.rw_scr = k.dram("rw_scr", [4, 1024, T], F32)

        def sub(tb, l):
            v = TB(tb.t[l])
            v.b = tb.b
            return v

        xcur = X
        for l in layers:
            C.prm_dram = prm[l]
            phase_inproj(k, C, xcur, "attn_norm_g", sub(w_in, l), PT)
            dW = {"rwkv_w2": rw2.t[l], "rwkv_w2_tb": rw2, "rwkv_a2": ra2.t[l], "rwkv_a2_tb": ra2, "rwkv_g2": rg2.t[l], "rwkv_g2_tb": rg2}
            if not os.environ.get("K_SKIP_MIX"):
                mixer_rwkv(k, C, PT, YT, 0, dW)
                mixer_gdn(k, C, PT, YT, 1024)
                mixer_ret(k, C, PT, YT, 2048)
                mixer_ssd(k, C, PT, YT, 3072)
            phase_outproj(k, C, YT, sub(w_out, l), xcur, XB)
            phase_ffn(k, C, XB, XA, "ffn_norm_g", sub(w_gu, l), sub(w_dn, l))
            xcur = XA
        if do_final:
            phase_final_norm(k, C, xcur, OUT, "final_norm_g")
        k.finish()
    return nc


_CACHE = {}


def kernel(**inputs):
    inp = {n: np.asarray(v) for n, v in inputs.items()}
    ents = [pack_layer_params(inp, l) for l in range(DEPTH)]
    prms = []
    prm_off = None
    for e in ents:
        arr, prm_off = layout_params(e)
        prms.append(arr)
    if "nc" not in _CACHE:
        _CACHE["nc"] = build_program(prm_off)
    nc = _CACHE["nc"]
    cs = make_consts()
    cs["c_wm"] = make_widemask()
    shared = dict(cs)
    for l in range(DEPTH):
        shared["prm%d" % l] = prms[l]
    for n_ in ("w_in", "w_out", "w_gate_up", "w_down", "rwkv_w2", "rwkv_a2", "rwkv_g2", "final_norm_g"):
        shared[n_] = np.ascontiguousarray(inp[n_], dtype=np.float32)
    x = np.ascontiguousarray(inp["x"], dtype=np.float32)
    in_maps = []
    for b in range(8):
        m = dict(shared)
        m["x"] = x[b]
        in_maps.append(m)
    res = run_bass_kernel_spmd(nc, in_maps, core_ids=list(range(8)))
    return np.stack([np.asarray(r["out"]) for r in res.results], axis=0).astype(np.float32)
```
